# Optimizing a Trainium2 kernel written in Bass

```python
import math
import jax, jax.numpy as jnp
from jax import lax
import numpy as np

D_MODEL = 1024
BATCH = 8
SEQ = 2048
DEPTH = 2
DEC_BATCH = 8
DEC_SEQ = 8192
PAST_LEN = 128

N_HEADS = 8
N_KV_HEADS = 2
HEAD_DIM = 64
D_ATT = N_HEADS * HEAD_DIM
D_KV = N_KV_HEADS * HEAD_DIM
WINDOW = 128
BLOCK = 128
N_BUCKETS = 32
MAX_DISTANCE = 128
N_FGROUPS = 4
FGROUP_DIM = 128
D_FOURIER = N_FGROUPS * FGROUP_DIM
D_RNN = 1024
N_RG_BLOCKS = 16
RG_BLOCK_DIM = D_RNN // N_RG_BLOCKS
RG_C = 8.0
CONV_RNN_WIDTH = 4
CONV_RNN_LEFT = 2
D_FF = 2816
CONV_FFN_WIDTH = 3
CONV_FFN_LEFT = 1
N_BRANCHES = 3
Q_END = D_ATT
K_END = Q_END + D_KV
V_END = K_END + D_KV
F_END = V_END + D_FOURIER
RX_END = F_END + D_RNN
RY_END = RX_END + D_RNN
N_IN = RY_END + N_BRANCHES * D_MODEL
DN_ALPHA = (2 * DEPTH) ** 0.25
DN_BETA = (8 * DEPTH) ** -0.25
LN_EPS = 1e-5

kernel_name = "hybrid_bidir_encoder_gqa_fnet_rglru"


def layer_norm(x, g, b):
    xf = x.astype(jnp.float32)
    mu = jnp.mean(xf, axis=-1, keepdims=True)
    var = jnp.mean(jnp.square(xf - mu), axis=-1, keepdims=True)
    return ((xf - mu) * lax.rsqrt(var + LN_EPS) * g.astype(jnp.float32) + b.astype(jnp.float32)).astype(x.dtype)


def depthwise_conv(x, w, b, left):
    K = w.shape[0]
    S = x.shape[1]
    xp = jnp.pad(x, ((0, 0), (left, K - 1 - left), (0, 0)))
    out = b
    for k in range(K):
        out = out + w[k] * xp[:, k:k + S]
    return out


def t5_bucket(rel):
    half = N_BUCKETS // 2
    max_exact = half // 2
    ret = jnp.where(rel > 0, half, 0)
    n = jnp.abs(rel)
    nf = jnp.maximum(n, 1).astype(jnp.float32)
    large = max_exact + (jnp.log(nf / max_exact) / math.log(MAX_DISTANCE / max_exact)
                         * (half - max_exact)).astype(jnp.int32)
    large = jnp.minimum(large, half - 1)
    return ret + jnp.where(n < max_exact, n, large)


def windowed_attention(q, k, v, sink, rel_bias):
    B, S, _ = q.shape
    nb = S // BLOCK
    G = N_HEADS // N_KV_HEADS
    scale = HEAD_DIM ** -0.5
    qb = q.reshape(B, nb, BLOCK, N_KV_HEADS, G, HEAD_DIM).transpose(1, 0, 2, 3, 4, 5)

    def kv_blocks(t):
        t = t.reshape(B, S, N_KV_HEADS, HEAD_DIM)
        tp = jnp.pad(t, ((0, 0), (BLOCK, BLOCK), (0, 0), (0, 0)))
        blocks = jnp.concatenate(
            [tp[:, i * BLOCK:i * BLOCK + S].reshape(B, nb, BLOCK, N_KV_HEADS, HEAD_DIM) for i in range(3)],
            axis=2)
        return blocks.transpose(1, 0, 2, 3, 4)

    kb = kv_blocks(k)
    vb = kv_blocks(v)
    qi = jnp.arange(BLOCK)[:, None]
    kj = jnp.arange(3 * BLOCK)[None, :]
    rel = kj - BLOCK - qi
    band_mask = jnp.abs(rel) <= WINDOW
    band_bias = rel_bias.astype(jnp.float32)[t5_bucket(rel)]
    band_bias = band_bias.transpose(2, 0, 1).reshape(N_KV_HEADS, G, BLOCK, 3 * BLOCK)
    sink_f = sink.astype(jnp.float32).reshape(N_KV_HEADS, G)[None, :, :, None, None]

    def one_block(args):
        q_blk, k_blk, v_blk, blk = args
        s = jnp.einsum('bqkgd,bnkd->bkgqn', q_blk, k_blk).astype(jnp.float32) * scale + band_bias
        kpos = blk * BLOCK - BLOCK + jnp.arange(3 * BLOCK)
        valid = band_mask & ((kpos >= 0) & (kpos < S))[None, :]
        s = jnp.where(valid, s, -jnp.inf)
        m = jnp.maximum(jnp.max(s, axis=-1, keepdims=True), sink_f)
        p = jnp.exp(s - m)
        denom = jnp.sum(p, axis=-1, keepdims=True) + jnp.exp(sink_f - m)
        w = (p / denom).astype(v_blk.dtype)
        return jnp.einsum('bkgqn,bnkd->bqkgd', w, v_blk)

    out = lax.map(one_block, (qb, kb, vb, jnp.arange(nb)))
    return out.transpose(1, 0, 2, 3, 4, 5).reshape(B, S, D_ATT)


def fourier_mix(xf):
    B, S, _ = xf.shape
    xg = xf.reshape(B, S, N_FGROUPS, FGROUP_DIM).astype(jnp.float32)
    y = jnp.fft.fftn(xg, axes=(1, 3), norm="ortho").real
    return y.reshape(B, S, D_FOURIER).astype(xf.dtype)


def _lin_combine(left, right):
    a1, b1 = left
    a2, b2 = right
    return (a1 * a2, a2 * b1 + b2)


def rglru(x, w_r, b_r, w_i, b_i, lam, reverse):
    B, S, _ = x.shape
    xb = x.reshape(B, S, N_RG_BLOCKS, RG_BLOCK_DIM)
    r = jax.nn.sigmoid((jnp.einsum('bsnc,ncd->bsnd', xb, w_r).reshape(B, S, D_RNN) + b_r).astype(jnp.float32))
    i = jax.nn.sigmoid((jnp.einsum('bsnc,ncd->bsnd', xb, w_i).reshape(B, S, D_RNN) + b_i).astype(jnp.float32))
    log_a = -RG_C * r * jax.nn.softplus(-lam.astype(jnp.float32))
    a = jnp.exp(log_a)
    b = jnp.sqrt(-jnp.expm1(2.0 * log_a)) * i * x.astype(jnp.float32)
    _, h = lax.associative_scan(_lin_combine, (a, b), axis=1, reverse=reverse)
    return h.astype(x.dtype)


def encoder_layer(x, rel_bias, w_in, b_in, attn_sink, w_att_o, w_four_o, conv_rnn_w, conv_rnn_b,
                  w_rg_r, b_rg_r, w_rg_i, b_rg_i, rg_lambda, w_rnn_o, w_out, b_out, ln1_g, ln1_b,
                  w_ffn_up, conv_ffn_w, conv_ffn_b, w_ffn_down, ln2_g, ln2_b):
    B, S, _ = x.shape
    u = x @ w_in + b_in
    att = windowed_attention(u[..., :Q_END], u[..., Q_END:K_END], u[..., K_END:V_END], attn_sink, rel_bias)
    o_att = att @ w_att_o
    o_four = fourier_mix(u[..., V_END:F_END]) @ w_four_o
    xr = depthwise_conv(u[..., F_END:RX_END], conv_rnn_w, conv_rnn_b, CONV_RNN_LEFT)
    h = (rglru(xr, w_rg_r[0], b_rg_r[0], w_rg_i[0], b_rg_i[0], rg_lambda[0], False)
         + rglru(xr, w_rg_r[1], b_rg_r[1], w_rg_i[1], b_rg_i[1], rg_lambda[1], True))
    o_rnn = (h * jax.nn.gelu(u[..., RX_END:RY_END])) @ w_rnn_o
    gates = jax.nn.sigmoid(u[..., RY_END:]).reshape(B, S, N_BRANCHES, D_MODEL)
    mixed = gates[:, :, 0] * o_att + gates[:, :, 1] * o_four + gates[:, :, 2] * o_rnn
    x = layer_norm(DN_ALPHA * x + mixed @ w_out + b_out, ln1_g, ln1_b)
    hu = depthwise_conv(x @ w_ffn_up, conv_ffn_w, conv_ffn_b, CONV_FFN_LEFT)
    ffn = (jax.nn.gelu(hu[..., :D_FF]) * hu[..., D_FF:]) @ w_ffn_down
    x = layer_norm(DN_ALPHA * x + ffn, ln2_g, ln2_b)
    return x


def setup_inputs(seed: int = 0) -> dict:
    key = jax.random.key(seed)
    ks = jax.random.split(key, 28)
    L = DEPTH
    nrm = jax.random.normal
    a0 = jax.random.uniform(ks[14], (L, 2, D_RNN), minval=0.9, maxval=0.999)
    return {
        "x_prompt": nrm(ks[0], (BATCH, SEQ, D_MODEL), jnp.float32),
        "x_sample": nrm(ks[1], (DEC_BATCH, DEC_SEQ, D_MODEL), jnp.float32),
        "rel_bias": 0.2 * nrm(ks[2], (N_BUCKETS, N_HEADS), jnp.float32),
        "w_in": nrm(ks[3], (L, D_MODEL, N_IN), jnp.float32) * D_MODEL ** -0.5,
        "b_in": 0.02 * nrm(ks[4], (L, N_IN), jnp.float32),
        "attn_sink": 0.5 * nrm(ks[5], (L, N_HEADS), jnp.float32),
        "w_att_o": nrm(ks[6], (L, D_ATT, D_MODEL), jnp.float32) * D_ATT ** -0.5,
        "w_four_o": nrm(ks[7], (L, D_FOURIER, D_MODEL), jnp.float32) * D_FOURIER ** -0.5,
        "conv_rnn_w": nrm(ks[8], (L, CONV_RNN_WIDTH, D_RNN), jnp.float32) * CONV_RNN_WIDTH ** -0.5,
        "conv_rnn_b": 0.02 * nrm(ks[9], (L, D_RNN), jnp.float32),
        "w_rg_r": nrm(ks[10], (L, 2, N_RG_BLOCKS, RG_BLOCK_DIM, RG_BLOCK_DIM), jnp.float32) * RG_BLOCK_DIM ** -0.5,
        "b_rg_r": 0.02 * nrm(ks[11], (L, 2, D_RNN), jnp.float32),
        "w_rg_i": nrm(ks[12], (L, 2, N_RG_BLOCKS, RG_BLOCK_DIM, RG_BLOCK_DIM), jnp.float32) * RG_BLOCK_DIM ** -0.5,
        "b_rg_i": 0.02 * nrm(ks[13], (L, 2, D_RNN), jnp.float32),
        "rg_lambda": jnp.log(a0) - jnp.log1p(-a0),
        "w_rnn_o": nrm(ks[15], (L, D_RNN, D_MODEL), jnp.float32) * D_RNN ** -0.5,
        "w_out": nrm(ks[16], (L, D_MODEL, D_MODEL), jnp.float32) * (D_MODEL ** -0.5 * DN_BETA),
        "b_out": 0.02 * nrm(ks[17], (L, D_MODEL), jnp.float32),
        "ln1_g": 1.0 + 0.02 * nrm(ks[18], (L, D_MODEL), jnp.float32),
        "ln1_b": 0.02 * nrm(ks[19], (L, D_MODEL), jnp.float32),
        "w_ffn_up": nrm(ks[20], (L, D_MODEL, 2 * D_FF), jnp.float32) * D_MODEL ** -0.5,
        "conv_ffn_w": nrm(ks[21], (L, CONV_FFN_WIDTH, 2 * D_FF), jnp.float32) * CONV_FFN_WIDTH ** -0.5,
        "conv_ffn_b": 0.02 * nrm(ks[22], (L, 2 * D_FF), jnp.float32),
        "w_ffn_down": nrm(ks[23], (L, D_FF, D_MODEL), jnp.float32) * (D_FF ** -0.5 * DN_BETA),
        "ln2_g": 1.0 + 0.02 * nrm(ks[24], (L, D_MODEL), jnp.float32),
        "ln2_b": 0.02 * nrm(ks[25], (L, D_MODEL), jnp.float32),
    }


def reference(x_prompt, x_sample, rel_bias, w_in, b_in, attn_sink, w_att_o, w_four_o, conv_rnn_w,
              conv_rnn_b, w_rg_r, b_rg_r, w_rg_i, b_rg_i, rg_lambda, w_rnn_o, w_out, b_out, ln1_g, ln1_b,
              w_ffn_up, conv_ffn_w, conv_ffn_b, w_ffn_down, ln2_g, ln2_b):
    y_prompt = x_prompt
    y_sample = x_sample
    for l in range(DEPTH):
        layer_params = (w_in[l], b_in[l], attn_sink[l], w_att_o[l], w_four_o[l], conv_rnn_w[l], conv_rnn_b[l],
                        w_rg_r[l], b_rg_r[l], w_rg_i[l], b_rg_i[l], rg_lambda[l], w_rnn_o[l], w_out[l], b_out[l],
                        ln1_g[l], ln1_b[l], w_ffn_up[l], conv_ffn_w[l], conv_ffn_b[l], w_ffn_down[l],
                        ln2_g[l], ln2_b[l])
        y_prompt = encoder_layer(y_prompt, rel_bias, *layer_params)
        y_sample = encoder_layer(y_sample, rel_bias, *layer_params)
    return (y_prompt, y_sample)
```

```python
import math
from contextlib import ExitStack
import numpy as np
import ml_dtypes
import concourse.bass as bass
import concourse.mybir as mybir
from concourse.bass_utils import run_bass_kernel_spmd

F32 = mybir.dt.float32
BF16 = mybir.dt.bfloat16
AF = mybir.ActivationFunctionType
ALU = mybir.AluOpType
AX = mybir.AxisListType

D = 1024
NIN = 6400
DFF = 2816
T = 512
ALPHA = 4 ** 0.25
EPS = 1e-5
NEG = -30000.0


class Sem:
    def __init__(self, h):
        self.h = h
        self.cnt = 0


class Eng:
    def __init__(self, name, eng, sem):
        self.name = name
        self.eng = eng
        self.sem = sem
        self.seen = {}


class Buf:
    def __init__(self, ap, excl=False):
        self.ap = ap
        self.excl = excl
        self.w = {}
        self.r = {}

    def __getitem__(self, k):
        return self.ap[k]


class Ctx:
    def __init__(self, nc, stack):
        self.nc = nc
        self.stack = stack
        self.E = {}
        for name, eng in (("pe", nc.tensor), ("act", nc.scalar), ("dve", nc.vector),
                          ("pool", nc.gpsimd), ("sp", nc.sync)):
            s = Sem(stack.enter_context(nc.semaphore("s_" + name)))
            self.E[name] = Eng(name, eng, s)
        self.dsems = []
        self.free_dsems = []
        self.uid = 0

    def dsem(self):
        if self.free_dsems:
            return self.free_dsems.pop()
        s = Sem(self.stack.enter_context(self.nc.semaphore("d%d" % len(self.dsems))))
        self.dsems.append(s)
        return s

    def release(self, sems):
        self.free_dsems.extend(sems)

    def _deps(self, e, reads, writes):
        need = {}
        for b in reads:
            for s, v in b.w.items():
                if need.get(s, 0) < v:
                    need[s] = v
        for b in writes:
            for s, v in b.w.items():
                if need.get(s, 0) < v:
                    need[s] = v
            for s, v in b.r.items():
                if need.get(s, 0) < v:
                    need[s] = v
        for s, v in need.items():
            if s is e.sem and e.name == "pe":
                continue
            if e.seen.get(s, 0) < v:
                e.eng.wait_ge(s.h, v)
                e.seen[s] = v

    @staticmethod
    def _mark(s, v, reads, writes):
        for b in reads:
            if b.r.get(s, 0) < v:
                b.r[s] = v
        for b in writes:
            b.w = {s: v}
            b.r = {}

    def op(self, ename, fn, reads=(), writes=()):
        e = self.E[ename]
        if any(b.excl for b in reads):
            writes = list(writes) + [b for b in reads if b.excl]
            reads = [b for b in reads if not b.excl]
        self._deps(e, reads, writes)
        ins = fn(e.eng)
        e.sem.cnt += 1
        ins.then_inc(e.sem.h, 1)
        self._mark(e.sem, e.sem.cnt, reads, writes)

    def dma(self, qname, pairs, reads, writes, dsem, **kw):
        e = self.E[qname]
        self._deps(e, reads, writes)
        for (o, i) in pairs:
            e.eng.dma_start(out=o, in_=i, **kw).then_inc(dsem.h, 16)
            dsem.cnt += 16
        self._mark(dsem, dsem.cnt, reads, writes)

    def barrier(self):
        sems = [e.sem for e in self.E.values()] + self.dsems
        for e in self.E.values():
            for s in sems:
                if s.cnt > 0 and e.seen.get(s, 0) < s.cnt:
                    e.eng.wait_ge(s.h, s.cnt)
                    e.seen[s] = s.cnt


class Phase:
    def __init__(self, cx, name):
        self.cx = cx
        self.name = name
        self.stack = ExitStack()
        self.sems = []
        self.n = 0

    def sb(self, shape, dt, name=None):
        self.n += 1
        nm = "%s_%s_%d" % (self.name, name or "t", self.n)
        return self.stack.enter_context(self.cx.nc.sbuf_tensor(nm, list(shape), dt))

    def ps(self, shape, dt=F32, name=None):
        self.n += 1
        nm = "%s_%s_%d" % (self.name, name or "p", self.n)
        return self.stack.enter_context(self.cx.nc.psum_tensor(nm, list(shape), dt))

    def dsem(self):
        s = self.cx.dsem()
        self.sems.append(s)
        return s

    def close(self):
        self.cx.barrier()
        self.cx.release(self.sems)
        self.stack.close()


def t5_bucket_np(rel):
    half, max_exact = 16, 8
    ret = np.where(rel > 0, half, 0)
    n = np.abs(rel)
    nf = np.maximum(n, 1).astype(np.float32)
    large = max_exact + (np.log(nf / max_exact) / np.float32(math.log(128 / max_exact))
                         * (half - max_exact)).astype(np.int32)
    large = np.minimum(large, half - 1)
    return ret + np.where(n < max_exact, n, large)


def host_consts():
    c = {}
    c["c_ident"] = np.eye(128, dtype=np.float32).astype(ml_dtypes.bfloat16)
    k = np.arange(128)
    ang = 2 * np.pi * np.outer(k, k) / 128.0
    c["c_d128"] = np.concatenate([np.cos(ang), -np.sin(ang)], axis=1).astype(ml_dtypes.bfloat16)
    c["c_identf"] = np.eye(128, dtype=np.float32)
    c["c_jrev"] = np.ascontiguousarray(np.eye(128, dtype=np.float32)[::-1])
    c["c_onesm"] = np.full((128, 128), 1.0 / 1024.0, dtype=np.float32).astype(ml_dtypes.bfloat16)
    rel = np.arange(512) - 256
    oh = np.zeros((33, 512), np.float32)
    b = t5_bucket_np(rel)
    for j in range(512):
        if abs(rel[j]) <= 128:
            oh[b[j], j] = 1.0
        else:
            oh[32, j] = 1.0
    c["c_oh"] = oh
    for S in (2048, 8192):
        N2 = S // 128
        s = np.arange(S, dtype=np.float64)[:, None]
        k1 = np.arange(128, dtype=np.float64)[None, :]
        ang = 2 * np.pi * s * k1 / S
        c["c_ec%d" % S] = np.cos(ang).astype(ml_dtypes.bfloat16)
        c["c_es%d" % S] = np.sin(ang).astype(ml_dtypes.bfloat16)
        c["c_en%d" % S] = (-np.sin(ang)).astype(ml_dtypes.bfloat16)
        s2 = np.arange(N2, dtype=np.float64)[:, None]
        k2 = np.arange(N2, dtype=np.float64)[None, :]
        a2 = 2 * np.pi * s2 * k2 / N2
        nrm = 1.0 / math.sqrt(S * 128.0)
        c["c_cs%d" % S] = (np.concatenate([np.cos(a2), np.sin(a2)], axis=0) * nrm).astype(ml_dtypes.bfloat16)
    return c


WNAMES = ["rel_bias", "w_in", "b_in", "attn_sink", "w_att_o", "w_four_o", "conv_rnn_w", "conv_rnn_b",
          "w_rg_r", "b_rg_r", "w_rg_i", "b_rg_i", "rg_lambda", "w_rnn_o", "w_out", "b_out", "ln1_g", "ln1_b",
          "w_ffn_up", "conv_ffn_w", "conv_ffn_b", "w_ffn_down", "ln2_g", "ln2_b"]
WSHAPES = {"rel_bias": [32, 8], "w_in": [2, 1024, 6400], "b_in": [2, 6400], "attn_sink": [2, 8],
           "w_att_o": [2, 512, 1024], "w_four_o": [2, 512, 1024], "conv_rnn_w": [2, 4, 1024],
           "conv_rnn_b": [2, 1024], "w_rg_r": [2, 2, 16, 64, 64], "b_rg_r": [2, 2, 1024],
           "w_rg_i": [2, 2, 16, 64, 64], "b_rg_i": [2, 2, 1024], "rg_lambda": [2, 2, 1024],
           "w_rnn_o": [2, 1024, 1024], "w_out": [2, 1024, 1024], "b_out": [2, 1024], "ln1_g": [2, 1024],
           "ln1_b": [2, 1024], "w_ffn_up": [2, 1024, 5632], "conv_ffn_w": [2, 3, 5632],
           "conv_ffn_b": [2, 5632], "w_ffn_down": [2, 2816, 1024], "ln2_g": [2, 1024], "ln2_b": [2, 1024]}


def build_program(seqs=(2048, 8192), layers=2, stop_after=None, debug=False):
    nc = bass.Bass("TRN2", target_bir_lowering=False)
    W = {n: nc.dram_tensor(n, WSHAPES[n], F32, kind="ExternalInput").ap() for n in WNAMES}
    hc = host_consts()
    C = {}
    for n, a in hc.items():
        if n[-4:] in ("2048", "8192") and int(n[-4:]) not in seqs:
            continue
        C[n] = nc.dram_tensor(n, list(a.shape), BF16 if a.dtype != np.float32 else F32, kind="ExternalInput").ap()
    skind = "ExternalOutput" if debug else "Internal"
    X = {}
    Y = {}
    SCR = {}
    for S in seqs:
        X[S] = nc.dram_tensor("xT%d" % S, [D, S], F32, kind="ExternalInput").ap()
        Y[S] = nc.dram_tensor("yT%d" % S, [D, S], F32, kind="ExternalOutput").ap()
        d = {}
        d["q"] = nc.dram_tensor("s_q%d" % S, [512, S], BF16, kind=skind).ap()
        d["kd"] = nc.dram_tensor("s_kd%d" % S, [256, S], BF16, kind=skind).ap()
        d["v"] = nc.dram_tensor("s_v%d" % S, [S, 128], BF16, kind=skind).ap()
        d["a"] = nc.dram_tensor("s_a%d" % S, [S, 1024], BF16, kind=skind).ap()
        d["rxh"] = nc.dram_tensor("s_rxh%d" % S, [D, S], BF16, kind=skind).ap()
        d["rxl"] = nc.dram_tensor("s_rxl%d" % S, [D, S], BF16, kind=skind).ap()
        d["hf"] = nc.dram_tensor("s_hf%d" % S, [D, S], F32, kind=skind).ap()
        d["h"] = nc.dram_tensor("s_h%d" % S, [D, S], BF16, kind=skind).ap()
        d["bp"] = nc.dram_tensor("s_bp%d" % S, [128, S // 128, 1024], BF16, kind=skind).ap()
        d["yt"] = nc.dram_tensor("s_yt%d" % S, [512, S], BF16, kind=skind).ap()
        d["at"] = nc.dram_tensor("s_at%d" % S, [512, S], BF16, kind=skind).ap()
        d["x1"] = nc.dram_tensor("s_x1%d" % S, [D, S], F32, kind=skind).ap()
        d["xm"] = nc.dram_tensor("s_xm%d" % S, [D, S], F32, kind=skind).ap()
        SCR[S] = d
    tb_dram = nc.dram_tensor("s_tb", [8, 512], F32, kind=skind)

    with ExitStack() as stack:
        cx = Ctx(nc, stack)
        op = cx.op
        dma = cx.dma

        def done(tag):
            return stop_after is not None and stop_after == tag

        G = Phase(cx, "g")
        ident = G.sb([128, 128], BF16, "ident")
        onesm = G.sb([128, 128], BF16, "onesm")
        b_ident, b_onesm = Buf(ident), Buf(onesm)
        gsem = G.dsem()
        dma("sp", [(ident[:], C["c_ident"]), (onesm[:], C["c_onesm"])], [], [b_ident, b_onesm], gsem)
        with ExitStack() as st0:
            rb = st0.enter_context(nc.sbuf_tensor("g_rb", [33, 8], F32))
            oh = st0.enter_context(nc.sbuf_tensor("g_oh", [33, 512], F32))
            tbs = st0.enter_context(nc.sbuf_tensor("g_tbs", [8, 512], F32))
            tps = st0.enter_context(nc.psum_tensor("g_tps", [8, 512], F32))
            b_rb, b_oh, b_tbs, b_tps = Buf(rb), Buf(oh), Buf(tbs), Buf(tps, True)
            op("dve", lambda e: e.memset(rb[32:33, :], NEG), [], [b_rb])
            dma("sp", [(rb[0:32, :], W["rel_bias"]), (oh[:], C["c_oh"])], [], [b_rb, b_oh], gsem)
            op("pe", lambda e: e.matmul(tps[:], lhsT=rb[:], rhs=oh[:], start=True, stop=True), [b_rb, b_oh], [b_tps])
            op("dve", lambda e: e.tensor_copy(out=tbs[:], in_=tps[:]), [b_tps], [b_tbs])
            dma("sp", [(tb_dram.ap(), tbs[:])], [b_tbs], [], gsem)
            cx.barrier()

        class WB:
            def __init__(self, ph, dst, src2d, blocks, q="pool"):
                self.blocks = []
                self.ph, self.dst, self.src2d, self.q = ph, dst, src2d, q
                self.add(blocks)

            def add(self, blocks):
                K = self.src2d.shape[0] // 128
                v = self.src2d.rearrange("(k p) n -> p k n", p=128)
                dst = self.dst
                for (d0, s0, n) in blocks:
                    b = Buf(dst[:, :, d0:d0 + n])
                    dma(self.q, [(dst[:, k, d0:d0 + n], v[:, k, s0:s0 + n]) for k in range(K)], [], [b], self.ph.dsem())
                    self.blocks.append((d0, d0 + n, b))

            def at(self, col):
                for (a, b_, buf) in self.blocks:
                    if a <= col < b_:
                        return buf
                raise KeyError(col)

        def load_cols(ph, dst, src2d, c0, c1, dsem, bufs, q="pool"):
            K = src2d.shape[0] // 128
            v = src2d.rearrange("(k p) n -> p k n", p=128)
            pairs = [(dst[:, k, :], v[:, k, c0:c1]) for k in range(K)]
            dma(q, pairs, [], bufs, dsem)

        def vec_param(dst, src1d, dsem, buf):
            dma("sp", [(dst, src1d.rearrange("(c p) -> p c", p=128))], [], [buf], dsem,
                allow_slow_non_contiguous=True)

        for l in range(layers):
            last = (l == layers - 1)
            xin = {S: (X[S] if l == 0 else SCR[S]["xm"]) for S in seqs}
            xout = {S: (Y[S] if last else SCR[S]["xm"]) for S in seqs}

            P = Phase(cx, "p1_%d" % l)
            w1 = P.sb([128, 8, 2432], BF16, "w1")
            ws = P.dsem()
            segs = [(0, 0, 512), (512, 512, 64), (576, 512, 64), (640, 576, 64), (704, 576, 64),
                    (768, 640, 128), (896, 768, 512), (1408, 1280, 512), (1920, 1792, 512)]
            W1 = WB(P, w1, W["w_in"][l], segs)
            bia = P.sb([128, 19], F32, "bia")
            b_bia = Buf(bia)
            bin_l = W["b_in"][l]
            bpairs = [(bia[:, 0:4], bin_l[0:512].rearrange("(c p) -> p c", p=128)),
                      (bia[0:64, 4:5], bin_l[512:576].rearrange("(c p) -> p c", p=64)),
                      (bia[64:128, 4:5], bin_l[512:576].rearrange("(c p) -> p c", p=64)),
                      (bia[0:64, 5:6], bin_l[576:640].rearrange("(c p) -> p c", p=64)),
                      (bia[64:128, 5:6], bin_l[576:640].rearrange("(c p) -> p c", p=64)),
                      (bia[:, 6:10], bin_l[768:1280].rearrange("(c p) -> p c", p=128)),
                      (bia[:, 10:18], bin_l[1280:2304].rearrange("(c p) -> p c", p=128))]
            dma("sp", bpairs, [], [b_bia], ws, allow_slow_non_contiguous=True)
            vb = P.sb([128, 128], F32, "vb")
            b_vb = Buf(vb)
            dma("sp", [(vb[:], bin_l[640:768].partition_broadcast(128))], [], [b_vb], ws)
            d128 = P.sb([128, 256], BF16, "d128")
            b_d128 = Buf(d128)
            dma("sp", [(d128[:], C["c_d128"])], [], [b_d128], ws)

            NB = 2
            xb = [P.sb([128, 8, T], BF16, "xb") for _ in range(NB)]
            b_xb = [Buf(t) for t in xb]
            xs = [P.dsem() for _ in range(NB)]
            oq = [P.sb([128, 4, T], BF16, "oq") for _ in range(NB)]
            okd = [P.sb([128, 2, T], BF16, "okd") for _ in range(NB)]
            of = [P.sb([128, 4, T], BF16, "of") for _ in range(NB)]
            orx = [P.sb([128, 8, T], F32, "orx") for _ in range(NB)]
            orh = [P.sb([128, 8, T], BF16, "orh") for _ in range(NB)]
            orl = [P.sb([128, 8, T], BF16, "orl") for _ in range(NB)]
            b_orh = [[Buf(t[:, c]) for c in range(8)] for t in orh]
            b_orl = [[Buf(t[:, c]) for c in range(8)] for t in orl]
            ov = [P.sb([128, 4, 128], BF16, "ov") for _ in range(NB)]
            oa = [P.sb([128, 4, 1024], BF16, "oa") for _ in range(NB)]
            b_oq = [[Buf(t[:, c]) for c in range(4)] for t in oq]
            b_okd = [[Buf(t[:, c]) for c in range(2)] for t in okd]
            b_of = [[Buf(t[:, c]) for c in range(4)] for t in of]
            b_orx = [[Buf(t[:, c]) for c in range(8)] for t in orx]
            b_ov = [[Buf(t[:, c]) for c in range(4)] for t in ov]
            b_oa = [[Buf(t[:, c]) for c in range(4)] for t in oa]
            osem = [[P.dsem() for _ in range(7)] for _ in range(NB)]
            pm = [P.ps([128, 512], F32, "pm") for _ in range(4)]
            b_pm = [Buf(t, True) for t in pm]
            pa = [P.ps([128, 1024], F32, "pa") for _ in range(2)]
            b_pa = [Buf(t, True) for t in pa]

            tiles = [(S, i) for S in seqs for i in range(S // T)]

            def p1_load(n):
                S, i = tiles[n]
                sl = n % NB
                v = xin[S].rearrange("(k p) t -> p k t", p=128)
                dma("pool", [(xb[sl][:, :, :], v[:, :, i * T:(i + 1) * T])], [], [b_xb[sl]], xs[sl])

            p1_load(0)
            pmi = 0
            pai = 0
            for n, (S, i) in enumerate(tiles):
                sl = n % NB
                if n + 1 < len(tiles):
                    p1_load(n + 1)
                t0 = i * T
                sc = SCR[S]
                chunks = []
                for c in range(4):
                    chunks.append((c * 128, c, oq[sl][:, c, :], b_oq[sl][c]))
                for c in range(2):
                    chunks.append((512 + c * 128, 4 + c, okd[sl][:, c, :], b_okd[sl][c]))
                for c in range(4):
                    chunks.append((896 + c * 128, 6 + c, of[sl][:, c, :], b_of[sl][c]))
                for c in range(8):
                    chunks.append((1408 + c * 128, 10 + c, orx[sl][:, c, :], b_orx[sl][c]))
                for (wc, bc, oap, ob) in chunks:
                    pb = pm[pmi % 4]
                    bpb = b_pm[pmi % 4]
                    pmi += 1
                    for k in range(8):
                        op("pe", lambda e, k=k, wc=wc, pb=pb: e.matmul(pb[:], lhsT=w1[:, k, wc:wc + 128],
                                                                     rhs=xb[sl][:, k, :], start=(k == 0), stop=(k == 7)),
                           [W1.at(wc), W1.at(wc + 64), b_xb[sl]], [bpb])
                    op("act", lambda e, pb=pb, oap=oap, bc=bc: e.activation(out=oap, in_=pb[:], func=AF.Identity,
                                                                           bias=bia[:, bc:bc + 1], scale=1.0),
                       [bpb, b_bia], [ob])
                    if bc >= 10:
                        rc = bc - 10
                        op("pool", lambda e, rc=rc: e.tensor_copy(out=orh[sl][:, rc, :], in_=orx[sl][:, rc, :]), [ob], [b_orh[sl][rc]])
                        op("dve", lambda e, rc=rc: e.tensor_tensor(out=orl[sl][:, rc, :], in0=orx[sl][:, rc, :], in1=orh[sl][:, rc, :],
                                                                 op=ALU.subtract), [ob, b_orh[sl][rc]], [b_orl[sl][rc]])
                for j in range(4):
                    pb = pm[pmi % 4]
                    bpb = b_pm[pmi % 4]
                    pmi += 1
                    for k in range(8):
                        op("pe", lambda e, k=k, j=j, pb=pb: e.matmul(pb[:, 0:128], lhsT=xb[sl][:, k, j * 128:(j + 1) * 128],
                                                                   rhs=w1[:, k, 768:896], start=(k == 0), stop=(k == 7)),
                           [W1.at(768), b_xb[sl]], [bpb])
                    op("dve", lambda e, j=j, pb=pb: e.tensor_tensor(out=ov[sl][:, j, :], in0=pb[:, 0:128], in1=vb[:], op=ALU.add),
                       [bpb, b_vb], [b_ov[sl][j]])
                for j in range(4):
                    pb = pa[pai % 2]
                    bpb = b_pa[pai % 2]
                    pai += 1
                    for g in range(4):
                        op("pe", lambda e, j=j, g=g, pb=pb: e.matmul(pb[:, g * 256:(g + 1) * 256],
                                                                   lhsT=of[sl][:, g, j * 128:(j + 1) * 128],
                                                                   rhs=d128[:], start=True, stop=True),
                           [b_of[sl][g], b_d128], [bpb])
                    ov4 = oa[sl][:, j, :].rearrange("p (r g c) -> p g r c", r=2, g=4)
                    iv4 = pb[:].rearrange("p (g r c) -> p g r c", g=4, r=2)
                    op("dve" if j % 2 else "act",
                       (lambda e, ov4=ov4, iv4=iv4: e.tensor_copy(out=ov4, in_=iv4)) if j % 2 else
                       (lambda e, ov4=ov4, iv4=iv4: e.activation(out=ov4, in_=iv4, func=AF.Copy)),
                       [bpb], [b_oa[sl][j]])
                sls = slice(t0, t0 + T)
                dma("sp", [(sc["q"].rearrange("(c p) t -> p c t", p=128)[:, :, sls], oq[sl][:])], b_oq[sl], [], osem[sl][0])
                dma("sp", [(sc["kd"].rearrange("(c p) t -> p c t", p=128)[:, :, sls], okd[sl][:])], b_okd[sl], [], osem[sl][1])
                dma("sp", [(sc["rxh"].rearrange("(c p) t -> p c t", p=128)[:, :, sls], orh[sl][:])], b_orh[sl], [], osem[sl][2])
                dma("sp", [(sc["rxl"].rearrange("(c p) t -> p c t", p=128)[:, :, sls], orl[sl][:])], b_orl[sl], [], osem[sl][5])
                dma("sp", [(sc["v"][sls, :].rearrange("(j p) d -> p j d", p=128), ov[sl][:])], b_ov[sl], [], osem[sl][3])
                dma("sp", [(sc["a"][sls, :].rearrange("(j p) d -> p j d", p=128), oa[sl][:])], b_oa[sl], [], osem[sl][4])
            P.close()
            if done("p1"):
                break

            for dr in (0, 1):
                P = Phase(cx, "p2_%d_%d" % (l, dr))
                wr = P.sb([128, 8, 128], BF16, "wr")
                wi = P.sb([128, 8, 128], BF16, "wi")
                b_wr, b_wi = Buf(wr), Buf(wi)
                ws = P.dsem()
                op("dve", lambda e: e.memset(wr[:], 0.0), [], [b_wr])
                op("dve", lambda e: e.memset(wi[:], 0.0), [], [b_wi])
                vr = W["w_rg_r"][l, dr].rearrange("(c two) i o -> two i c o", two=2)
                vi = W["w_rg_i"][l, dr].rearrange("(c two) i o -> two i c o", two=2)
                dma("pool", [(wr[0:64, :, 0:64], vr[0]), (wr[64:128, :, 64:128], vr[1])], [], [b_wr], ws)
                dma("pool", [(wi[0:64, :, 0:64], vi[0]), (wi[64:128, :, 64:128], vi[1])], [], [b_wi], ws)
                prm = P.sb([128, 12, 8], F32, "prm")
                b_prm = Buf(prm)
                ppairs = [(prm[:, 0, :], W["b_rg_r"][l, dr].rearrange("(c p) -> p c", p=128)),
                          (prm[:, 1, :], W["b_rg_i"][l, dr].rearrange("(c p) -> p c", p=128)),
                          (prm[:, 2, :], W["rg_lambda"][l, dr].rearrange("(c p) -> p c", p=128)),
                          (prm[:, 3, :], W["conv_rnn_b"][l].rearrange("(c p) -> p c", p=128))]
                for k in range(4):
                    ppairs.append((prm[:, 4 + k, :], W["conv_rnn_w"][l, k].rearrange("(c p) -> p c", p=128)))
                dma("sp", ppairs, [], [b_prm], ws, allow_slow_non_contiguous=True)
                op("dve", lambda e: e.tensor_scalar_mul(out=prm[:, 8, :], in0=prm[:, 0, :], scalar1=0.5), [b_prm], [b_prm])
                op("dve", lambda e: e.tensor_scalar_mul(out=prm[:, 9, :], in0=prm[:, 1, :], scalar1=0.5), [b_prm], [b_prm])
                op("act", lambda e: e.activation(out=prm[:, 11, :], in_=prm[:, 2, :], func=AF.Exp, scale=-1.0), [b_prm], [b_prm])
                op("act", lambda e: e.activation(out=prm[:, 11, :], in_=prm[:, 11, :], func=AF.Ln, bias=1.0, scale=1.0), [b_prm], [b_prm])
                op("dve", lambda e: e.tensor_scalar_mul(out=prm[:, 10, :], in0=prm[:, 11, :], scalar1=-4.0), [b_prm], [b_prm])

                exh = [P.sb([128, 8, 515], BF16, "exh") for _ in range(2)]
                b_exh = [Buf(t) for t in exh]
                hsem = [P.dsem() for _ in range(2)]
                exl = [P.sb([128, 8, 515], BF16, "exl") for _ in range(2)]
                b_exl = [Buf(t) for t in exl]
                lsem2 = [P.dsem() for _ in range(2)]
                xr_t = [P.sb([128, T], F32, "xr") for _ in range(2)]
                b_xr = [Buf(t) for t in xr_t]
                xrb_t = [P.sb([128, T], BF16, "xrb") for _ in range(2)]
                b_xrb = [Buf(t) for t in xrb_t]
                tr_t = [P.sb([128, T], F32, "tr") for _ in range(2)]
                b_tr = [Buf(t) for t in tr_t]
                ti_t = [P.sb([128, T], F32, "ti") for _ in range(2)]
                b_ti = [Buf(t) for t in ti_t]
                A_t = [P.sb([128, 8, T], F32, "A") for _ in range(2)]
                TM_t = [P.sb([128, 8, T], F32, "TM") for _ in range(2)]
                A2_t = [P.sb([128, 8, T], F32, "A2") for _ in range(2)]
                b_A = [[Buf(t[:, c]) for c in range(8)] for t in A_t]
                b_TM = [[Buf(t[:, c]) for c in range(8)] for t in TM_t]
                b_A2 = [[Buf(t[:, c]) for c in range(8)] for t in A2_t]
                hosem = [P.dsem() for _ in range(2)]
                carry = P.sb([128, 8], F32, "carry")
                b_carry = [Buf(carry[:, c:c + 1]) for c in range(8)]
                if dr == 1:
                    hfi = P.sb([128, 8, T], F32, "hfi")
                    b_hfi = [Buf(hfi[:, c]) for c in range(8)]
                    hfsem = P.dsem()
                    hb16 = P.sb([128, 8, T], BF16, "hb16")
                    b_hb16 = [Buf(hb16[:, c]) for c in range(8)]
                    hbsem = P.dsem()
                pz = [P.ps([128, 512], F32, "pz") for _ in range(4)]
                b_pz = [Buf(t, True) for t in pz]
                pc = [P.ps([128, 512], F32, "pc") for _ in range(2)]
                b_pc = [Buf(t, True) for t in pc]
                idf = P.sb([128, 128], F32, "idf")
                b_idf = Buf(idf)
                dma("sp", [(idf[:], C["c_identf"])], [], [b_idf], ws)
                wsp = P.sb([128, 2, 4, 8], F32, "wsp")
                wb16 = P.sb([128, 4, 8], BF16, "wb16")
                b_wsp = Buf(wsp)
                op("dve", lambda e: e.tensor_copy(out=wb16[:], in_=prm[:, 4:8, :]), [b_prm], [b_wsp])
                op("dve", lambda e: e.tensor_copy(out=wsp[:, 0], in_=wb16[:]), [b_wsp], [b_wsp])
                op("dve", lambda e: e.tensor_tensor(out=wsp[:, 1], in0=prm[:, 4:8, :], in1=wsp[:, 0], op=ALU.subtract), [b_prm, b_wsp], [b_wsp])
                dgh = P.sb([128, 8, 4, 128], BF16, "dgh")
                dgl = P.sb([128, 8, 4, 128], BF16, "dgl")
                b_dg = Buf(dgh)
                for c in range(8):
                    for k in range(4):
                        op("dve", lambda e, c=c, k=k: e.tensor_scalar(out=dgh[:, c, k, :], in0=idf[:], scalar1=wsp[:, 0, k, c:c + 1],
                                                                      scalar2=None, op0=ALU.mult), [b_idf, b_wsp], [b_dg])
                        op("dve", lambda e, c=c, k=k: e.tensor_scalar(out=dgl[:, c, k, :], in0=idf[:], scalar1=wsp[:, 1, k, c:c + 1],
                                                                      scalar2=None, op0=ALU.mult), [b_idf, b_wsp], [b_dg])

                tl = []
                for S in seqs:
                    order = list(range(S // T))
                    if dr == 1:
                        order = order[::-1]
                    for pos, i in enumerate(order):
                        tl.append((S, i, pos == 0))
                NTL = len(tl)

                def rng(n):
                    S, i, first = tl[n]
                    t0 = i * T
                    lo = max(t0 - 2, 0)
                    hi = min(t0 + 513, S)
                    return S, t0, lo, hi

                def load_x(n):
                    if n >= NTL:
                        return
                    S, t0, lo, hi = rng(n)
                    sl = n % 2
                    for (tl_, bt, nm, sem) in ((exh[sl], b_exh[sl], "rxh", hsem[sl]), (exl[sl], b_exl[sl], "rxl", lsem2[sl])):
                        if t0 == 0:
                            op("dve", lambda e, tl_=tl_: e.memset(tl_[:, :, 0:2], 0.0), [], [bt])
                        if t0 + T == S:
                            op("dve", lambda e, tl_=tl_: e.memset(tl_[:, :, 514:515], 0.0), [], [bt])
                        v = SCR[S][nm].rearrange("(c p) t -> p c t", p=128)
                        dma("sp", [(tl_[:, :, lo - (t0 - 2):hi - (t0 - 2)], v[:, :, lo:hi])], [], [bt], sem)

                pcs = {}
                pzs = {}
                cnt = [0, 0]

                def L1_conv(n, c):
                    sl = n % 2
                    pcb = pc[cnt[0] % 2]; bpc = b_pc[cnt[0] % 2]; cnt[0] += 1
                    pcs[(n, c)] = (pcb, bpc)
                    trip = []
                    for k in range(4):
                        trip += [(dgh, exh[sl], k), (dgh, exl[sl], k), (dgl, exh[sl], k)]
                    for ti_, (dgt, xt_, k) in enumerate(trip):
                        op("pe", lambda e, pcb=pcb, dgt=dgt, xt_=xt_, k=k, ti_=ti_: e.matmul(
                            pcb[:], lhsT=dgt[:, c, k, :], rhs=xt_[:, c, k:k + 512], start=(ti_ == 0), stop=(ti_ == 11)),
                           [b_dg, b_exh[sl], b_exl[sl]], [bpc])

                def L1_evac_b(n, c):
                    s2 = c % 2
                    pcb, bpc = pcs[(n, c)]
                    op("act", lambda e: e.activation(out=xrb_t[s2][:], in_=pcb[:], func=AF.Identity,
                                                   bias=prm[:, 3, c:c + 1], scale=1.0), [bpc, b_prm], [b_xrb[s2]])

                def L1_evac_f(n, c):
                    s2 = c % 2
                    pcb, bpc = pcs.pop((n, c))
                    op("act", lambda e: e.activation(out=xr_t[s2][:], in_=pcb[:], func=AF.Identity,
                                                   bias=prm[:, 3, c:c + 1], scale=1.0), [bpc, b_prm], [b_xr[s2]])

                def L1_gates(n, c):
                    s2 = c % 2
                    pr = pz[cnt[1] % 4]; bpr = b_pz[cnt[1] % 4]; cnt[1] += 1
                    pi = pz[cnt[1] % 4]; bpi = b_pz[cnt[1] % 4]; cnt[1] += 1
                    pzs[(n, c)] = (pr, bpr, pi, bpi)
                    op("pe", lambda e: e.matmul(pr[:], lhsT=wr[:, c, :], rhs=xrb_t[s2][:], start=True, stop=True), [b_wr, b_xrb[s2]], [bpr])
                    op("pe", lambda e: e.matmul(pi[:], lhsT=wi[:, c, :], rhs=xrb_t[s2][:], start=True, stop=True), [b_wi, b_xrb[s2]], [bpi])

                def L1_act(n, c):
                    s2 = c % 2
                    sl = n % 2
                    pr, bpr, pi, bpi = pzs.pop((n, c))
                    op("act", lambda e: e.activation(out=tr_t[s2][:], in_=pr[:], func=AF.Tanh, bias=prm[:, 8, c:c + 1], scale=0.5),
                       [bpr, b_prm], [b_tr[s2]])
                    op("act", lambda e: e.activation(out=ti_t[s2][:], in_=pi[:], func=AF.Tanh, bias=prm[:, 9, c:c + 1], scale=0.5),
                       [bpi, b_prm], [b_ti[s2]])
                    op("act", lambda e: e.activation(out=A_t[sl][:, c, :], in_=tr_t[s2][:], func=AF.Exp,
                                                   bias=prm[:, 10, c:c + 1], scale=prm[:, 10, c:c + 1]), [b_tr[s2], b_prm], [b_A[sl][c]])
                    op("dve", lambda e: e.scalar_tensor_tensor(out=TM_t[sl][:, c, :], in0=ti_t[s2][:], scalar=1.0, in1=xr_t[s2][:],
                                                             op0=ALU.add, op1=ALU.mult), [b_ti[s2], b_xr[s2]], [b_TM[sl][c]])
                    op("pool", lambda e: e.tensor_tensor(out=A2_t[sl][:, c, :], in0=A_t[sl][:, c, :], in1=A_t[sl][:, c, :], op=ALU.mult),
                       [b_A[sl][c]], [b_A2[sl][c]])

                def L2_head(m):
                    sl = m % 2
                    S, i, first = tl[m]
                    if dr == 1:
                        v = SCR[S]["hf"].rearrange("(c p) t -> p c t", p=128)
                        dma("sp", [(hfi[:], v[:, :, i * T:(i + 1) * T])], [], b_hfi, hfsem)
                    for c in range(8):
                        op("act", lambda e, c=c: e.activation(out=A2_t[sl][:, c, :], in_=A2_t[sl][:, c, :], func=AF.Sqrt, bias=1.0, scale=-1.0),
                           [b_A2[sl][c]], [b_A2[sl][c]])

                def L2_chunk(m, c):
                    sl = m % 2
                    S, i, first = tl[m]
                    TMc = TM_t[sl][:, c, :]
                    op("dve", lambda e: e.scalar_tensor_tensor(out=TMc, in0=TMc, scalar=0.5, in1=A2_t[sl][:, c, :],
                                                             op0=ALU.mult, op1=ALU.mult), [b_TM[sl][c], b_A2[sl][c]], [b_TM[sl][c]])
                    rds = [b_A[sl][c], b_TM[sl][c]]
                    if first:
                        init = 0.0
                    else:
                        init = carry[:, c:c + 1]
                        rds = rds + [b_carry[c]]
                    if dr == 0:
                        op("dve", lambda e: e.tensor_tensor_scan(out=TMc, data0=A_t[sl][:, c, :], data1=TMc,
                                                               initial=init, op0=ALU.mult, op1=ALU.add), rds, [b_TM[sl][c]])
                        op("dve", lambda e: e.tensor_copy(out=carry[:, c:c + 1], in_=TM_t[sl][:, c, 511:512]), [b_TM[sl][c]], [b_carry[c]])
                    else:
                        op("dve", lambda e: e.tensor_tensor_scan(out=TM_t[sl][:, c, ::-1], data0=A_t[sl][:, c, ::-1], data1=TM_t[sl][:, c, ::-1],
                                                               initial=init, op0=ALU.mult, op1=ALU.add), rds, [b_TM[sl][c]])
                        op("dve", lambda e: e.tensor_copy(out=carry[:, c:c + 1], in_=TM_t[sl][:, c, 0:1]), [b_TM[sl][c]], [b_carry[c]])
                        op("pool", lambda e: e.tensor_tensor(out=hb16[:, c, :], in0=TMc, in1=hfi[:, c, :], op=ALU.add),
                           [b_TM[sl][c], b_hfi[c]], [b_hb16[c]])

                def L2_tail(m):
                    S, i, first = tl[m]
                    sl = m % 2
                    sls = slice(i * T, (i + 1) * T)
                    if dr == 0:
                        v = SCR[S]["hf"].rearrange("(c p) t -> p c t", p=128)
                        dma("sp", [(v[:, :, sls], TM_t[sl][:])], b_TM[sl], [], hosem[sl])
                    else:
                        v = SCR[S]["h"].rearrange("(c p) t -> p c t", p=128)
                        dma("sp", [(v[:, :, sls], hb16[:])], b_hb16, [], hbsem)

                load_x(0)
                load_x(1)
                for n in range(NTL + 1):
                    live = n < NTL
                    if n >= 1:
                        L2_head(n - 1)
                    if live:
                        L1_conv(n, 0)
                    for c in range(8):
                        if live:
                            if c + 1 < 8:
                                L1_conv(n, c + 1)
                            L1_evac_b(n, c)
                            L1_gates(n, c)
                            if c >= 1:
                                L1_act(n, c - 1)
                            L1_evac_f(n, c)
                        if n >= 1:
                            L2_chunk(n - 1, c)
                    if live:
                        L1_act(n, 7)
                        load_x(n + 2)
                    if n >= 1:
                        L2_tail(n - 1)
                P.close()
            if done("p2"):
                break

            for S in seqs:
                N2 = S // 128
                sc = SCR[S]
                P = Phase(cx, "p3a_%d_%d" % (l, S))
                ec = P.sb([128, N2, 128], BF16, "ec")
                es = P.sb([128, N2, 128], BF16, "es")
                en = P.sb([128, N2, 128], BF16, "en")
                b_tab = Buf(ec)
                ws = P.dsem()
                dma("sp", [(ec[:], C["c_ec%d" % S].rearrange("(a b) k -> a b k", b=N2)),
                           (es[:], C["c_es%d" % S].rearrange("(a b) k -> a b k", b=N2)),
                           (en[:], C["c_en%d" % S].rearrange("(a b) k -> a b k", b=N2))], [], [b_tab], ws)
                GB = 4
                At = [P.sb([128, GB, 1024], BF16, "At") for _ in range(2)]
                b_At = [Buf(t) for t in At]
                asem = [P.dsem() for _ in range(2)]
                Bt = [P.sb([128, GB, 1024], BF16, "Bt") for _ in range(2)]
                b_Bt = [[Buf(t[:, j]) for j in range(GB)] for t in Bt]
                bsem = [P.dsem() for _ in range(2)]
                pp = [P.ps([128, 512], F32, "pp") for _ in range(4)]
                b_pp = [Buf(t, True) for t in pp]
                av = sc["a"].rearrange("(a b) c -> a b c", b=N2)
                ngr = N2 // GB

                def p3_load(gi):
                    sl = gi % 2
                    dma("sp", [(At[sl][:], av[:, gi * GB:(gi + 1) * GB, :])], [], [b_At[sl]], asem[sl])

                p3_load(0)
                ppi = 0
                for gi in range(ngr):
                    sl = gi % 2
                    if gi + 1 < ngr:
                        p3_load(gi + 1)
                    for j in range(GB):
                        s2 = gi * GB + j
                        pre = pp[ppi % 4]; bpre = b_pp[ppi % 4]; ppi += 1
                        pim = pp[ppi % 4]; bpim = b_pp[ppi % 4]; ppi += 1
                        a_re = At[sl][:, j, 0:512]
                        a_im = At[sl][:, j, 512:1024]
                        op("pe", lambda e, pre=pre, s2=s2, a_re=a_re: e.matmul(pre[:], lhsT=ec[:, s2, :], rhs=a_re, start=True, stop=False),
                           [b_tab, b_At[sl]], [bpre])
                        op("pe", lambda e, pre=pre, s2=s2, a_im=a_im: e.matmul(pre[:], lhsT=es[:, s2, :], rhs=a_im, start=False, stop=True),
                           [b_tab, b_At[sl]], [bpre])
                        op("pe", lambda e, pim=pim, s2=s2, a_im=a_im: e.matmul(pim[:], lhsT=ec[:, s2, :], rhs=a_im, start=True, stop=False),
                           [b_tab, b_At[sl]], [bpim])
                        op("pe", lambda e, pim=pim, s2=s2, a_re=a_re: e.matmul(pim[:], lhsT=en[:, s2, :], rhs=a_re, start=False, stop=True),
                           [b_tab, b_At[sl]], [bpim])
                        op("act", lambda e, pre=pre, j=j: e.activation(out=Bt[sl][:, j, 0:512], in_=pre[:], func=AF.Copy),
                           [bpre], [b_Bt[sl][j]])
                        op("dve", lambda e, pim=pim, j=j: e.tensor_copy(out=Bt[sl][:, j, 512:1024], in_=pim[:]),
                           [bpim], [b_Bt[sl][j]])
                    dma("sp", [(sc["bp"][:, gi * GB:(gi + 1) * GB, :], Bt[sl][:])], b_Bt[sl], [], bsem[sl])
                P.close()

                P = Phase(cx, "p3b_%d_%d" % (l, S))
                cs2 = P.sb([2 * N2, N2], BF16, "cs2")
                b_cs2 = Buf(cs2)
                ws = P.dsem()
                dma("sp", [(cs2[:], C["c_cs%d" % S])], [], [b_cs2], ws)
                yts = P.sb([128, 4, S], BF16, "yts")
                b_yts = [Buf(yts[:, g]) for g in range(4)]
                ysem = P.dsem()
                KB = 8
                Bs = [P.sb([2 * N2, KB, 512], BF16, "Bs") for _ in range(2)]
                b_Bs = [Buf(t) for t in Bs]
                ssem = [P.dsem() for _ in range(2)]
                pq = [P.ps([128, 512], F32, "pq") for _ in range(4)]
                b_pq = [Buf(t, True) for t in pq]
                nk = 128 // KB

                def p3b_load(ki):
                    sl = ki % 2
                    src = sc["bp"][ki * KB:(ki + 1) * KB]
                    dma("sp", [(Bs[sl][0:N2, :, :], src[:, :, 0:512].rearrange("k s c -> s k c")),
                               (Bs[sl][N2:2 * N2, :, :], src[:, :, 512:1024].rearrange("k s c -> s k c"))],
                        [], [b_Bs[sl]], ssem[sl])

                p3b_load(0)
                pqi = 0
                for ki in range(nk):
                    sl = ki % 2
                    if ki + 1 < nk:
                        p3b_load(ki + 1)
                    for g in range(4):
                        pb = pq[pqi % 4]; bpb = b_pq[pqi % 4]; pqi += 1
                        for kl in range(KB):
                            op("pe", lambda e, pb=pb, kl=kl, g=g: e.matmul(
                                pb[:, kl * N2:(kl + 1) * N2], lhsT=Bs[sl][:, kl, g * 128:(g + 1) * 128], rhs=cs2[:],
                                start=True, stop=True), [b_Bs[sl], b_cs2], [bpb])
                        oap = yts[:, g, :].rearrange("p (k2 k1) -> p k1 k2", k1=128)[:, ki * KB:(ki + 1) * KB, :]
                        iap = pb[:, 0:KB * N2].rearrange("p (a b) -> p a b", b=N2)
                        if g % 2 == 0:
                            op("act", lambda e, oap=oap, iap=iap: e.activation(out=oap, in_=iap, func=AF.Copy), [bpb], [b_yts[g]])
                        else:
                            op("dve", lambda e, oap=oap, iap=iap: e.tensor_copy(out=oap, in_=iap), [bpb], [b_yts[g]])
                dma("sp", [(sc["yt"].rearrange("(g p) s -> p g s", p=128), yts[:])], b_yts, [], ysem)
                P.close()
            if done("p3"):
                break

            PW = Phase(cx, "pw_%d" % l)
            w4 = PW.sb([128, 8, 4096], BF16, "w4")
            wao = PW.sb([128, 4, 1024], BF16, "wao")
            wfo = PW.sb([128, 4, 1024], BF16, "wfo")
            wro = PW.sb([128, 8, 1024], BF16, "wro")
            wo = PW.sb([128, 8, 1024], BF16, "wo")
            wsw = PW.dsem()
            blk4 = [(0, 2304, 512), (512, 2816, 512)]
            for hlf in range(2):
                for b in range(3):
                    blk4.append((1024 + b * 1024 + hlf * 512, 3328 + b * 1024 + hlf * 512, 512))
            W4 = WB(PW, w4, W["w_in"][l], blk4[:5])
            b_wao = WB(PW, wao, W["w_att_o"][l], [(0, 0, 1024)]).at(0)
            b_wfo = WB(PW, wfo, W["w_four_o"][l], [(0, 0, 1024)]).at(0)
            b_wro = WB(PW, wro, W["w_rnn_o"][l], [(0, 0, 1024)]).at(0)
            W4.add(blk4[5:])
            b_wo = WB(PW, wo, W["w_out"][l], [(0, 0, 1024)]).at(0)
            prm = PW.sb([128, 56], F32, "prm4")
            b_prm = Buf(prm)
            dma("sp", [(prm[:, 0:32], W["b_in"][l][2304:6400].rearrange("(c p) -> p c", p=128)),
                       (prm[:, 32:40], W["b_out"][l].rearrange("(c p) -> p c", p=128)),
                       (prm[:, 40:48], W["ln1_g"][l].rearrange("(c p) -> p c", p=128)),
                       (prm[:, 48:56], W["ln1_b"][l].rearrange("(c p) -> p c", p=128))], [], [b_prm], wsw,
                allow_slow_non_contiguous=True)
            P = Phase(cx, "p4a_%d" % l)
            band = P.sb([128, 8, 384], F32, "band")
            b_band = Buf(band)
            bsm = P.dsem()
            with ExitStack() as st1:
                rr = st1.enter_context(nc.sbuf_tensor("g_rr%d" % l, [128, 8, 384], F32))
                jm = st1.enter_context(nc.sbuf_tensor("g_jm%d" % l, [128, 128], F32))
                bps = st1.enter_context(nc.psum_tensor("g_bps%d" % l, [128, 512], F32))
                b_rr, b_jm, b_bps = Buf(rr), Buf(jm), Buf(bps, True)
                src = bass.AP(tb_dram, 1, [[1, 128], [512, 8], [1, 384]])
                dma("sp", [(rr[:], src), (jm[:], C["c_jrev"])], [], [b_rr, b_jm], bsm)
                rrf = rr[:].rearrange("p h j -> p (h j)")
                bandf = band[:].rearrange("p h j -> p (h j)")
                for cc in range(6):
                    op("pe", lambda e, cc=cc: e.matmul(bps[:], lhsT=jm[:], rhs=rrf[:, cc * 512:(cc + 1) * 512],
                                                       start=True, stop=True), [b_rr, b_jm], [b_bps])
                    op("dve", lambda e, cc=cc: e.tensor_copy(out=bandf[:, cc * 512:(cc + 1) * 512], in_=bps[:]),
                       [b_bps], [b_band])
                cx.barrier()
            skb = P.sb([128, 8], F32, "skb")
            nskb = P.sb([128, 8], F32, "nskb")
            b_skb = Buf(skb)
            ws = P.dsem()
            dma("sp", [(skb[:], W["attn_sink"][l].partition_broadcast(128))], [], [b_skb], ws)
            op("dve", lambda e: e.tensor_scalar_mul(out=nskb[:], in0=skb[:], scalar1=-1.0), [b_skb], [b_skb])
            qt = [P.sb([128, 4, T], BF16, "qt") for _ in range(2)]
            kt = [P.sb([128, 2, 768], BF16, "kt") for _ in range(2)]
            vt = [P.sb([128, 6, 128], BF16, "vt") for _ in range(2)]
            b_qt = [Buf(t) for t in qt]
            b_kt = [Buf(t) for t in kt]
            b_vt = [Buf(t) for t in vt]
            lsem = [[P.dsem() for _ in range(3)] for _ in range(2)]
            aT = [P.sb([128, 4, T], BF16, "aT") for _ in range(2)]
            b_aT = [[Buf(t[:, :, j * 128:(j + 1) * 128]) for j in range(4)] for t in aT]
            atsem = [P.dsem() for _ in range(2)]
            S4 = P.ps([128, 4, 512], F32, "S4")
            b_S4 = Buf(S4, True)
            sb4 = [P.sb([128, 4, 384], F32, "sb4") for _ in range(2)]
            b_sb4 = [Buf(t) for t in sb4]
            P4 = [P.sb([128, 4, 384], BF16, "P4") for _ in range(2)]
            b_P4 = [[Buf(t[:, hh]) for hh in range(4)] for t in P4]
            ptp = [P.ps([128, 2, 3, 128], BF16, "ptp") for _ in range(2)]
            b_ptp = [Buf(t, True) for t in ptp]
            PT = [P.sb([128, 4, 3, 128], BF16, "PT") for _ in range(2)]
            b_PT = [[Buf(t[:, 0:2]), Buf(t[:, 2:4])] for t in PT]
            ops1 = P.ps([128, 512], F32, "ops1")
            b_ops1 = Buf(ops1, True)
            atm = [P.sb([128, 512], BF16, "atm") for _ in range(2)]
            b_atm = [[Buf(t[:, 0:256]), Buf(t[:, 256:512])] for t in atm]
            tpp = P.ps([128, 4, 128], BF16, "tpp")
            b_tpp = Buf(tpp, True)
            stt = [P.sb([128, 24], F32, "stt") for _ in range(4)]
            b_stt = [Buf(t) for t in stt]
            tiles = [(S, i) for S in seqs for i in range(S // T)]

            def p4a_load(n):
                if n >= len(tiles):
                    return
                S, i = tiles[n]
                sl = n % 2
                t0 = i * T
                sc = SCR[S]
                dma("sp", [(qt[sl][:], sc["q"].rearrange("(c p) t -> p c t", p=128)[:, :, t0:t0 + T])], [], [b_qt[sl]], lsem[sl][0])
                lo = max(t0 - 128, 0)
                hi = min(t0 + 640, S)
                dma("sp", [(kt[sl][:, :, lo - (t0 - 128):hi - (t0 - 128)],
                            sc["kd"].rearrange("(c p) t -> p c t", p=128)[:, :, lo:hi])], [], [b_kt[sl]], lsem[sl][1])
                b0 = (lo - (t0 - 128)) // 128
                nb_ = (hi - lo) // 128
                dma("sp", [(vt[sl][:, b0:b0 + nb_, :], sc["v"][lo:hi, :].rearrange("(j p) d -> p j d", p=128))],
                    [], [b_vt[sl]], lsem[sl][2])

            groups = [(n, jb, g) for n in range(len(tiles)) for jb in range(4) for g in range(2)]

            def geom(n, jb):
                S, i = tiles[n]
                t0 = i * T
                gb = t0 // 128 + jb
                kb0 = 1 if gb == 0 else 0
                kb1 = 2 if gb == S // 128 - 1 else 3
                return kb0, kb1 - kb0

            def gparams(gi):
                n, jb, g = groups[gi]
                kb0, nb = geom(n, jb)
                return n, jb, g, n % 2, gi % 2, stt[gi % 4], b_stt[gi % 4], kb0, nb

            def pe_qk(gi):
                n, jb, g, sl, r2, st, bst, kb0, nb = gparams(gi)
                Wd = nb * 128
                kw0 = (jb + kb0) * 128
                for hh in range(4):
                    h = g * 4 + hh
                    c, half = h // 2, h % 2
                    pr = slice(half * 64, half * 64 + 64)
                    op("pe", lambda e, hh=hh, pr=pr, c=c: e.matmul(
                        S4[:, hh, 0:Wd], lhsT=qt[sl][pr, c, jb * 128:(jb + 1) * 128], rhs=kt[sl][pr, g, kw0:kw0 + Wd],
                        start=True, stop=True), [b_qt[sl], b_kt[sl]], [b_S4])

            def dve_softmax_pre(gi):
                n, jb, g, sl, r2, st, bst, kb0, nb = gparams(gi)
                Wd = nb * 128
                bw0 = kb0 * 128
                op("dve", lambda e: e.scalar_tensor_tensor(
                    out=sb4[r2][:, :, 0:Wd], in0=S4[:, :, 0:Wd], scalar=0.125, in1=band[:, g * 4:(g + 1) * 4, bw0:bw0 + Wd],
                    op0=ALU.mult, op1=ALU.add), [b_S4, b_band], [b_sb4[r2]])
                op("dve", lambda e: e.reduce_max(out=st[:, 0:4], in_=sb4[r2][:, :, 0:Wd], axis=AX.X), [b_sb4[r2]], [bst])
                op("dve", lambda e: e.scalar_tensor_tensor(out=st[:, 4:8], in0=st[:, 0:4], scalar=-1.0, in1=nskb[:, g * 4:(g + 1) * 4],
                                                         op0=ALU.mult, op1=ALU.min), [bst, b_skb], [bst])
                op("dve", lambda e: e.tensor_tensor(out=st[:, 12:16], in0=skb[:, g * 4:(g + 1) * 4], in1=st[:, 4:8], op=ALU.add),
                   [bst, b_skb], [bst])

            def act_exp(gi):
                n, jb, g, sl, r2, st, bst, kb0, nb = gparams(gi)
                Wd = nb * 128
                for hh in range(4):
                    op("act", lambda e, hh=hh: e.activation(out=P4[r2][:, hh, 0:Wd], in_=sb4[r2][:, hh, 0:Wd], func=AF.Exp,
                                                          bias=st[:, 4 + hh:5 + hh], scale=1.0, accum_out=st[:, 8 + hh:9 + hh]),
                       [b_sb4[r2], bst], [b_P4[r2][hh], bst])
                op("act", lambda e: e.activation(out=st[:, 12:16], in_=st[:, 12:16], func=AF.Exp), [bst], [bst])

            def pe_T(gi):
                n, jb, g, sl, r2, st, bst, kb0, nb = gparams(gi)
                for hh in range(4):
                    for kb in range(nb):
                        op("pe", lambda e, hh=hh, kb=kb: e.transpose(out=ptp[hh // 2][:, hh % 2, kb, :],
                                                                   in_=P4[r2][:, hh, kb * 128:(kb + 1) * 128], identity=ident[:]),
                           [b_P4[r2][hh], b_ident], [b_ptp[hh // 2]])

            def copies_den(gi):
                n, jb, g, sl, r2, st, bst, kb0, nb = gparams(gi)
                op("act", lambda e: e.activation(out=PT[r2][:, 0:2, 0:nb, :], in_=ptp[0][:, :, 0:nb, :], func=AF.Copy),
                   [b_ptp[0]], [b_PT[r2][0]])
                op("dve", lambda e: e.tensor_copy(out=PT[r2][:, 2:4, 0:nb, :], in_=ptp[1][:, :, 0:nb, :]), [b_ptp[1]], [b_PT[r2][1]])
                op("dve", lambda e: e.tensor_tensor(out=st[:, 16:20], in0=st[:, 8:12], in1=st[:, 12:16], op=ALU.add), [bst], [bst])
                op("dve", lambda e: e.reciprocal(out=st[:, 20:24], in_=st[:, 16:20]), [bst], [bst])

            def pe_pv(gi):
                n, jb, g, sl, r2, st, bst, kb0, nb = gparams(gi)
                for hh in range(4):
                    h = g * 4 + hh
                    for kb in range(nb):
                        op("pe", lambda e, hh=hh, kb=kb, h=h: e.matmul(
                            ops1[:, h * 64:(h + 1) * 64], lhsT=PT[r2][:, hh, kb, :],
                            rhs=vt[sl][:, jb + kb0 + kb, g * 64:(g + 1) * 64], start=(kb == 0), stop=(kb == nb - 1)),
                           [b_PT[r2][hh // 2], b_vt[sl]], [b_ops1])

            def act_norm(gi):
                n, jb, g, sl, r2, st, bst, kb0, nb = gparams(gi)
                osl = (gi // 2) % 2
                for hh in range(4):
                    h = g * 4 + hh
                    op("act", lambda e, hh=hh, h=h: e.activation(out=atm[osl][:, h * 64:(h + 1) * 64], in_=ops1[:, h * 64:(h + 1) * 64],
                                                               func=AF.Identity, scale=st[:, 20 + hh:21 + hh]),
                       [b_ops1, bst], [b_atm[osl][g]])

            def finalize(gi):
                n, jb, g, sl, r2, st, bst, kb0, nb = gparams(gi)
                S, i = tiles[n]
                osl = (gi // 2) % 2
                for c in range(4):
                    op("pe", lambda e, c=c: e.transpose(out=tpp[:, c, :], in_=atm[osl][:, c * 128:(c + 1) * 128], identity=ident[:]),
                       [b_atm[osl][c // 2], b_ident], [b_tpp])
                op("act", lambda e: e.activation(out=aT[sl][:, :, jb * 128:(jb + 1) * 128], in_=tpp[:], func=AF.Copy),
                   [b_tpp], [b_aT[sl][jb]])
                if jb == 3:
                    t0 = i * T
                    dma("sp", [(SCR[S]["at"].rearrange("(c p) t -> p c t", p=128)[:, :, t0:t0 + T], aT[sl][:])],
                        b_aT[sl], [], atsem[sl])

            p4a_load(0)
            p4a_load(1)
            NG = len(groups)
            for sidx in range(NG + 4):
                g2, g1, g0, g3 = sidx - 2, sidx - 1, sidx, sidx - 3
                if 0 <= g2 < NG:
                    copies_den(g2)
                if g0 < NG:
                    pe_qk(g0)
                if 0 <= g2 < NG:
                    pe_pv(g2)
                if g0 < NG:
                    dve_softmax_pre(g0)
                if 0 <= g1 < NG:
                    act_exp(g1)
                    pe_T(g1)
                if 0 <= g2 < NG:
                    act_norm(g2)
                    if groups[g2][1] == 3 and groups[g2][2] == 1:
                        p4a_load(groups[g2][0] + 2)
                if 0 <= g3 < NG and groups[g3][2] == 1:
                    finalize(g3)
            P.close()
            if done("p4a"):
                PW.close()
                break

            P = Phase(cx, "p4b_%d" % l)
            xf = P.sb([128, 8, T], F32, "xf")
            b_xf = [Buf(xf[:, c]) for c in range(8)]
            xb = P.sb([128, 8, T], BF16, "xb")
            b_xb = Buf(xb)
            att = P.sb([128, 4, T], BF16, "att")
            b_att = Buf(att)
            ytt = P.sb([128, 4, T], BF16, "ytt")
            b_ytt = Buf(ytt)
            ht = P.sb([128, 8, T], BF16, "ht")
            b_ht = [Buf(ht[:, c]) for c in range(8)]
            hg = P.sb([128, 8, T], BF16, "hg")
            b_hg = [Buf(hg[:, c]) for c in range(8)]
            mixed = P.sb([128, 8, T], BF16, "mixed")
            b_mixed = [Buf(mixed[:, c]) for c in range(8)]
            zsq = P.sb([128, 8, T], BF16, "zsq")
            b_zsq = [Buf(zsq[:, c]) for c in range(8)]
            sems4 = [P.dsem() for _ in range(6)]
            gy = [P.sb([128, T], F32, "gy") for _ in range(2)]
            b_gy = [Buf(t) for t in gy]
            sg = [P.sb([128, T], F32, "sg") for _ in range(4)]
            b_sg = [Buf(t) for t in sg]
            tm = [P.sb([128, T], F32, "tm") for _ in range(4)]
            b_tm = [Buf(t) for t in tm]
            lnt = [P.sb([128, T], F32, "lnt") for _ in range(3)]
            b_lnt = [Buf(t) for t in lnt]
            pg = [P.ps([128, 512], F32, "pg") for _ in range(3)]
            b_pg = [Buf(t, True) for t in pg]
            po = [P.ps([128, 512], F32, "po") for _ in range(3)]
            b_po = [Buf(t, True) for t in po]
            pl = [P.ps([128, 512], F32, "pl") for _ in range(2)]
            b_pl = [Buf(t, True) for t in pl]

            def p4b_load(n, which):
                S, i = tiles[n]
                sls = slice(i * T, (i + 1) * T)
                sc = SCR[S]
                if which == 0:
                    dma("pool", [(xb[:], xin[S].rearrange("(k p) t -> p k t", p=128)[:, :, sls])], [], [b_xb], sems4[0])
                    dma("sp", [(ht[:], sc["h"].rearrange("(c p) t -> p c t", p=128)[:, :, sls])], [], b_ht, sems4[1])
                    dma("sp", [(att[:], sc["at"].rearrange("(c p) t -> p c t", p=128)[:, :, sls])], [], [b_att], sems4[2])
                    dma("sp", [(ytt[:], sc["yt"].rearrange("(c p) t -> p c t", p=128)[:, :, sls])], [], [b_ytt], sems4[3])
                else:
                    dma("sp", [(xf[:], xin[S].rearrange("(k p) t -> p k t", p=128)[:, :, sls])], [], b_xf, sems4[4])

            def layer_norm(xfb, b_xfb, zb_t, b_zb, zsq_t, b_zsq_, gcol, bcol, prm_t, b_prm_t):
                for m in range(8):
                    op("pe", lambda e, m=m: e.matmul(pl[0][:], lhsT=onesm[:], rhs=zb_t[:, m, 0:T], start=(m == 0), stop=(m == 7)),
                       [b_onesm, b_zb[m]], [b_pl[0]])
                for m in range(8):
                    op("pe", lambda e, m=m: e.matmul(pl[1][:], lhsT=onesm[:], rhs=zsq_t[:, m, :], start=(m == 0), stop=(m == 7)),
                       [b_onesm, b_zsq_[m]], [b_pl[1]])
                op("act", lambda e: e.activation(out=lnt[0][:], in_=pl[0][:], func=AF.Copy), [b_pl[0]], [b_lnt[0]])
                op("act", lambda e: e.activation(out=lnt[1][:], in_=pl[0][:], func=AF.Square), [b_pl[0]], [b_lnt[1]])
                op("dve", lambda e: e.tensor_tensor(out=lnt[1][:], in0=pl[1][:], in1=lnt[1][:], op=ALU.subtract), [b_pl[1], b_lnt[1]], [b_lnt[1]])
                op("act", lambda e: e.activation(out=lnt[1][:], in_=lnt[1][:], func=AF.Sqrt, bias=EPS, scale=1.0), [b_lnt[1]], [b_lnt[1]])
                ri = 2 if len(lnt) > 2 else 1
                op("dve", lambda e: e.reciprocal(out=lnt[ri][:], in_=lnt[1][:]), [b_lnt[1]], [b_lnt[ri]])
                for m in range(8):
                    op("dve", lambda e, m=m: e.tensor_tensor(out=xfb[:, m, :], in0=xfb[:, m, :], in1=lnt[0][:], op=ALU.subtract),
                       [b_xfb[m], b_lnt[0]], [b_xfb[m]])
                    op("dve", lambda e, m=m: e.tensor_tensor(out=xfb[:, m, :], in0=xfb[:, m, :], in1=lnt[ri][:], op=ALU.mult),
                       [b_xfb[m], b_lnt[ri]], [b_xfb[m]])
                    op("act", lambda e, m=m: e.activation(out=xfb[:, m, :], in_=xfb[:, m, :], func=AF.Identity,
                                                        bias=prm_t[:, bcol + m:bcol + m + 1], scale=prm_t[:, gcol + m:gcol + m + 1]),
                       [b_xfb[m], b_prm_t], [b_xfb[m]])

            p4b_load(0, 0)
            p4b_load(0, 1)
            pgi = 0
            poi = 0
            for n, (S, i) in enumerate(tiles):
                sls = slice(i * T, (i + 1) * T)
                for c in range(8):
                    pb = pg[pgi % 3]; bpb = b_pg[pgi % 3]; pgi += 1
                    for k in range(8):
                        op("pe", lambda e, pb=pb, k=k, c=c: e.matmul(pb[:], lhsT=w4[:, k, c * 128:(c + 1) * 128], rhs=xb[:, k, :],
                                                                   start=(k == 0), stop=(k == 7)), [W4.at(c * 128), b_xb], [bpb])
                    op("act", lambda e, pb=pb, c=c: e.activation(out=gy[c % 2][:], in_=pb[:], func=AF.Gelu_apprx_tanh,
                                                               bias=prm[:, c:c + 1], scale=1.0), [bpb, b_prm], [b_gy[c % 2]])
                    op("pool", lambda e, c=c: e.tensor_tensor(out=hg[:, c, :], in0=ht[:, c, :], in1=gy[c % 2][:], op=ALU.mult),
                       [b_ht[c], b_gy[c % 2]], [b_hg[c]])
                sgi = 0
                for m in range(8):
                    srcs = [(wao, b_wao, att, [b_att] * 4, 4), (wfo, b_wfo, ytt, [b_ytt] * 4, 4), (wro, b_wro, hg, b_hg, 8)]
                    tms = []
                    for b in range(3):
                        pb = pg[pgi % 3]; bpb = b_pg[pgi % 3]; pgi += 1
                        col = 1024 + b * 1024 + m * 128
                        for k in range(8):
                            op("pe", lambda e, pb=pb, k=k, col=col: e.matmul(pb[:], lhsT=w4[:, k, col:col + 128], rhs=xb[:, k, :],
                                                                           start=(k == 0), stop=(k == 7)), [W4.at(col), b_xb], [bpb])
                        sgt = sg[sgi % 4]; bsg = b_sg[sgi % 4]
                        tmt = tm[sgi % 4]; btm = b_tm[sgi % 4]
                        sgi += 1
                        op("act", lambda e, pb=pb, sgt=sgt, b=b, m=m: e.activation(out=sgt[:], in_=pb[:], func=AF.Sigmoid,
                                                                                 bias=prm[:, 8 + b * 8 + m:9 + b * 8 + m], scale=1.0),
                           [bpb, b_prm], [bsg])
                        wt, bwt, rt, brt, nk_ = srcs[b]
                        pb2 = po[poi % 3]; bpb2 = b_po[poi % 3]; poi += 1
                        for k in range(nk_):
                            op("pe", lambda e, pb2=pb2, k=k, wt=wt, rt=rt, m=m, nk_=nk_: e.matmul(
                                pb2[:], lhsT=wt[:, k, m * 128:(m + 1) * 128], rhs=rt[:, k, :], start=(k == 0), stop=(k == nk_ - 1)),
                               [bwt, brt[k]], [bpb2])
                        op("dve", lambda e, pb2=pb2, sgt=sgt, tmt=tmt: e.tensor_tensor(out=tmt[:], in0=pb2[:], in1=sgt[:], op=ALU.mult),
                           [bpb2, bsg], [btm])
                        tms.append((tmt, btm))
                    op("pool", lambda e, tms=tms: e.tensor_tensor(out=tms[0][0][:], in0=tms[0][0][:], in1=tms[1][0][:], op=ALU.add),
                       [tms[0][1], tms[1][1]], [tms[0][1]])
                    op("pool", lambda e, tms=tms, m=m: e.tensor_tensor(out=mixed[:, m, :], in0=tms[0][0][:], in1=tms[2][0][:], op=ALU.add),
                       [tms[0][1], tms[2][1]], [b_mixed[m]])
                for m in range(8):
                    pb2 = po[poi % 3]; bpb2 = b_po[poi % 3]; poi += 1
                    for k in range(8):
                        op("pe", lambda e, pb2=pb2, k=k, m=m: e.matmul(pb2[:], lhsT=wo[:, k, m * 128:(m + 1) * 128], rhs=mixed[:, k, :],
                                                                     start=(k == 0), stop=(k == 7)), [b_wo, b_mixed[k]], [bpb2])
                    tmt = tm[m % 4]; btm = b_tm[m % 4]
                    op("act", lambda e, pb2=pb2, tmt=tmt, m=m: e.activation(out=tmt[:], in_=pb2[:], func=AF.Identity,
                                                                          bias=prm[:, 32 + m:33 + m], scale=1.0), [bpb2, b_prm], [btm])
                    op("dve", lambda e, tmt=tmt, m=m: e.scalar_tensor_tensor(out=xf[:, m, :], in0=xf[:, m, :], scalar=ALPHA, in1=tmt[:],
                                                                           op0=ALU.mult, op1=ALU.add), [b_xf[m], btm], [b_xf[m]])
                    op("pool", lambda e, m=m: e.tensor_copy(out=hg[:, m, :], in_=xf[:, m, :]), [b_xf[m]], [b_hg[m]])
                    op("act", lambda e, m=m: e.activation(out=zsq[:, m, :], in_=xf[:, m, :], func=AF.Square), [b_xf[m]], [b_zsq[m]])
                if n + 1 < len(tiles):
                    p4b_load(n + 1, 0)
                layer_norm(xf, b_xf, hg, b_hg, zsq, b_zsq, 40, 48, prm, b_prm)
                dma("sp", [(SCR[S]["x1"].rearrange("(c p) t -> p c t", p=128)[:, :, sls], xf[:])], b_xf, [], sems4[5])
                if n + 1 < len(tiles):
                    p4b_load(n + 1, 1)
            P.close()
            PW.close()
            if done("p4b"):
                break

            P = Phase(cx, "p5_%d" % l)
            wup = P.sb([128, 8, 5632], BF16, "wup")
            wdn = P.sb([128, 22, 1024], BF16, "wdn")
            ws = P.dsem()
            blk5 = []
            for (c0, n) in ((0, 768), (768, 768), (1536, 640), (2176, 640)):
                blk5.append((c0, c0, n))
                blk5.append((2816 + c0, 2816 + c0, n))
            WUP = WB(P, wup, W["w_ffn_up"][l], blk5)
            b_wdn = WB(P, wdn, W["w_ffn_down"][l], [(0, 0, 1024)]).at(0)
            prm = P.sb([128, 192], F32, "prm5")
            b_prm = Buf(prm)
            pp5 = [(prm[:, 0:44], W["conv_ffn_b"][l].rearrange("(c p) -> p c", p=128)),
                   (prm[:, 176:184], W["ln2_g"][l].rearrange("(c p) -> p c", p=128)),
                   (prm[:, 184:192], W["ln2_b"][l].rearrange("(c p) -> p c", p=128))]
            for k in range(3):
                pp5.append((prm[:, 44 + 44 * k:88 + 44 * k], W["conv_ffn_w"][l, k].rearrange("(c p) -> p c", p=128)))
            dma("sp", pp5, [], [b_prm], ws, allow_slow_non_contiguous=True)
            xf = P.sb([128, 8, T], F32, "xf5")
            b_xf = [Buf(xf[:, c]) for c in range(8)]
            xbe2 = [P.sb([128, 8, 514], BF16, "xbe") for _ in range(2)]
            b_xbe2 = [Buf(t) for t in xbe2]
            actb = P.sb([128, 22, T], BF16, "actb")
            b_actb = [Buf(actb[:, j]) for j in range(22)]
            zsq = actb
            b_zsq = b_actb[0:8]
            ext = [P.sb([128, 514], F32, "ext5") for _ in range(3)]
            b_ext = [Buf(t) for t in ext]
            cv = [P.sb([128, T], F32, "cv5") for _ in range(4)]
            b_cv = [Buf(t) for t in cv]
            lnt = [P.sb([128, T], F32, "lnt5") for _ in range(2)]
            b_lnt = [Buf(t) for t in lnt]
            sems5 = [P.dsem() for _ in range(4)]
            pu = [P.ps([128, 512], F32, "pu") for _ in range(3)]
            b_pu = [Buf(t, True) for t in pu]
            phl = [P.ps([128, 2], F32, "phl") for _ in range(2)]
            b_phl = [Buf(t, True) for t in phl]
            po = [P.ps([128, 512], F32, "po5") for _ in range(2)]
            b_po = [Buf(t, True) for t in po]
            pl = po
            b_pl = b_po

            def p5_load(n, which):
                if n >= len(tiles):
                    return
                S, i = tiles[n]
                t0 = i * T
                if which == 0:
                    xbe = xbe2[n % 2]
                    b_xbe = b_xbe2[n % 2]
                    lo = max(t0 - 1, 0)
                    hi = min(t0 + 513, S)
                    if t0 == 0:
                        op("dve", lambda e: e.memset(xbe[:, :, 0:1], 0.0), [], [b_xbe])
                    if t0 + T == S:
                        op("dve", lambda e: e.memset(xbe[:, :, 513:514], 0.0), [], [b_xbe])
                    dma("pool", [(xbe[:, :, lo - (t0 - 1):hi - (t0 - 1)],
                                  SCR[S]["x1"].rearrange("(k p) t -> p k t", p=128)[:, :, lo:hi])], [], [b_xbe], sems5[n % 2])
                else:
                    dma("sp", [(xf[:], SCR[S]["x1"].rearrange("(k p) t -> p k t", p=128)[:, :, t0:t0 + T])], [], b_xf, sems5[2])

            p5_load(0, 0)
            p5_load(0, 1)
            p5_load(1, 0)
            pui = 0
            phi_ = 0
            eci = 0
            poi = 0
            for n, (S, i) in enumerate(tiles):
                sls = slice(i * T, (i + 1) * T)
                xbe = xbe2[n % 2]
                b_xbe = b_xbe2[n % 2]
                b_zb5 = [b_xbe] * 8
                for j in range(22):
                    cvs = []
                    for which, ch in ((0, j), (1, 22 + j)):
                        pb = pu[pui % 3]; bpb = b_pu[pui % 3]; pui += 1
                        ph = phl[phi_ % 2][:, :]; bph = b_phl[phi_ % 2]; phi_ += 1
                        for k in range(8):
                            op("pe", lambda e, pb=pb, k=k, ch=ch: e.matmul(pb[:], lhsT=wup[:, k, ch * 128:(ch + 1) * 128], rhs=xbe[:, k, 1:513],
                                                                         start=(k == 0), stop=(k == 7)), [WUP.at(ch * 128), b_xbe], [bpb])
                        for k in range(8):
                            op("pe", lambda e, ph=ph, k=k, ch=ch: e.matmul(ph, lhsT=wup[:, k, ch * 128:(ch + 1) * 128], rhs=xbe[:, k, 0:514:513],
                                                                         start=(k == 0), stop=(k == 7)), [WUP.at(ch * 128), b_xbe], [bph])
                        ex = ext[eci % 3]; bex = b_ext[eci % 3]
                        cvt = cv[eci % 4]; bcv = b_cv[eci % 4]
                        eci += 1
                        op("act", lambda e, ex=ex, pb=pb: e.activation(out=ex[:, 1:513], in_=pb[:], func=AF.Copy), [bpb], [bex])
                        op("act", lambda e, ex=ex, ph=ph: e.activation(out=ex[:, 0:514:513], in_=ph, func=AF.Copy), [bph], [bex])
                        op("act", lambda e, pb=pb, cvt=cvt, ch=ch: e.activation(out=cvt[:], in_=pb[:], func=AF.Identity,
                                                                              bias=prm[:, ch:ch + 1], scale=prm[:, 88 + ch:89 + ch]),
                           [bpb, b_prm], [bcv])
                        for k in (0, 2):
                            op("dve", lambda e, ex=ex, cvt=cvt, ch=ch, k=k: e.scalar_tensor_tensor(
                                out=cvt[:], in0=ex[:, k:k + 512], scalar=prm[:, 44 + 44 * k + ch:45 + 44 * k + ch], in1=cvt[:],
                                op0=ALU.mult, op1=ALU.add), [bex, b_prm, bcv], [bcv])
                        cvs.append((cvt, bcv))
                    op("act", lambda e, cvs=cvs: e.activation(out=cvs[0][0][:], in_=cvs[0][0][:], func=AF.Gelu_apprx_tanh),
                       [cvs[0][1]], [cvs[0][1]])
                    op("pool", lambda e, cvs=cvs, j=j: e.tensor_tensor(out=actb[:, j, :], in0=cvs[0][0][:], in1=cvs[1][0][:], op=ALU.mult),
                       [cvs[0][1], cvs[1][1]], [b_actb[j]])
                for m in range(8):
                    pb2 = po[poi % 2]; bpb2 = b_po[poi % 2]; poi += 1
                    for j in range(22):
                        op("pe", lambda e, pb2=pb2, j=j, m=m: e.matmul(pb2[:], lhsT=wdn[:, j, m * 128:(m + 1) * 128], rhs=actb[:, j, :],
                                                                     start=(j == 0), stop=(j == 21)), [b_wdn, b_actb[j]], [bpb2])
                    op("dve", lambda e, pb2=pb2, m=m: e.scalar_tensor_tensor(out=xf[:, m, :], in0=xf[:, m, :], scalar=ALPHA, in1=pb2[:],
                                                                           op0=ALU.mult, op1=ALU.add), [b_xf[m], bpb2], [b_xf[m]])
                for m in range(8):
                    op("pool", lambda e, m=m: e.tensor_copy(out=xbe[:, m, 0:T], in_=xf[:, m, :]), [b_xf[m]], [b_xbe])
                    op("act", lambda e, m=m: e.activation(out=zsq[:, m, :], in_=xf[:, m, :], func=AF.Square), [b_xf[m]], [b_zsq[m]])
                layer_norm(xf, b_xf, xbe, b_zb5, zsq, b_zsq, 176, 184, prm, b_prm)
                dma("sp", [(xout[S].rearrange("(c p) t -> p c t", p=128)[:, :, sls], xf[:])], b_xf, [], sems5[3])
                p5_load(n + 1, 1)
                p5_load(n + 2, 0)
            P.close()

        cx.barrier()
    return nc, hc


def kernel(**inputs):
    n = 8
    xp = np.asarray(inputs["x_prompt"], dtype=np.float32)
    xsm = np.asarray(inputs["x_sample"], dtype=np.float32)
    nc, hc = build_program()
    shared = {k: np.ascontiguousarray(np.asarray(inputs[k], dtype=np.float32)) for k in WNAMES}
    shared.update(hc)
    in_maps = []
    for c in range(n):
        m = dict(shared)
        m["xT2048"] = np.ascontiguousarray(xp[c].T)
        m["xT8192"] = np.ascontiguousarray(xsm[c].T)
        in_maps.append(m)
    res = run_bass_kernel_spmd(nc, in_maps, core_ids=list(range(n)))
    yp = np.stack([np.ascontiguousarray(res.results[c]["yT2048"].T) for c in range(n)], axis=0)
    ys = np.stack([np.ascontiguousarray(res.results[c]["yT8192"].T) for c in range(n)], axis=0)
    return yp.astype(np.float32), ys.astype(np.float32)
```

```python
import math
from contextlib import ExitStack
import numpy as np
import ml_dtypes
import concourse.bass as bass
import concourse.mybir as mybir
from concourse.bass_utils import run_bass_kernel_spmd

F32 = mybir.dt.float32
BF16 = mybir.dt.bfloat16
AF = mybir.ActivationFunctionType
ALU = mybir.AluOpType
AX = mybir.AxisListType

D = 1024
NIN = 6400
DFF = 2816
T = 512
ALPHA = 4 ** 0.25
EPS = 1e-5
NEG = -30000.0


class Sem:
    def __init__(self, h):
        self.h = h
        self.cnt = 0


class Eng:
    def __init__(self, name, eng, sem):
        self.name = name
        self.eng = eng
        self.sem = sem
        self.seen = {}


class Buf:
    def __init__(self, ap, excl=False):
        self.ap = ap
        self.excl = excl
        self.w = {}
        self.r = {}

    def __getitem__(self, k):
        return self.ap[k]


class Ctx:
    def __init__(self, nc, stack):
        self.nc = nc
        self.stack = stack
        self.E = {}
        for name, eng in (("pe", nc.tensor), ("act", nc.scalar), ("dve", nc.vector),
                          ("pool", nc.gpsimd), ("sp", nc.sync)):
            s = Sem(stack.enter_context(nc.semaphore("s_" + name)))
            self.E[name] = Eng(name, eng, s)
        self.dsems = []
        self.free_dsems = []
        self.uid = 0

    def dsem(self):
        if self.free_dsems:
            return self.free_dsems.pop()
        s = Sem(self.stack.enter_context(self.nc.semaphore("d%d" % len(self.dsems))))
        self.dsems.append(s)
        return s

    def release(self, sems):
        self.free_dsems.extend(sems)

    def _deps(self, e, reads, writes):
        need = {}
        for b in reads:
            for s, v in b.w.items():
                if need.get(s, 0) < v:
                    need[s] = v
        for b in writes:
            for s, v in b.w.items():
                if need.get(s, 0) < v:
                    need[s] = v
            for s, v in b.r.items():
                if need.get(s, 0) < v:
                    need[s] = v
        for s, v in need.items():
            if s is e.sem and e.name == "pe":
                continue
            if e.seen.get(s, 0) < v:
                e.eng.wait_ge(s.h, v)
                e.seen[s] = v

    @staticmethod
    def _mark(s, v, reads, writes):
        for b in reads:
            if b.r.get(s, 0) < v:
                b.r[s] = v
        for b in writes:
            b.w = {s: v}
            b.r = {}

    def op(self, ename, fn, reads=(), writes=()):
        e = self.E[ename]
        if any(b.excl for b in reads):
            writes = list(writes) + [b for b in reads if b.excl]
            reads = [b for b in reads if not b.excl]
        self._deps(e, reads, writes)
        ins = fn(e.eng)
        e.sem.cnt += 1
        ins.then_inc(e.sem.h, 1)
        self._mark(e.sem, e.sem.cnt, reads, writes)

    def dma(self, qname, pairs, reads, writes, dsem, **kw):
        e = self.E[qname]
        self._deps(e, reads, writes)
        for (o, i) in pairs:
            e.eng.dma_start(out=o, in_=i, **kw).then_inc(dsem.h, 16)
            dsem.cnt += 16
        self._mark(dsem, dsem.cnt, reads, writes)

    def barrier(self):
        sems = [e.sem for e in self.E.values()] + self.dsems
        for e in self.E.values():
            for s in sems:
                if s.cnt > 0 and e.seen.get(s, 0) < s.cnt:
                    e.eng.wait_ge(s.h, s.cnt)
                    e.seen[s] = s.cnt


class Phase:
    def __init__(self, cx, name):
        self.cx = cx
        self.name = name
        self.stack = ExitStack()
        self.sems = []
        self.n = 0

    def sb(self, shape, dt, name=None):
        self.n += 1
        nm = "%s_%s_%d" % (self.name, name or "t", self.n)
        return self.stack.enter_context(self.cx.nc.sbuf_tensor(nm, list(shape), dt))

    def ps(self, shape, dt=F32, name=None):
        self.n += 1
        nm = "%s_%s_%d" % (self.name, name or "p", self.n)
        return self.stack.enter_context(self.cx.nc.psum_tensor(nm, list(shape), dt))

    def dsem(self):
        s = self.cx.dsem()
        self.sems.append(s)
        return s

    def close(self):
        self.cx.barrier()
        self.cx.release(self.sems)
        self.stack.close()


def t5_bucket_np(rel):
    half, max_exact = 16, 8
    ret = np.where(rel > 0, half, 0)
    n = np.abs(rel)
    nf = np.maximum(n, 1).astype(np.float32)
    large = max_exact + (np.log(nf / max_exact) / np.float32(math.log(128 / max_exact))
                         * (half - max_exact)).astype(np.int32)
    large = np.minimum(large, half - 1)
    return ret + np.where(n < max_exact, n, large)


def host_consts():
    c = {}
    c["c_ident"] = np.eye(128, dtype=np.float32).astype(ml_dtypes.bfloat16)
    k = np.arange(128)
    ang = 2 * np.pi * np.outer(k, k) / 128.0
    c["c_d128"] = np.concatenate([np.cos(ang), -np.sin(ang)], axis=1).astype(ml_dtypes.bfloat16)
    c["c_identf"] = np.eye(128, dtype=np.float32)
    c["c_jrev"] = np.ascontiguousarray(np.eye(128, dtype=np.float32)[::-1])
    c["c_onesm"] = np.full((128, 128), 1.0 / 1024.0, dtype=np.float32).astype(ml_dtypes.bfloat16)
    rel = np.arange(512) - 256
    oh = np.zeros((33, 512), np.float32)
    b = t5_bucket_np(rel)
    for j in range(512):
        if abs(rel[j]) <= 128:
            oh[b[j], j] = 1.0
        else:
            oh[32, j] = 1.0
    c["c_oh"] = oh
    for S in (2048, 8192):
        N2 = S // 128
        s = np.arange(S, dtype=np.float64)[:, None]
        k1 = np.arange(128, dtype=np.float64)[None, :]
        ang = 2 * np.pi * s * k1 / S
        c["c_ec%d" % S] = np.cos(ang).astype(ml_dtypes.bfloat16)
        c["c_es%d" % S] = np.sin(ang).astype(ml_dtypes.bfloat16)
        c["c_en%d" % S] = (-np.sin(ang)).astype(ml_dtypes.bfloat16)
        s2 = np.arange(N2, dtype=np.float64)[:, None]
        k2 = np.arange(N2, dtype=np.float64)[None, :]
        a2 = 2 * np.pi * s2 * k2 / N2
        nrm = 1.0 / math.sqrt(S * 128.0)
        c["c_cs%d" % S] = (np.concatenate([np.cos(a2), np.sin(a2)], axis=0) * nrm).astype(ml_dtypes.bfloat16)
    return c


WNAMES = ["rel_bias", "w_in", "b_in", "attn_sink", "w_att_o", "w_four_o", "conv_rnn_w", "conv_rnn_b",
          "w_rg_r", "b_rg_r", "w_rg_i", "b_rg_i", "rg_lambda", "w_rnn_o", "w_out", "b_out", "ln1_g", "ln1_b",
          "w_ffn_up", "conv_ffn_w", "conv_ffn_b", "w_ffn_down", "ln2_g", "ln2_b"]
WSHAPES = {"rel_bias": [32, 8], "w_in": [2, 1024, 6400], "b_in": [2, 6400], "attn_sink": [2, 8],
           "w_att_o": [2, 512, 1024], "w_four_o": [2, 512, 1024], "conv_rnn_w": [2, 4, 1024],
           "conv_rnn_b": [2, 1024], "w_rg_r": [2, 2, 16, 64, 64], "b_rg_r": [2, 2, 1024],
           "w_rg_i": [2, 2, 16, 64, 64], "b_rg_i": [2, 2, 1024], "rg_lambda": [2, 2, 1024],
           "w_rnn_o": [2, 1024, 1024], "w_out": [2, 1024, 1024], "b_out": [2, 1024], "ln1_g": [2, 1024],
           "ln1_b": [2, 1024], "w_ffn_up": [2, 1024, 5632], "conv_ffn_w": [2, 3, 5632],
           "conv_ffn_b": [2, 5632], "w_ffn_down": [2, 2816, 1024], "ln2_g": [2, 1024], "ln2_b": [2, 1024]}


def build_program(seqs=(2048, 8192), layers=2, stop_after=None, debug=False):
    nc = bass.Bass("TRN2", target_bir_lowering=False)
    W = {n: nc.dram_tensor(n, WSHAPES[n], F32, kind="ExternalInput").ap() for n in WNAMES}
    hc = host_consts()
    C = {}
    for n, a in hc.items():
        if n[-4:] in ("2048", "8192") and int(n[-4:]) not in seqs:
            continue
        C[n] = nc.dram_tensor(n, list(a.shape), BF16 if a.dtype != np.float32 else F32, kind="ExternalInput").ap()
    skind = "ExternalOutput" if debug else "Internal"
    X = {}
    Y = {}
    SCR = {}
    for S in seqs:
        X[S] = nc.dram_tensor("xT%d" % S, [D, S], F32, kind="ExternalInput").ap()
        Y[S] = nc.dram_tensor("yT%d" % S, [D, S], F32, kind="ExternalOutput").ap()
        d = {}
        d["q"] = nc.dram_tensor("s_q%d" % S, [512, S], BF16, kind=skind).ap()
        d["kd"] = nc.dram_tensor("s_kd%d" % S, [256, S], BF16, kind=skind).ap()
        d["v"] = nc.dram_tensor("s_v%d" % S, [S, 128], BF16, kind=skind).ap()
        d["a"] = nc.dram_tensor("s_a%d" % S, [S, 1024], BF16, kind=skind).ap()
        d["rxh"] = nc.dram_tensor("s_rxh%d" % S, [D, S], BF16, kind=skind).ap()
        d["rxl"] = nc.dram_tensor("s_rxl%d" % S, [D, S], BF16, kind=skind).ap()
        d["xr"] = nc.dram_tensor("s_xr%d" % S, [D, S], F32, kind=skind).ap()
        d["hf"] = nc.dram_tensor("s_hf%d" % S, [D, S], F32, kind=skind).ap()
        d["h"] = nc.dram_tensor("s_h%d" % S, [D, S], BF16, kind=skind).ap()
        d["bp"] = nc.dram_tensor("s_bp%d" % S, [128, S // 128, 1024], BF16, kind=skind).ap()
        d["yt"] = nc.dram_tensor("s_yt%d" % S, [512, S], BF16, kind=skind).ap()
        d["at"] = nc.dram_tensor("s_at%d" % S, [512, S], BF16, kind=skind).ap()
        d["x1"] = nc.dram_tensor("s_x1%d" % S, [D, S], F32, kind=skind).ap()
        d["xm"] = nc.dram_tensor("s_xm%d" % S, [D, S], F32, kind=skind).ap()
        SCR[S] = d
    tb_dram = nc.dram_tensor("s_tb", [8, 512], F32, kind=skind)

    with ExitStack() as stack:
        cx = Ctx(nc, stack)
        op = cx.op
        dma = cx.dma

        def done(tag):
            return stop_after is not None and stop_after == tag

        G = Phase(cx, "g")
        ident = G.sb([128, 128], BF16, "ident")
        onesm = G.sb([128, 128], BF16, "onesm")
        b_ident, b_onesm = Buf(ident), Buf(onesm)
        gsem = G.dsem()
        dma("sp", [(ident[:], C["c_ident"]), (onesm[:], C["c_onesm"])], [], [b_ident, b_onesm], gsem)
        with ExitStack() as st0:
            rb = st0.enter_context(nc.sbuf_tensor("g_rb", [33, 8], F32))
            oh = st0.enter_context(nc.sbuf_tensor("g_oh", [33, 512], F32))
            tbs = st0.enter_context(nc.sbuf_tensor("g_tbs", [8, 512], F32))
            tps = st0.enter_context(nc.psum_tensor("g_tps", [8, 512], F32))
            b_rb, b_oh, b_tbs, b_tps = Buf(rb), Buf(oh), Buf(tbs), Buf(tps, True)
            op("dve", lambda e: e.memset(rb[32:33, :], NEG), [], [b_rb])
            dma("sp", [(rb[0:32, :], W["rel_bias"]), (oh[:], C["c_oh"])], [], [b_rb, b_oh], gsem)
            op("pe", lambda e: e.matmul(tps[:], lhsT=rb[:], rhs=oh[:], start=True, stop=True), [b_rb, b_oh], [b_tps])
            op("dve", lambda e: e.tensor_copy(out=tbs[:], in_=tps[:]), [b_tps], [b_tbs])
            dma("sp", [(tb_dram.ap(), tbs[:])], [b_tbs], [], gsem)
            cx.barrier()

        class WB:
            def __init__(self, ph, dst, src2d, blocks, q="pool"):
                self.blocks = []
                self.ph, self.dst, self.src2d, self.q = ph, dst, src2d, q
                self.add(blocks)

            def add(self, blocks):
                K = self.src2d.shape[0] // 128
                v = self.src2d.rearrange("(k p) n -> p k n", p=128)
                dst = self.dst
                for (d0, s0, n) in blocks:
                    b = Buf(dst[:, :, d0:d0 + n])
                    dma(self.q, [(dst[:, k, d0:d0 + n], v[:, k, s0:s0 + n]) for k in range(K)], [], [b], self.ph.dsem())
                    self.blocks.append((d0, d0 + n, b))

            def at(self, col):
                for (a, b_, buf) in self.blocks:
                    if a <= col < b_:
                        return buf
                raise KeyError(col)

        def load_cols(ph, dst, src2d, c0, c1, dsem, bufs, q="pool"):
            K = src2d.shape[0] // 128
            v = src2d.rearrange("(k p) n -> p k n", p=128)
            pairs = [(dst[:, k, :], v[:, k, c0:c1]) for k in range(K)]
            dma(q, pairs, [], bufs, dsem)

        def vec_param(dst, src1d, dsem, buf):
            dma("sp", [(dst, src1d.rearrange("(c p) -> p c", p=128))], [], [buf], dsem,
                allow_slow_non_contiguous=True)

        for l in range(layers):
            last = (l == layers - 1)
            xin = {S: (X[S] if l == 0 else SCR[S]["xm"]) for S in seqs}
            xout = {S: (Y[S] if last else SCR[S]["xm"]) for S in seqs}

            P = Phase(cx, "p1_%d" % l)
            w1 = P.sb([128, 8, 2432], BF16, "w1")
            ws = P.dsem()
            segs = [(0, 0, 512), (512, 512, 64), (576, 512, 64), (640, 576, 64), (704, 576, 64),
                    (768, 640, 128), (896, 768, 512), (1408, 1280, 512), (1920, 1792, 512)]
            W1 = WB(P, w1, W["w_in"][l], segs)
            bia = P.sb([128, 19], F32, "bia")
            b_bia = Buf(bia)
            bin_l = W["b_in"][l]
            bpairs = [(bia[:, 0:4], bin_l[0:512].rearrange("(c p) -> p c", p=128)),
                      (bia[0:64, 4:5], bin_l[512:576].rearrange("(c p) -> p c", p=64)),
                      (bia[64:128, 4:5], bin_l[512:576].rearrange("(c p) -> p c", p=64)),
                      (bia[0:64, 5:6], bin_l[576:640].rearrange("(c p) -> p c", p=64)),
                      (bia[64:128, 5:6], bin_l[576:640].rearrange("(c p) -> p c", p=64)),
                      (bia[:, 6:10], bin_l[768:1280].rearrange("(c p) -> p c", p=128)),
                      (bia[:, 10:18], bin_l[1280:2304].rearrange("(c p) -> p c", p=128))]
            dma("sp", bpairs, [], [b_bia], ws, allow_slow_non_contiguous=True)
            vb = P.sb([128, 128], F32, "vb")
            b_vb = Buf(vb)
            dma("sp", [(vb[:], bin_l[640:768].partition_broadcast(128))], [], [b_vb], ws)
            d128 = P.sb([128, 256], BF16, "d128")
            b_d128 = Buf(d128)
            dma("sp", [(d128[:], C["c_d128"])], [], [b_d128], ws)

            NB = 2
            xb = [P.sb([128, 8, T], BF16, "xb") for _ in range(NB)]
            b_xb = [Buf(t) for t in xb]
            xs = [P.dsem() for _ in range(NB)]
            oq = [P.sb([128, 4, T], BF16, "oq") for _ in range(NB)]
            okd = [P.sb([128, 2, T], BF16, "okd") for _ in range(NB)]
            of = [P.sb([128, 4, T], BF16, "of") for _ in range(NB)]
            orx = [P.sb([128, 8, T], F32, "orx") for _ in range(NB)]
            orh = [P.sb([128, 8, T], BF16, "orh") for _ in range(NB)]
            orl = [P.sb([128, 8, T], BF16, "orl") for _ in range(NB)]
            b_orh = [[Buf(t[:, c]) for c in range(8)] for t in orh]
            b_orl = [[Buf(t[:, c]) for c in range(8)] for t in orl]
            ov = [P.sb([128, 4, 128], BF16, "ov") for _ in range(NB)]
            oa = [P.sb([128, 4, 1024], BF16, "oa") for _ in range(NB)]
            b_oq = [[Buf(t[:, c]) for c in range(4)] for t in oq]
            b_okd = [[Buf(t[:, c]) for c in range(2)] for t in okd]
            b_of = [[Buf(t[:, c]) for c in range(4)] for t in of]
            b_orx = [[Buf(t[:, c]) for c in range(8)] for t in orx]
            b_ov = [[Buf(t[:, c]) for c in range(4)] for t in ov]
            b_oa = [[Buf(t[:, c]) for c in range(4)] for t in oa]
            osem = [[P.dsem() for _ in range(7)] for _ in range(NB)]
            pm = [P.ps([128, 512], F32, "pm") for _ in range(4)]
            b_pm = [Buf(t, True) for t in pm]
            pa = [P.ps([128, 1024], F32, "pa") for _ in range(2)]
            b_pa = [Buf(t, True) for t in pa]

            tiles = [(S, i) for S in seqs for i in range(S // T)]

            def p1_load(n):
                S, i = tiles[n]
                sl = n % NB
                v = xin[S].rearrange("(k p) t -> p k t", p=128)
                dma("pool", [(xb[sl][:, :, :], v[:, :, i * T:(i + 1) * T])], [], [b_xb[sl]], xs[sl])

            p1_load(0)
            pmi = 0
            pai = 0
            for n, (S, i) in enumerate(tiles):
                sl = n % NB
                if n + 1 < len(tiles):
                    p1_load(n + 1)
                t0 = i * T
                sc = SCR[S]
                chunks = []
                for c in range(4):
                    chunks.append((c * 128, c, oq[sl][:, c, :], b_oq[sl][c]))
                for c in range(2):
                    chunks.append((512 + c * 128, 4 + c, okd[sl][:, c, :], b_okd[sl][c]))
                for c in range(4):
                    chunks.append((896 + c * 128, 6 + c, of[sl][:, c, :], b_of[sl][c]))
                for c in range(8):
                    chunks.append((1408 + c * 128, 10 + c, orx[sl][:, c, :], b_orx[sl][c]))
                for (wc, bc, oap, ob) in chunks:
                    pb = pm[pmi % 4]
                    bpb = b_pm[pmi % 4]
                    pmi += 1
                    for k in range(8):
                        op("pe", lambda e, k=k, wc=wc, pb=pb: e.matmul(pb[:], lhsT=w1[:, k, wc:wc + 128],
                                                                     rhs=xb[sl][:, k, :], start=(k == 0), stop=(k == 7)),
                           [W1.at(wc), W1.at(wc + 64), b_xb[sl]], [bpb])
                    op("act", lambda e, pb=pb, oap=oap, bc=bc: e.activation(out=oap, in_=pb[:], func=AF.Identity,
                                                                           bias=bia[:, bc:bc + 1], scale=1.0),
                       [bpb, b_bia], [ob])
                    if bc >= 10:
                        rc = bc - 10
                        op("pool", lambda e, rc=rc: e.tensor_copy(out=orh[sl][:, rc, :], in_=orx[sl][:, rc, :]), [ob], [b_orh[sl][rc]])
                        op("dve", lambda e, rc=rc: e.tensor_tensor(out=orl[sl][:, rc, :], in0=orx[sl][:, rc, :], in1=orh[sl][:, rc, :],
                                                                 op=ALU.subtract), [ob, b_orh[sl][rc]], [b_orl[sl][rc]])
                for j in range(4):
                    pb = pm[pmi % 4]
                    bpb = b_pm[pmi % 4]
                    pmi += 1
                    for k in range(8):
                        op("pe", lambda e, k=k, j=j, pb=pb: e.matmul(pb[:, 0:128], lhsT=xb[sl][:, k, j * 128:(j + 1) * 128],
                                                                   rhs=w1[:, k, 768:896], start=(k == 0), stop=(k == 7)),
                           [W1.at(768), b_xb[sl]], [bpb])
                    op("dve", lambda e, j=j, pb=pb: e.tensor_tensor(out=ov[sl][:, j, :], in0=pb[:, 0:128], in1=vb[:], op=ALU.add),
                       [bpb, b_vb], [b_ov[sl][j]])
                for j in range(4):
                    pb = pa[pai % 2]
                    bpb = b_pa[pai % 2]
                    pai += 1
                    for g in range(4):
                        op("pe", lambda e, j=j, g=g, pb=pb: e.matmul(pb[:, g * 256:(g + 1) * 256],
                                                                   lhsT=of[sl][:, g, j * 128:(j + 1) * 128],
                                                                   rhs=d128[:], start=True, stop=True),
                           [b_of[sl][g], b_d128], [bpb])
                    ov4 = oa[sl][:, j, :].rearrange("p (r g c) -> p g r c", r=2, g=4)
                    iv4 = pb[:].rearrange("p (g r c) -> p g r c", g=4, r=2)
                    op("dve" if j % 2 else "act",
                       (lambda e, ov4=ov4, iv4=iv4: e.tensor_copy(out=ov4, in_=iv4)) if j % 2 else
                       (lambda e, ov4=ov4, iv4=iv4: e.activation(out=ov4, in_=iv4, func=AF.Copy)),
                       [bpb], [b_oa[sl][j]])
                sls = slice(t0, t0 + T)
                dma("sp", [(sc["q"].rearrange("(c p) t -> p c t", p=128)[:, :, sls], oq[sl][:])], b_oq[sl], [], osem[sl][0])
                dma("sp", [(sc["kd"].rearrange("(c p) t -> p c t", p=128)[:, :, sls], okd[sl][:])], b_okd[sl], [], osem[sl][1])
                dma("sp", [(sc["rxh"].rearrange("(c p) t -> p c t", p=128)[:, :, sls], orh[sl][:])], b_orh[sl], [], osem[sl][2])
                dma("sp", [(sc["rxl"].rearrange("(c p) t -> p c t", p=128)[:, :, sls], orl[sl][:])], b_orl[sl], [], osem[sl][5])
                dma("sp", [(sc["v"][sls, :].rearrange("(j p) d -> p j d", p=128), ov[sl][:])], b_ov[sl], [], osem[sl][3])
                dma("sp", [(sc["a"][sls, :].rearrange("(j p) d -> p j d", p=128), oa[sl][:])], b_oa[sl], [], osem[sl][4])
            P.close()
            if done("p1"):
                break

            for dr in (0, 1):
                P = Phase(cx, "p2_%d_%d" % (l, dr))
                wr = P.sb([128, 8, 128], BF16, "wr")
                wi = P.sb([128, 8, 128], BF16, "wi")
                b_wr, b_wi = Buf(wr), Buf(wi)
                ws = P.dsem()
                op("dve", lambda e: e.memset(wr[:], 0.0), [], [b_wr])
                op("dve", lambda e: e.memset(wi[:], 0.0), [], [b_wi])
                vr = W["w_rg_r"][l, dr].rearrange("(c two) i o -> two i c o", two=2)
                vi = W["w_rg_i"][l, dr].rearrange("(c two) i o -> two i c o", two=2)
                dma("pool", [(wr[0:64, :, 0:64], vr[0]), (wr[64:128, :, 64:128], vr[1])], [], [b_wr], ws)
                dma("pool", [(wi[0:64, :, 0:64], vi[0]), (wi[64:128, :, 64:128], vi[1])], [], [b_wi], ws)
                prm = P.sb([128, 12, 8], F32, "prm")
                b_prm = Buf(prm)
                ppairs = [(prm[:, 0, :], W["b_rg_r"][l, dr].rearrange("(c p) -> p c", p=128)),
                          (prm[:, 1, :], W["b_rg_i"][l, dr].rearrange("(c p) -> p c", p=128)),
                          (prm[:, 2, :], W["rg_lambda"][l, dr].rearrange("(c p) -> p c", p=128)),
                          (prm[:, 3, :], W["conv_rnn_b"][l].rearrange("(c p) -> p c", p=128))]
                for k in range(4):
                    ppairs.append((prm[:, 4 + k, :], W["conv_rnn_w"][l, k].rearrange("(c p) -> p c", p=128)))
                dma("sp", ppairs, [], [b_prm], ws, allow_slow_non_contiguous=True)
                op("dve", lambda e: e.tensor_scalar_mul(out=prm[:, 8, :], in0=prm[:, 0, :], scalar1=0.5), [b_prm], [b_prm])
                op("dve", lambda e: e.tensor_scalar_mul(out=prm[:, 9, :], in0=prm[:, 1, :], scalar1=0.5), [b_prm], [b_prm])
                op("act", lambda e: e.activation(out=prm[:, 11, :], in_=prm[:, 2, :], func=AF.Exp, scale=-1.0), [b_prm], [b_prm])
                op("act", lambda e: e.activation(out=prm[:, 11, :], in_=prm[:, 11, :], func=AF.Ln, bias=1.0, scale=1.0), [b_prm], [b_prm])
                op("dve", lambda e: e.tensor_scalar_mul(out=prm[:, 10, :], in0=prm[:, 11, :], scalar1=-4.0), [b_prm], [b_prm])

                if dr == 0:
                    exh = [P.sb([128, 8, 515], BF16, "exh") for _ in range(2)]
                    b_exh = [Buf(t) for t in exh]
                    hsem = [P.dsem() for _ in range(2)]
                    exl = [P.sb([128, 8, 515], BF16, "exl") for _ in range(2)]
                    b_exl = [Buf(t) for t in exl]
                    lsem2 = [P.dsem() for _ in range(2)]
                if dr == 0:
                    NXR = 4
                    xr_t = [P.sb([128, T], F32, "xr") for _ in range(NXR)]
                    b_xr = [Buf(t) for t in xr_t]
                    xrsem = [P.dsem() for _ in range(NXR)]
                    xrb_t = [P.sb([128, T], BF16, "xrb") for _ in range(2)]
                    b_xrb = [Buf(t) for t in xrb_t]
                else:
                    xrF = [P.sb([128, 8, T], F32, "xrF") for _ in range(2)]
                    b_xrF = [Buf(t) for t in xrF]
                    xrB = [P.sb([128, 8, T], BF16, "xrB") for _ in range(2)]
                    b_xrB = [Buf(t) for t in xrB]
                    xfsem = [P.dsem() for _ in range(2)]
                    xbsem = [P.dsem() for _ in range(2)]
                tr_t = [P.sb([128, T], F32, "tr") for _ in range(2)]
                b_tr = [Buf(t) for t in tr_t]
                ti_t = [P.sb([128, T], F32, "ti") for _ in range(2)]
                b_ti = [Buf(t) for t in ti_t]
                A_t = [P.sb([128, 8, T], F32, "A") for _ in range(2)]
                TM_t = [P.sb([128, 8, T], F32, "TM") for _ in range(2)]
                A2_t = [P.sb([128, 8, T], F32, "A2") for _ in range(2)]
                b_A = [[Buf(t[:, c]) for c in range(8)] for t in A_t]
                b_TM = [[Buf(t[:, c]) for c in range(8)] for t in TM_t]
                b_A2 = [[Buf(t[:, c]) for c in range(8)] for t in A2_t]
                hosem = [P.dsem() for _ in range(2)]
                carry = P.sb([128, 8], F32, "carry")
                b_carry = [Buf(carry[:, c:c + 1]) for c in range(8)]
                if dr == 1:
                    hfi = P.sb([128, 8, T], F32, "hfi")
                    b_hfi = [Buf(hfi[:, c]) for c in range(8)]
                    hfsem = P.dsem()
                    hb16 = P.sb([128, 8, T], BF16, "hb16")
                    b_hb16 = [Buf(hb16[:, c]) for c in range(8)]
                    hbsem = P.dsem()
                pz = [P.ps([128, 512], F32, "pz") for _ in range(4)]
                b_pz = [Buf(t, True) for t in pz]
                if dr == 0:
                    pc = [P.ps([128, 512], F32, "pc") for _ in range(2)]
                    b_pc = [Buf(t, True) for t in pc]
                    idf = P.sb([128, 128], F32, "idf")
                    b_idf = Buf(idf)
                    dma("sp", [(idf[:], C["c_identf"])], [], [b_idf], ws)
                    wsp = P.sb([128, 2, 4, 8], F32, "wsp")
                    wb16 = P.sb([128, 4, 8], BF16, "wb16")
                    b_wsp = Buf(wsp)
                    op("dve", lambda e: e.tensor_copy(out=wb16[:], in_=prm[:, 4:8, :]), [b_prm], [b_wsp])
                    op("dve", lambda e: e.tensor_copy(out=wsp[:, 0], in_=wb16[:]), [b_wsp], [b_wsp])
                    op("dve", lambda e: e.tensor_tensor(out=wsp[:, 1], in0=prm[:, 4:8, :], in1=wsp[:, 0], op=ALU.subtract), [b_prm, b_wsp], [b_wsp])
                    dgh = P.sb([128, 8, 4, 128], BF16, "dgh")
                    dgl = P.sb([128, 8, 4, 128], BF16, "dgl")
                    b_dg = Buf(dgh)
                    for c in range(8):
                        for k in range(4):
                            op("dve", lambda e, c=c, k=k: e.tensor_scalar(out=dgh[:, c, k, :], in0=idf[:], scalar1=wsp[:, 0, k, c:c + 1],
                                                                          scalar2=None, op0=ALU.mult), [b_idf, b_wsp], [b_dg])
                            op("dve", lambda e, c=c, k=k: e.tensor_scalar(out=dgl[:, c, k, :], in0=idf[:], scalar1=wsp[:, 1, k, c:c + 1],
                                                                          scalar2=None, op0=ALU.mult), [b_idf, b_wsp], [b_dg])


                tl = []
                for S in seqs:
                    order = list(range(S // T))
                    if dr == 1:
                        order = order[::-1]
                    for pos, i in enumerate(order):
                        tl.append((S, i, pos == 0))
                NTL = len(tl)

                def rng(n):
                    S, i, first = tl[n]
                    t0 = i * T
                    lo = max(t0 - 2, 0)
                    hi = min(t0 + 513, S)
                    return S, t0, lo, hi

                def load_x(n):
                    if n >= NTL:
                        return
                    S, t0, lo, hi = rng(n)
                    sl = n % 2
                    if dr == 1:
                        v = SCR[S]["xr"].rearrange("(c p) t -> p c t", p=128)[:, :, t0:t0 + T]
                        dma("sp", [(xrF[sl][:], v)], [], [b_xrF[sl]], xfsem[sl])
                        dma("pool", [(xrB[sl][:], v)], [], [b_xrB[sl]], xbsem[sl])
                        return
                    for (tl_, bt, nm, sem) in ((exh[sl], b_exh[sl], "rxh", hsem[sl]), (exl[sl], b_exl[sl], "rxl", lsem2[sl])):
                        if t0 == 0:
                            op("dve", lambda e, tl_=tl_: e.memset(tl_[:, :, 0:2], 0.0), [], [bt])
                        if t0 + T == S:
                            op("dve", lambda e, tl_=tl_: e.memset(tl_[:, :, 514:515], 0.0), [], [bt])
                        v = SCR[S][nm].rearrange("(c p) t -> p c t", p=128)
                        dma("sp", [(tl_[:, :, lo - (t0 - 2):hi - (t0 - 2)], v[:, :, lo:hi])], [], [bt], sem)

                pcs = {}
                pzs = {}
                cnt = [0, 0]

                def L1_conv(n, c):
                    sl = n % 2
                    pcb = pc[cnt[0] % 2]; bpc = b_pc[cnt[0] % 2]; cnt[0] += 1
                    pcs[(n, c)] = (pcb, bpc)
                    trip = []
                    for k in range(4):
                        trip += [(dgh, exh[sl], k), (dgh, exl[sl], k), (dgl, exh[sl], k)]
                    for ti_, (dgt, xt_, k) in enumerate(trip):
                        op("pe", lambda e, pcb=pcb, dgt=dgt, xt_=xt_, k=k, ti_=ti_: e.matmul(
                            pcb[:], lhsT=dgt[:, c, k, :], rhs=xt_[:, c, k:k + 512], start=(ti_ == 0), stop=(ti_ == 11)),
                           [b_dg, b_exh[sl], b_exl[sl]], [bpc])

                def L1_evac_b(n, c):
                    s2 = c % 2
                    pcb, bpc = pcs[(n, c)]
                    op("act", lambda e: e.activation(out=xrb_t[s2][:], in_=pcb[:], func=AF.Identity,
                                                   bias=prm[:, 3, c:c + 1], scale=1.0), [bpc, b_prm], [b_xrb[s2]])

                def L1_evac_f(n, c):
                    s4 = c % NXR
                    S, i, first = tl[n]
                    pcb, bpc = pcs.pop((n, c))
                    op("act", lambda e: e.activation(out=xr_t[s4][:], in_=pcb[:], func=AF.Identity,
                                                   bias=prm[:, 3, c:c + 1], scale=1.0), [bpc, b_prm], [b_xr[s4]])
                    dma("sp", [(SCR[S]["xr"][c * 128:(c + 1) * 128, i * T:(i + 1) * T], xr_t[s4][:])], [b_xr[s4]], [], xrsem[s4])

                def L1_gates(n, c):
                    s2 = c % 2
                    pr = pz[cnt[1] % 4]; bpr = b_pz[cnt[1] % 4]; cnt[1] += 1
                    pi = pz[cnt[1] % 4]; bpi = b_pz[cnt[1] % 4]; cnt[1] += 1
                    pzs[(n, c)] = (pr, bpr, pi, bpi)
                    if dr == 0:
                        rhs, brhs = xrb_t[s2][:], b_xrb[s2]
                    else:
                        rhs, brhs = xrB[n % 2][:, c, :], b_xrB[n % 2]
                    op("pe", lambda e: e.matmul(pr[:], lhsT=wr[:, c, :], rhs=rhs, start=True, stop=True), [b_wr, brhs], [bpr])
                    op("pe", lambda e: e.matmul(pi[:], lhsT=wi[:, c, :], rhs=rhs, start=True, stop=True), [b_wi, brhs], [bpi])

                def L1_act(n, c):
                    s2 = c % 2
                    sl = n % 2
                    pr, bpr, pi, bpi = pzs.pop((n, c))
                    op("act", lambda e: e.activation(out=tr_t[s2][:], in_=pr[:], func=AF.Tanh, bias=prm[:, 8, c:c + 1], scale=0.5),
                       [bpr, b_prm], [b_tr[s2]])
                    op("act", lambda e: e.activation(out=ti_t[s2][:], in_=pi[:], func=AF.Tanh, bias=prm[:, 9, c:c + 1], scale=0.5),
                       [bpi, b_prm], [b_ti[s2]])
                    op("act", lambda e: e.activation(out=A_t[sl][:, c, :], in_=tr_t[s2][:], func=AF.Exp,
                                                   bias=prm[:, 10, c:c + 1], scale=prm[:, 10, c:c + 1]), [b_tr[s2], b_prm], [b_A[sl][c]])
                    if dr == 0:
                        xsrc, bxs = xr_t[c % NXR][:], b_xr[c % NXR]
                    else:
                        xsrc, bxs = xrF[sl][:, c, :], b_xrF[sl]
                    op("dve", lambda e: e.scalar_tensor_tensor(out=TM_t[sl][:, c, :], in0=ti_t[s2][:], scalar=1.0, in1=xsrc,
                                                             op0=ALU.add, op1=ALU.mult), [b_ti[s2], bxs], [b_TM[sl][c]])
                    op("pool", lambda e: e.tensor_tensor(out=A2_t[sl][:, c, :], in0=A_t[sl][:, c, :], in1=A_t[sl][:, c, :], op=ALU.mult),
                       [b_A[sl][c]], [b_A2[sl][c]])

                def L2_head(m):
                    sl = m % 2
                    S, i, first = tl[m]
                    if dr == 1:
                        v = SCR[S]["hf"].rearrange("(c p) t -> p c t", p=128)
                        dma("sp", [(hfi[:], v[:, :, i * T:(i + 1) * T])], [], b_hfi, hfsem)
                    for c in range(8):
                        op("act", lambda e, c=c: e.activation(out=A2_t[sl][:, c, :], in_=A2_t[sl][:, c, :], func=AF.Sqrt, bias=1.0, scale=-1.0),
                           [b_A2[sl][c]], [b_A2[sl][c]])

                def L2_chunk(m, c):
                    sl = m % 2
                    S, i, first = tl[m]
                    TMc = TM_t[sl][:, c, :]
                    op("dve", lambda e: e.scalar_tensor_tensor(out=TMc, in0=TMc, scalar=0.5, in1=A2_t[sl][:, c, :],
                                                             op0=ALU.mult, op1=ALU.mult), [b_TM[sl][c], b_A2[sl][c]], [b_TM[sl][c]])
                    rds = [b_A[sl][c], b_TM[sl][c]]
                    if first:
                        init = 0.0
                    else:
                        init = carry[:, c:c + 1]
                        rds = rds + [b_carry[c]]
                    if dr == 0:
                        op("dve", lambda e: e.tensor_tensor_scan(out=TMc, data0=A_t[sl][:, c, :], data1=TMc,
                                                               initial=init, op0=ALU.mult, op1=ALU.add), rds, [b_TM[sl][c]])
                        op("dve", lambda e: e.tensor_copy(out=carry[:, c:c + 1], in_=TM_t[sl][:, c, 511:512]), [b_TM[sl][c]], [b_carry[c]])
                    else:
                        op("dve", lambda e: e.tensor_tensor_scan(out=TM_t[sl][:, c, ::-1], data0=A_t[sl][:, c, ::-1], data1=TM_t[sl][:, c, ::-1],
                                                               initial=init, op0=ALU.mult, op1=ALU.add), rds, [b_TM[sl][c]])
                        op("dve", lambda e: e.tensor_copy(out=carry[:, c:c + 1], in_=TM_t[sl][:, c, 0:1]), [b_TM[sl][c]], [b_carry[c]])
                        op("pool", lambda e: e.tensor_tensor(out=hb16[:, c, :], in0=TMc, in1=hfi[:, c, :], op=ALU.add),
                           [b_TM[sl][c], b_hfi[c]], [b_hb16[c]])

                def L2_tail(m):
                    S, i, first = tl[m]
                    sl = m % 2
                    sls = slice(i * T, (i + 1) * T)
                    if dr == 0:
                        v = SCR[S]["hf"].rearrange("(c p) t -> p c t", p=128)
                        dma("sp", [(v[:, :, sls], TM_t[sl][:])], b_TM[sl], [], hosem[sl])
                    else:
                        v = SCR[S]["h"].rearrange("(c p) t -> p c t", p=128)
                        dma("sp", [(v[:, :, sls], hb16[:])], b_hb16, [], hbsem)

                load_x(0)
                load_x(1)
                for n in range(NTL + 1):
                    live = n < NTL
                    if n >= 1:
                        L2_head(n - 1)
                    if live and dr == 0:
                        L1_conv(n, 0)
                    for c in range(8):
                        if live:
                            if dr == 0:
                                if c + 1 < 8:
                                    L1_conv(n, c + 1)
                                L1_evac_b(n, c)
                            L1_gates(n, c)
                            if c >= 1:
                                L1_act(n, c - 1)
                            if dr == 0:
                                L1_evac_f(n, c)
                        if n >= 1:
                            L2_chunk(n - 1, c)
                    if live:
                        L1_act(n, 7)
                        load_x(n + 2)
                    if n >= 1:
                        L2_tail(n - 1)
                P.close()
            if done("p2"):
                break

            for S in seqs:
                N2 = S // 128
                sc = SCR[S]
                P = Phase(cx, "p3a_%d_%d" % (l, S))
                ec = P.sb([128, N2, 128], BF16, "ec")
                es = P.sb([128, N2, 128], BF16, "es")
                en = P.sb([128, N2, 128], BF16, "en")
                b_tab = Buf(ec)
                ws = P.dsem()
                dma("sp", [(ec[:], C["c_ec%d" % S].rearrange("(a b) k -> a b k", b=N2)),
                           (es[:], C["c_es%d" % S].rearrange("(a b) k -> a b k", b=N2)),
                           (en[:], C["c_en%d" % S].rearrange("(a b) k -> a b k", b=N2))], [], [b_tab], ws)
                GB = 4
                At = [P.sb([128, GB, 1024], BF16, "At") for _ in range(2)]
                b_At = [Buf(t) for t in At]
                asem = [P.dsem() for _ in range(2)]
                Bt = [P.sb([128, GB, 1024], BF16, "Bt") for _ in range(2)]
                b_Bt = [[Buf(t[:, j]) for j in range(GB)] for t in Bt]
                bsem = [P.dsem() for _ in range(2)]
                pp = [P.ps([128, 512], F32, "pp") for _ in range(4)]
                b_pp = [Buf(t, True) for t in pp]
                av = sc["a"].rearrange("(a b) c -> a b c", b=N2)
                ngr = N2 // GB

                def p3_load(gi):
                    sl = gi % 2
                    dma("sp", [(At[sl][:], av[:, gi * GB:(gi + 1) * GB, :])], [], [b_At[sl]], asem[sl])

                p3_load(0)
                ppi = 0
                for gi in range(ngr):
                    sl = gi % 2
                    if gi + 1 < ngr:
                        p3_load(gi + 1)
                    for j in range(GB):
                        s2 = gi * GB + j
                        pre = pp[ppi % 4]; bpre = b_pp[ppi % 4]; ppi += 1
                        pim = pp[ppi % 4]; bpim = b_pp[ppi % 4]; ppi += 1
                        a_re = At[sl][:, j, 0:512]
                        a_im = At[sl][:, j, 512:1024]
                        op("pe", lambda e, pre=pre, s2=s2, a_re=a_re: e.matmul(pre[:], lhsT=ec[:, s2, :], rhs=a_re, start=True, stop=False),
                           [b_tab, b_At[sl]], [bpre])
                        op("pe", lambda e, pre=pre, s2=s2, a_im=a_im: e.matmul(pre[:], lhsT=es[:, s2, :], rhs=a_im, start=False, stop=True),
                           [b_tab, b_At[sl]], [bpre])
                        op("pe", lambda e, pim=pim, s2=s2, a_im=a_im: e.matmul(pim[:], lhsT=ec[:, s2, :], rhs=a_im, start=True, stop=False),
                           [b_tab, b_At[sl]], [bpim])
                        op("pe", lambda e, pim=pim, s2=s2, a_re=a_re: e.matmul(pim[:], lhsT=en[:, s2, :], rhs=a_re, start=False, stop=True),
                           [b_tab, b_At[sl]], [bpim])
                        op("act", lambda e, pre=pre, j=j: e.activation(out=Bt[sl][:, j, 0:512], in_=pre[:], func=AF.Copy),
                           [bpre], [b_Bt[sl][j]])
                        op("dve", lambda e, pim=pim, j=j: e.tensor_copy(out=Bt[sl][:, j, 512:1024], in_=pim[:]),
                           [bpim], [b_Bt[sl][j]])
                    dma("sp", [(sc["bp"][:, gi * GB:(gi + 1) * GB, :], Bt[sl][:])], b_Bt[sl], [], bsem[sl])
                P.close()

                P = Phase(cx, "p3b_%d_%d" % (l, S))
                cs2 = P.sb([2 * N2, N2], BF16, "cs2")
                b_cs2 = Buf(cs2)
                ws = P.dsem()
                dma("sp", [(cs2[:], C["c_cs%d" % S])], [], [b_cs2], ws)
                yts = P.sb([128, 4, S], BF16, "yts")
                b_yts = [Buf(yts[:, g]) for g in range(4)]
                ysem = P.dsem()
                KB = 8
                Bs = [P.sb([2 * N2, KB, 512], BF16, "Bs") for _ in range(2)]
                b_Bs = [Buf(t) for t in Bs]
                ssem = [P.dsem() for _ in range(2)]
                pq = [P.ps([128, 512], F32, "pq") for _ in range(4)]
                b_pq = [Buf(t, True) for t in pq]
                nk = 128 // KB

                def p3b_load(ki):
                    sl = ki % 2
                    src = sc["bp"][ki * KB:(ki + 1) * KB]
                    dma("sp", [(Bs[sl][0:N2, :, :], src[:, :, 0:512].rearrange("k s c -> s k c")),
                               (Bs[sl][N2:2 * N2, :, :], src[:, :, 512:1024].rearrange("k s c -> s k c"))],
                        [], [b_Bs[sl]], ssem[sl])

                p3b_load(0)
                pqi = 0
                for ki in range(nk):
                    sl = ki % 2
                    if ki + 1 < nk:
                        p3b_load(ki + 1)
                    for g in range(4):
                        pb = pq[pqi % 4]; bpb = b_pq[pqi % 4]; pqi += 1
                        for kl in range(KB):
                            op("pe", lambda e, pb=pb, kl=kl, g=g: e.matmul(
                                pb[:, kl * N2:(kl + 1) * N2], lhsT=Bs[sl][:, kl, g * 128:(g + 1) * 128], rhs=cs2[:],
                                start=True, stop=True), [b_Bs[sl], b_cs2], [bpb])
                        oap = yts[:, g, :].rearrange("p (k2 k1) -> p k1 k2", k1=128)[:, ki * KB:(ki + 1) * KB, :]
                        iap = pb[:, 0:KB * N2].rearrange("p (a b) -> p a b", b=N2)
                        if g % 2 == 0:
                            op("act", lambda e, oap=oap, iap=iap: e.activation(out=oap, in_=iap, func=AF.Copy), [bpb], [b_yts[g]])
                        else:
                            op("dve", lambda e, oap=oap, iap=iap: e.tensor_copy(out=oap, in_=iap), [bpb], [b_yts[g]])
                dma("sp", [(sc["yt"].rearrange("(g p) s -> p g s", p=128), yts[:])], b_yts, [], ysem)
                P.close()
            if done("p3"):
                break

            P = Phase(cx, "p4a_%d" % l)
            band = P.sb([128, 8, 384], F32, "band")
            b_band = Buf(band)
            bsm = P.dsem()
            with ExitStack() as st1:
                rr = st1.enter_context(nc.sbuf_tensor("g_rr%d" % l, [128, 8, 384], F32))
                jm = st1.enter_context(nc.sbuf_tensor("g_jm%d" % l, [128, 128], F32))
                bps = st1.enter_context(nc.psum_tensor("g_bps%d" % l, [128, 512], F32))
                b_rr, b_jm, b_bps = Buf(rr), Buf(jm), Buf(bps, True)
                src = bass.AP(tb_dram, 1, [[1, 128], [512, 8], [1, 384]])
                dma("sp", [(rr[:], src), (jm[:], C["c_jrev"])], [], [b_rr, b_jm], bsm)
                rrf = rr[:].rearrange("p h j -> p (h j)")
                bandf = band[:].rearrange("p h j -> p (h j)")
                for cc in range(6):
                    op("pe", lambda e, cc=cc: e.matmul(bps[:], lhsT=jm[:], rhs=rrf[:, cc * 512:(cc + 1) * 512],
                                                       start=True, stop=True), [b_rr, b_jm], [b_bps])
                    op("dve", lambda e, cc=cc: e.tensor_copy(out=bandf[:, cc * 512:(cc + 1) * 512], in_=bps[:]),
                       [b_bps], [b_band])
                cx.barrier()
            skb = P.sb([128, 8], F32, "skb")
            nskb = P.sb([128, 8], F32, "nskb")
            b_skb = Buf(skb)
            ws = P.dsem()
            dma("sp", [(skb[:], W["attn_sink"][l].partition_broadcast(128))], [], [b_skb], ws)
            op("dve", lambda e: e.tensor_scalar_mul(out=nskb[:], in0=skb[:], scalar1=-1.0), [b_skb], [b_skb])
            qt = [P.sb([128, 4, T], BF16, "qt") for _ in range(2)]
            kt = [P.sb([128, 2, 768], BF16, "kt") for _ in range(2)]
            vt = [P.sb([128, 6, 128], BF16, "vt") for _ in range(2)]
            b_qt = [Buf(t) for t in qt]
            b_kt = [Buf(t) for t in kt]
            b_vt = [Buf(t) for t in vt]
            lsem = [[P.dsem() for _ in range(3)] for _ in range(2)]
            aT = [P.sb([128, 4, T], BF16, "aT") for _ in range(2)]
            b_aT = [[Buf(t[:, :, j * 128:(j + 1) * 128]) for j in range(4)] for t in aT]
            atsem = [P.dsem() for _ in range(2)]
            S4 = P.ps([128, 4, 512], F32, "S4")
            b_S4 = Buf(S4, True)
            sb4 = [P.sb([128, 4, 384], F32, "sb4") for _ in range(2)]
            b_sb4 = [Buf(t) for t in sb4]
            P4 = [P.sb([128, 4, 384], BF16, "P4") for _ in range(2)]
            b_P4 = [[Buf(t[:, hh]) for hh in range(4)] for t in P4]
            ptp = [P.ps([128, 2, 3, 128], BF16, "ptp") for _ in range(2)]
            b_ptp = [Buf(t, True) for t in ptp]
            PT = [P.sb([128, 4, 3, 128], BF16, "PT") for _ in range(2)]
            b_PT = [[Buf(t[:, 0:2]), Buf(t[:, 2:4])] for t in PT]
            ops1 = P.ps([128, 512], F32, "ops1")
            b_ops1 = Buf(ops1, True)
            atm = [P.sb([128, 512], BF16, "atm") for _ in range(2)]
            b_atm = [[Buf(t[:, 0:256]), Buf(t[:, 256:512])] for t in atm]
            tpp = P.ps([128, 4, 128], BF16, "tpp")
            b_tpp = Buf(tpp, True)
            stt = [P.sb([128, 24], F32, "stt") for _ in range(4)]
            b_stt = [Buf(t) for t in stt]
            tiles = [(S, i) for S in seqs for i in range(S // T)]

            def p4a_load(n):
                if n >= len(tiles):
                    return
                S, i = tiles[n]
                sl = n % 2
                t0 = i * T
                sc = SCR[S]
                dma("sp", [(qt[sl][:], sc["q"].rearrange("(c p) t -> p c t", p=128)[:, :, t0:t0 + T])], [], [b_qt[sl]], lsem[sl][0])
                lo = max(t0 - 128, 0)
                hi = min(t0 + 640, S)
                dma("sp", [(kt[sl][:, :, lo - (t0 - 128):hi - (t0 - 128)],
                            sc["kd"].rearrange("(c p) t -> p c t", p=128)[:, :, lo:hi])], [], [b_kt[sl]], lsem[sl][1])
                b0 = (lo - (t0 - 128)) // 128
                nb_ = (hi - lo) // 128
                dma("sp", [(vt[sl][:, b0:b0 + nb_, :], sc["v"][lo:hi, :].rearrange("(j p) d -> p j d", p=128))],
                    [], [b_vt[sl]], lsem[sl][2])

            groups = [(n, jb, g) for n in range(len(tiles)) for jb in range(4) for g in range(2)]

            def geom(n, jb):
                S, i = tiles[n]
                t0 = i * T
                gb = t0 // 128 + jb
                kb0 = 1 if gb == 0 else 0
                kb1 = 2 if gb == S // 128 - 1 else 3
                return kb0, kb1 - kb0

            def gparams(gi):
                n, jb, g = groups[gi]
                kb0, nb = geom(n, jb)
                return n, jb, g, n % 2, gi % 2, stt[gi % 4], b_stt[gi % 4], kb0, nb

            def pe_qk(gi):
                n, jb, g, sl, r2, st, bst, kb0, nb = gparams(gi)
                Wd = nb * 128
                kw0 = (jb + kb0) * 128
                for hh in range(4):
                    h = g * 4 + hh
                    c, half = h // 2, h % 2
                    pr = slice(half * 64, half * 64 + 64)
                    op("pe", lambda e, hh=hh, pr=pr, c=c: e.matmul(
                        S4[:, hh, 0:Wd], lhsT=qt[sl][pr, c, jb * 128:(jb + 1) * 128], rhs=kt[sl][pr, g, kw0:kw0 + Wd],
                        start=True, stop=True), [b_qt[sl], b_kt[sl]], [b_S4])

            def dve_softmax_pre(gi):
                n, jb, g, sl, r2, st, bst, kb0, nb = gparams(gi)
                Wd = nb * 128
                bw0 = kb0 * 128
                op("dve", lambda e: e.scalar_tensor_tensor(
                    out=sb4[r2][:, :, 0:Wd], in0=S4[:, :, 0:Wd], scalar=0.125, in1=band[:, g * 4:(g + 1) * 4, bw0:bw0 + Wd],
                    op0=ALU.mult, op1=ALU.add), [b_S4, b_band], [b_sb4[r2]])
                op("dve", lambda e: e.reduce_max(out=st[:, 0:4], in_=sb4[r2][:, :, 0:Wd], axis=AX.X), [b_sb4[r2]], [bst])
                op("dve", lambda e: e.scalar_tensor_tensor(out=st[:, 4:8], in0=st[:, 0:4], scalar=-1.0, in1=nskb[:, g * 4:(g + 1) * 4],
                                                         op0=ALU.mult, op1=ALU.min), [bst, b_skb], [bst])
                op("dve", lambda e: e.tensor_tensor(out=st[:, 12:16], in0=skb[:, g * 4:(g + 1) * 4], in1=st[:, 4:8], op=ALU.add),
                   [bst, b_skb], [bst])

            def act_exp(gi):
                n, jb, g, sl, r2, st, bst, kb0, nb = gparams(gi)
                Wd = nb * 128
                for hh in range(4):
                    op("act", lambda e, hh=hh: e.activation(out=P4[r2][:, hh, 0:Wd], in_=sb4[r2][:, hh, 0:Wd], func=AF.Exp,
                                                          bias=st[:, 4 + hh:5 + hh], scale=1.0, accum_out=st[:, 8 + hh:9 + hh]),
                       [b_sb4[r2], bst], [b_P4[r2][hh], bst])
                op("act", lambda e: e.activation(out=st[:, 12:16], in_=st[:, 12:16], func=AF.Exp), [bst], [bst])

            def pe_T(gi):
                n, jb, g, sl, r2, st, bst, kb0, nb = gparams(gi)
                for hh in range(4):
                    for kb in range(nb):
                        op("pe", lambda e, hh=hh, kb=kb: e.transpose(out=ptp[hh // 2][:, hh % 2, kb, :],
                                                                   in_=P4[r2][:, hh, kb * 128:(kb + 1) * 128], identity=ident[:]),
                           [b_P4[r2][hh], b_ident], [b_ptp[hh // 2]])

            def copies_den(gi):
                n, jb, g, sl, r2, st, bst, kb0, nb = gparams(gi)
                op("act", lambda e: e.activation(out=PT[r2][:, 0:2, 0:nb, :], in_=ptp[0][:, :, 0:nb, :], func=AF.Copy),
                   [b_ptp[0]], [b_PT[r2][0]])
                op("dve", lambda e: e.tensor_copy(out=PT[r2][:, 2:4, 0:nb, :], in_=ptp[1][:, :, 0:nb, :]), [b_ptp[1]], [b_PT[r2][1]])
                op("dve", lambda e: e.tensor_tensor(out=st[:, 16:20], in0=st[:, 8:12], in1=st[:, 12:16], op=ALU.add), [bst], [bst])
                op("dve", lambda e: e.reciprocal(out=st[:, 20:24], in_=st[:, 16:20]), [bst], [bst])

            def pe_pv(gi):
                n, jb, g, sl, r2, st, bst, kb0, nb = gparams(gi)
                for hh in range(4):
                    h = g * 4 + hh
                    for kb in range(nb):
                        op("pe", lambda e, hh=hh, kb=kb, h=h: e.matmul(
                            ops1[:, h * 64:(h + 1) * 64], lhsT=PT[r2][:, hh, kb, :],
                            rhs=vt[sl][:, jb + kb0 + kb, g * 64:(g + 1) * 64], start=(kb == 0), stop=(kb == nb - 1)),
                           [b_PT[r2][hh // 2], b_vt[sl]], [b_ops1])

            def act_norm(gi):
                n, jb, g, sl, r2, st, bst, kb0, nb = gparams(gi)
                osl = (gi // 2) % 2
                for hh in range(4):
                    h = g * 4 + hh
                    op("act", lambda e, hh=hh, h=h: e.activation(out=atm[osl][:, h * 64:(h + 1) * 64], in_=ops1[:, h * 64:(h + 1) * 64],
                                                               func=AF.Identity, scale=st[:, 20 + hh:21 + hh]),
                       [b_ops1, bst], [b_atm[osl][g]])

            def finalize(gi):
                n, jb, g, sl, r2, st, bst, kb0, nb = gparams(gi)
                S, i = tiles[n]
                osl = (gi // 2) % 2
                for c in range(4):
                    op("pe", lambda e, c=c: e.transpose(out=tpp[:, c, :], in_=atm[osl][:, c * 128:(c + 1) * 128], identity=ident[:]),
                       [b_atm[osl][c // 2], b_ident], [b_tpp])
                op("act", lambda e: e.activation(out=aT[sl][:, :, jb * 128:(jb + 1) * 128], in_=tpp[:], func=AF.Copy),
                   [b_tpp], [b_aT[sl][jb]])
                if jb == 3:
                    t0 = i * T
                    dma("sp", [(SCR[S]["at"].rearrange("(c p) t -> p c t", p=128)[:, :, t0:t0 + T], aT[sl][:])],
                        b_aT[sl], [], atsem[sl])

            p4a_load(0)
            p4a_load(1)
            NG = len(groups)
            for sidx in range(NG + 4):
                g2, g1, g0, g3 = sidx - 2, sidx - 1, sidx, sidx - 3
                if 0 <= g2 < NG:
                    copies_den(g2)
                if g0 < NG:
                    pe_qk(g0)
                if 0 <= g2 < NG:
                    pe_pv(g2)
                if g0 < NG:
                    dve_softmax_pre(g0)
                if 0 <= g1 < NG:
                    act_exp(g1)
                    pe_T(g1)
                if 0 <= g2 < NG:
                    act_norm(g2)
                    if groups[g2][1] == 3 and groups[g2][2] == 1:
                        p4a_load(groups[g2][0] + 2)
                if 0 <= g3 < NG and groups[g3][2] == 1:
                    finalize(g3)
            P.close()
            if done("p4a"):
                break

            P = Phase(cx, "p4b_%d" % l)
            w4 = P.sb([128, 8, 4096], BF16, "w4")
            wao = P.sb([128, 4, 1024], BF16, "wao")
            wfo = P.sb([128, 4, 1024], BF16, "wfo")
            wro = P.sb([128, 8, 1024], BF16, "wro")
            wo = P.sb([128, 8, 1024], BF16, "wo")
            ws = P.dsem()
            blk4 = [(0, 2304, 512), (512, 2816, 512)]
            for hlf in range(2):
                for b in range(3):
                    blk4.append((1024 + b * 1024 + hlf * 512, 3328 + b * 1024 + hlf * 512, 512))
            W4 = WB(P, w4, W["w_in"][l], blk4[:5])
            b_wao = WB(P, wao, W["w_att_o"][l], [(0, 0, 1024)]).at(0)
            b_wfo = WB(P, wfo, W["w_four_o"][l], [(0, 0, 1024)]).at(0)
            b_wro = WB(P, wro, W["w_rnn_o"][l], [(0, 0, 1024)]).at(0)
            W4.add(blk4[5:])
            b_wo = WB(P, wo, W["w_out"][l], [(0, 0, 1024)]).at(0)
            prm = P.sb([128, 56], F32, "prm4")
            b_prm = Buf(prm)
            dma("sp", [(prm[:, 0:32], W["b_in"][l][2304:6400].rearrange("(c p) -> p c", p=128)),
                       (prm[:, 32:40], W["b_out"][l].rearrange("(c p) -> p c", p=128)),
                       (prm[:, 40:48], W["ln1_g"][l].rearrange("(c p) -> p c", p=128)),
                       (prm[:, 48:56], W["ln1_b"][l].rearrange("(c p) -> p c", p=128))], [], [b_prm], ws,
                allow_slow_non_contiguous=True)
            xf = P.sb([128, 8, T], F32, "xf")
            b_xf = [Buf(xf[:, c]) for c in range(8)]
            xb = P.sb([128, 8, T], BF16, "xb")
            b_xb = Buf(xb)
            att = P.sb([128, 4, T], BF16, "att")
            b_att = Buf(att)
            ytt = P.sb([128, 4, T], BF16, "ytt")
            b_ytt = Buf(ytt)
            ht = P.sb([128, 8, T], BF16, "ht")
            b_ht = [Buf(ht[:, c]) for c in range(8)]
            hg = P.sb([128, 8, T], BF16, "hg")
            b_hg = [Buf(hg[:, c]) for c in range(8)]
            mixed = P.sb([128, 8, T], BF16, "mixed")
            b_mixed = [Buf(mixed[:, c]) for c in range(8)]
            zsq = P.sb([128, 8, T], BF16, "zsq")
            b_zsq = [Buf(zsq[:, c]) for c in range(8)]
            sems4 = [P.dsem() for _ in range(6)]
            gy = [P.sb([128, T], F32, "gy") for _ in range(2)]
            b_gy = [Buf(t) for t in gy]
            sg = [P.sb([128, T], F32, "sg") for _ in range(4)]
            b_sg = [Buf(t) for t in sg]
            tm = [P.sb([128, T], F32, "tm") for _ in range(4)]
            b_tm = [Buf(t) for t in tm]
            lnt = [P.sb([128, T], F32, "lnt") for _ in range(3)]
            b_lnt = [Buf(t) for t in lnt]
            pg = [P.ps([128, 512], F32, "pg") for _ in range(3)]
            b_pg = [Buf(t, True) for t in pg]
            po = [P.ps([128, 512], F32, "po") for _ in range(3)]
            b_po = [Buf(t, True) for t in po]
            pl = [P.ps([128, 512], F32, "pl") for _ in range(2)]
            b_pl = [Buf(t, True) for t in pl]

            def p4b_load(n, which):
                S, i = tiles[n]
                sls = slice(i * T, (i + 1) * T)
                sc = SCR[S]
                if which == 0:
                    dma("pool", [(xb[:], xin[S].rearrange("(k p) t -> p k t", p=128)[:, :, sls])], [], [b_xb], sems4[0])
                    dma("sp", [(ht[:], sc["h"].rearrange("(c p) t -> p c t", p=128)[:, :, sls])], [], b_ht, sems4[1])
                    dma("sp", [(att[:], sc["at"].rearrange("(c p) t -> p c t", p=128)[:, :, sls])], [], [b_att], sems4[2])
                    dma("sp", [(ytt[:], sc["yt"].rearrange("(c p) t -> p c t", p=128)[:, :, sls])], [], [b_ytt], sems4[3])
                else:
                    dma("sp", [(xf[:], xin[S].rearrange("(k p) t -> p k t", p=128)[:, :, sls])], [], b_xf, sems4[4])

            def layer_norm(xfb, b_xfb, zb_t, b_zb, zsq_t, b_zsq_, gcol, bcol, prm_t, b_prm_t):
                for m in range(8):
                    op("pe", lambda e, m=m: e.matmul(pl[0][:], lhsT=onesm[:], rhs=zb_t[:, m, 0:T], start=(m == 0), stop=(m == 7)),
                       [b_onesm, b_zb[m]], [b_pl[0]])
                for m in range(8):
                    op("pe", lambda e, m=m: e.matmul(pl[1][:], lhsT=onesm[:], rhs=zsq_t[:, m, :], start=(m == 0), stop=(m == 7)),
                       [b_onesm, b_zsq_[m]], [b_pl[1]])
                op("act", lambda e: e.activation(out=lnt[0][:], in_=pl[0][:], func=AF.Copy), [b_pl[0]], [b_lnt[0]])
                op("act", lambda e: e.activation(out=lnt[1][:], in_=pl[0][:], func=AF.Square), [b_pl[0]], [b_lnt[1]])
                op("dve", lambda e: e.tensor_tensor(out=lnt[1][:], in0=pl[1][:], in1=lnt[1][:], op=ALU.subtract), [b_pl[1], b_lnt[1]], [b_lnt[1]])
                op("act", lambda e: e.activation(out=lnt[1][:], in_=lnt[1][:], func=AF.Sqrt, bias=EPS, scale=1.0), [b_lnt[1]], [b_lnt[1]])
                ri = 2 if len(lnt) > 2 else 1
                op("dve", lambda e: e.reciprocal(out=lnt[ri][:], in_=lnt[1][:]), [b_lnt[1]], [b_lnt[ri]])
                for m in range(8):
                    op("dve", lambda e, m=m: e.tensor_tensor(out=xfb[:, m, :], in0=xfb[:, m, :], in1=lnt[0][:], op=ALU.subtract),
                       [b_xfb[m], b_lnt[0]], [b_xfb[m]])
                    op("dve", lambda e, m=m: e.tensor_tensor(out=xfb[:, m, :], in0=xfb[:, m, :], in1=lnt[ri][:], op=ALU.mult),
                       [b_xfb[m], b_lnt[ri]], [b_xfb[m]])
                    op("act", lambda e, m=m: e.activation(out=xfb[:, m, :], in_=xfb[:, m, :], func=AF.Identity,
                                                        bias=prm_t[:, bcol + m:bcol + m + 1], scale=prm_t[:, gcol + m:gcol + m + 1]),
                       [b_xfb[m], b_prm_t], [b_xfb[m]])

            p4b_load(0, 0)
            p4b_load(0, 1)
            pgi = 0
            poi = 0
            for n, (S, i) in enumerate(tiles):
                sls = slice(i * T, (i + 1) * T)
                for c in range(8):
                    pb = pg[pgi % 3]; bpb = b_pg[pgi % 3]; pgi += 1
                    for k in range(8):
                        op("pe", lambda e, pb=pb, k=k, c=c: e.matmul(pb[:], lhsT=w4[:, k, c * 128:(c + 1) * 128], rhs=xb[:, k, :],
                                                                   start=(k == 0), stop=(k == 7)), [W4.at(c * 128), b_xb], [bpb])
                    op("act", lambda e, pb=pb, c=c: e.activation(out=gy[c % 2][:], in_=pb[:], func=AF.Gelu_apprx_tanh,
                                                               bias=prm[:, c:c + 1], scale=1.0), [bpb, b_prm], [b_gy[c % 2]])
                    op("pool", lambda e, c=c: e.tensor_tensor(out=hg[:, c, :], in0=ht[:, c, :], in1=gy[c % 2][:], op=ALU.mult),
                       [b_ht[c], b_gy[c % 2]], [b_hg[c]])
                sgi = 0
                for m in range(8):
                    srcs = [(wao, b_wao, att, [b_att] * 4, 4), (wfo, b_wfo, ytt, [b_ytt] * 4, 4), (wro, b_wro, hg, b_hg, 8)]
                    tms = []
                    for b in range(3):
                        pb = pg[pgi % 3]; bpb = b_pg[pgi % 3]; pgi += 1
                        col = 1024 + b * 1024 + m * 128
                        for k in range(8):
                            op("pe", lambda e, pb=pb, k=k, col=col: e.matmul(pb[:], lhsT=w4[:, k, col:col + 128], rhs=xb[:, k, :],
                                                                           start=(k == 0), stop=(k == 7)), [W4.at(col), b_xb], [bpb])
                        sgt = sg[sgi % 4]; bsg = b_sg[sgi % 4]
                        tmt = tm[sgi % 4]; btm = b_tm[sgi % 4]
                        sgi += 1
                        op("act", lambda e, pb=pb, sgt=sgt, b=b, m=m: e.activation(out=sgt[:], in_=pb[:], func=AF.Sigmoid,
                                                                                 bias=prm[:, 8 + b * 8 + m:9 + b * 8 + m], scale=1.0),
                           [bpb, b_prm], [bsg])
                        wt, bwt, rt, brt, nk_ = srcs[b]
                        pb2 = po[poi % 3]; bpb2 = b_po[poi % 3]; poi += 1
                        for k in range(nk_):
                            op("pe", lambda e, pb2=pb2, k=k, wt=wt, rt=rt, m=m, nk_=nk_: e.matmul(
                                pb2[:], lhsT=wt[:, k, m * 128:(m + 1) * 128], rhs=rt[:, k, :], start=(k == 0), stop=(k == nk_ - 1)),
                               [bwt, brt[k]], [bpb2])
                        op("dve", lambda e, pb2=pb2, sgt=sgt, tmt=tmt: e.tensor_tensor(out=tmt[:], in0=pb2[:], in1=sgt[:], op=ALU.mult),
                           [bpb2, bsg], [btm])
                        tms.append((tmt, btm))
                    op("pool", lambda e, tms=tms: e.tensor_tensor(out=tms[0][0][:], in0=tms[0][0][:], in1=tms[1][0][:], op=ALU.add),
                       [tms[0][1], tms[1][1]], [tms[0][1]])
                    op("pool", lambda e, tms=tms, m=m: e.tensor_tensor(out=mixed[:, m, :], in0=tms[0][0][:], in1=tms[2][0][:], op=ALU.add),
                       [tms[0][1], tms[2][1]], [b_mixed[m]])
                for m in range(8):
                    pb2 = po[poi % 3]; bpb2 = b_po[poi % 3]; poi += 1
                    for k in range(8):
                        op("pe", lambda e, pb2=pb2, k=k, m=m: e.matmul(pb2[:], lhsT=wo[:, k, m * 128:(m + 1) * 128], rhs=mixed[:, k, :],
                                                                     start=(k == 0), stop=(k == 7)), [b_wo, b_mixed[k]], [bpb2])
                    tmt = tm[m % 4]; btm = b_tm[m % 4]
                    op("act", lambda e, pb2=pb2, tmt=tmt, m=m: e.activation(out=tmt[:], in_=pb2[:], func=AF.Identity,
                                                                          bias=prm[:, 32 + m:33 + m], scale=1.0), [bpb2, b_prm], [btm])
                    op("dve", lambda e, tmt=tmt, m=m: e.scalar_tensor_tensor(out=xf[:, m, :], in0=xf[:, m, :], scalar=ALPHA, in1=tmt[:],
                                                                           op0=ALU.mult, op1=ALU.add), [b_xf[m], btm], [b_xf[m]])
                    op("pool", lambda e, m=m: e.tensor_copy(out=hg[:, m, :], in_=xf[:, m, :]), [b_xf[m]], [b_hg[m]])
                    op("act", lambda e, m=m: e.activation(out=zsq[:, m, :], in_=xf[:, m, :], func=AF.Square), [b_xf[m]], [b_zsq[m]])
                if n + 1 < len(tiles):
                    p4b_load(n + 1, 0)
                layer_norm(xf, b_xf, hg, b_hg, zsq, b_zsq, 40, 48, prm, b_prm)
                dma("sp", [(SCR[S]["x1"].rearrange("(c p) t -> p c t", p=128)[:, :, sls], xf[:])], b_xf, [], sems4[5])
                if n + 1 < len(tiles):
                    p4b_load(n + 1, 1)
            P.close()
            if done("p4b"):
                break

            P = Phase(cx, "p5_%d" % l)
            wup = P.sb([128, 8, 5632], BF16, "wup")
            wdn = P.sb([128, 22, 1024], BF16, "wdn")
            ws = P.dsem()
            blk5 = []
            for (c0, n) in ((0, 768), (768, 768), (1536, 640), (2176, 640)):
                blk5.append((c0, c0, n))
                blk5.append((2816 + c0, 2816 + c0, n))
            WUP = WB(P, wup, W["w_ffn_up"][l], blk5)
            b_wdn = WB(P, wdn, W["w_ffn_down"][l], [(0, 0, 1024)]).at(0)
            prm = P.sb([128, 192], F32, "prm5")
            b_prm = Buf(prm)
            pp5 = [(prm[:, 0:44], W["conv_ffn_b"][l].rearrange("(c p) -> p c", p=128)),
                   (prm[:, 176:184], W["ln2_g"][l].rearrange("(c p) -> p c", p=128)),
                   (prm[:, 184:192], W["ln2_b"][l].rearrange("(c p) -> p c", p=128))]
            for k in range(3):
                pp5.append((prm[:, 44 + 44 * k:88 + 44 * k], W["conv_ffn_w"][l, k].rearrange("(c p) -> p c", p=128)))
            dma("sp", pp5, [], [b_prm], ws, allow_slow_non_contiguous=True)
            xf = P.sb([128, 8, T], F32, "xf5")
            b_xf = [Buf(xf[:, c]) for c in range(8)]
            xbe2 = [P.sb([128, 8, T], BF16, "xbe") for _ in range(2)]
            b_xbe2 = [Buf(t) for t in xbe2]
            actb = P.sb([128, 22, T], BF16, "actb")
            b_actb = [Buf(actb[:, j]) for j in range(22)]
            zsq = actb
            b_zsq = b_actb[0:8]
            ext = [P.sb([128, 514], F32, "ext5") for _ in range(3)]
            b_ext = [Buf(t) for t in ext]
            cv = [P.sb([128, T], F32, "cv5") for _ in range(4)]
            b_cv = [Buf(t) for t in cv]
            lnt = [cv[2], cv[3]]
            b_lnt = [b_cv[2], b_cv[3]]
            sems5 = [P.dsem() for _ in range(5)]
            NPU = 6
            pu = [P.ps([128, 512], F32, "pu") for _ in range(NPU)]
            b_pu = [Buf(t, True) for t in pu]
            po = [P.ps([128, 512], F32, "po5") for _ in range(2)]
            b_po = [Buf(t, True) for t in po]
            phl = po
            b_phl = b_po
            pl = po
            b_pl = b_po
            NHMAX = 2 * (max(seqs) // T - 1)
            zcol = P.sb([128, 2], F32, "zcol")
            b_zcol = Buf(zcol)
            op("dve", lambda e: e.memset(zcol[:], 0.0), [], [b_zcol])
            xh = P.sb([128, 8, NHMAX], BF16, "xh")
            b_xh = Buf(xh)
            hh = P.sb([128, 44, NHMAX], F32, "hh")
            b_hh = [Buf(hh[:, ch]) for ch in range(44)]

            def halo_table(S):
                NT = S // T
                NH = 2 * (NT - 1)
                xv = SCR[S]["x1"].rearrange("(k p) t -> p k t", p=128)
                pairs = []
                for k in range(8):
                    src = xv[:, k, 511:511 + 512 * (NT - 1)].rearrange("p (i r) -> p i r", r=512)[:, :, 0:2]
                    dst = xh[:, k, 0:NH].rearrange("p (i r) -> p i r", r=2)
                    for i0 in range(0, NT - 1, 3):
                        i1 = min(i0 + 3, NT - 1)
                        pairs.append((dst[:, i0:i1, :], src[:, i0:i1, :]))
                dma("pool", pairs, [], [b_xh], sems5[4])
                for ch in range(44):
                    ph = phl[ch % 2]; bph = b_phl[ch % 2]
                    for k in range(8):
                        op("pe", lambda e, ph=ph, k=k, ch=ch: e.matmul(ph[:, 0:NH], lhsT=wup[:, k, ch * 128:(ch + 1) * 128], rhs=xh[:, k, 0:NH],
                                                                     start=(k == 0), stop=(k == 7)), [WUP.at(ch * 128), b_xh], [bph])
                    op("act", lambda e, ph=ph, ch=ch: e.activation(out=hh[:, ch, 0:NH], in_=ph[:, 0:NH], func=AF.Copy), [bph], [b_hh[ch]])

            def p5_load(n, which):
                if n >= len(tiles):
                    return
                S, i = tiles[n]
                t0 = i * T
                if which == 0:
                    dma("pool", [(xbe2[n % 2][:], SCR[S]["x1"].rearrange("(k p) t -> p k t", p=128)[:, :, t0:t0 + T])],
                        [], [b_xbe2[n % 2]], sems5[n % 2])
                else:
                    dma("sp", [(xf[:], SCR[S]["x1"].rearrange("(k p) t -> p k t", p=128)[:, :, t0:t0 + T])], [], b_xf, sems5[2])

            p5_load(0, 0)
            p5_load(0, 1)
            p5_load(1, 0)
            pui = 0
            eci = 0
            poi = 0
            curS = None
            for n, (S, i) in enumerate(tiles):
                if S != curS:
                    halo_table(S)
                    curS = S
                NT = S // T
                sls = slice(i * T, (i + 1) * T)
                xbe = xbe2[n % 2]
                b_xbe = b_xbe2[n % 2]
                b_zb5 = [b_xbe] * 8
                for j in range(22):
                    cvs = []
                    for which, ch in ((0, j), (1, 22 + j)):
                        pb = pu[pui % NPU]; bpb = b_pu[pui % NPU]; pui += 1
                        ex = ext[eci % 3]; bex = b_ext[eci % 3]
                        cvt = cv[eci % 4]; bcv = b_cv[eci % 4]
                        eci += 1
                        for k in range(8):
                            op("pe", lambda e, pb=pb, k=k, ch=ch: e.matmul(pb[:], lhsT=wup[:, k, ch * 128:(ch + 1) * 128], rhs=xbe[:, k, :],
                                                                         start=(k == 0), stop=(k == 7)), [WUP.at(ch * 128), b_xbe], [bpb])
                        op("act", lambda e, ex=ex, pb=pb: e.activation(out=ex[:, 1:513], in_=pb[:], func=AF.Copy), [bpb], [bex])
                        if 1 <= i < NT - 1:
                            op("act", lambda e, ex=ex, ch=ch: e.activation(out=ex[:, 0:514:513], in_=hh[:, ch, 2 * i - 2:2 * i + 2:3], func=AF.Copy),
                               [b_hh[ch]], [bex])
                        else:
                            if i >= 1:
                                op("act", lambda e, ex=ex, ch=ch: e.activation(out=ex[:, 0:1], in_=hh[:, ch, 2 * i - 2:2 * i - 1], func=AF.Copy),
                                   [b_hh[ch]], [bex])
                            else:
                                op("act", lambda e, ex=ex: e.activation(out=ex[:, 0:1], in_=zcol[:, 0:1], func=AF.Copy), [b_zcol], [bex])
                            if i < NT - 1:
                                op("act", lambda e, ex=ex, ch=ch: e.activation(out=ex[:, 513:514], in_=hh[:, ch, 2 * i + 1:2 * i + 2], func=AF.Copy),
                                   [b_hh[ch]], [bex])
                            else:
                                op("act", lambda e, ex=ex: e.activation(out=ex[:, 513:514], in_=zcol[:, 0:1], func=AF.Copy), [b_zcol], [bex])
                        op("act", lambda e, pb=pb, cvt=cvt, ch=ch: e.activation(out=cvt[:], in_=pb[:], func=AF.Identity,
                                                                              bias=prm[:, ch:ch + 1], scale=prm[:, 88 + ch:89 + ch]),
                           [bpb, b_prm], [bcv])
                        for k in (0, 2):
                            op("dve", lambda e, ex=ex, cvt=cvt, ch=ch, k=k: e.scalar_tensor_tensor(
                                out=cvt[:], in0=ex[:, k:k + 512], scalar=prm[:, 44 + 44 * k + ch:45 + 44 * k + ch], in1=cvt[:],
                                op0=ALU.mult, op1=ALU.add), [bex, b_prm, bcv], [bcv])
                        cvs.append((cvt, bcv))
                    op("act", lambda e, cvs=cvs: e.activation(out=cvs[0][0][:], in_=cvs[0][0][:], func=AF.Gelu_apprx_tanh),
                       [cvs[0][1]], [cvs[0][1]])
                    op("pool", lambda e, cvs=cvs, j=j: e.tensor_tensor(out=actb[:, j, :], in0=cvs[0][0][:], in1=cvs[1][0][:], op=ALU.mult),
                       [cvs[0][1], cvs[1][1]], [b_actb[j]])
                for m in range(8):
                    pb2 = po[poi % 2]; bpb2 = b_po[poi % 2]; poi += 1
                    for j in range(22):
                        op("pe", lambda e, pb2=pb2, j=j, m=m: e.matmul(pb2[:], lhsT=wdn[:, j, m * 128:(m + 1) * 128], rhs=actb[:, j, :],
                                                                     start=(j == 0), stop=(j == 21)), [b_wdn, b_actb[j]], [bpb2])
                    op("dve", lambda e, pb2=pb2, m=m: e.scalar_tensor_tensor(out=xf[:, m, :], in0=xf[:, m, :], scalar=ALPHA, in1=pb2[:],
                                                                           op0=ALU.mult, op1=ALU.add), [b_xf[m], bpb2], [b_xf[m]])
                for m in range(8):
                    op("pool", lambda e, m=m: e.tensor_copy(out=xbe[:, m, :], in_=xf[:, m, :]), [b_xf[m]], [b_xbe])
                    op("act", lambda e, m=m: e.activation(out=zsq[:, m, :], in_=xf[:, m, :], func=AF.Square), [b_xf[m]], [b_zsq[m]])
                layer_norm(xf, b_xf, xbe, b_zb5, zsq, b_zsq, 176, 184, prm, b_prm)
                dma("sp", [(xout[S].rearrange("(c p) t -> p c t", p=128)[:, :, sls], xf[:])], b_xf, [], sems5[3])
                p5_load(n + 1, 1)
                p5_load(n + 2, 0)
            P.close()

        cx.barrier()
    return nc, hc


def kernel(**inputs):
    n = 8
    xp = np.asarray(inputs["x_prompt"], dtype=np.float32)
    xsm = np.asarray(inputs["x_sample"], dtype=np.float32)
    nc, hc = build_program()
    shared = {k: np.ascontiguousarray(np.asarray(inputs[k], dtype=np.float32)) for k in WNAMES}
    shared.update(hc)
    in_maps = []
    for c in range(n):
        m = dict(shared)
        m["xT2048"] = np.ascontiguousarray(xp[c].T)
        m["xT8192"] = np.ascontiguousarray(xsm[c].T)
        in_maps.append(m)
    res = run_bass_kernel_spmd(nc, in_maps, core_ids=list(range(n)))
    yp = np.stack([np.ascontiguousarray(res.results[c]["yT2048"].T) for c in range(n)], axis=0)
    ys = np.stack([np.ascontiguousarray(res.results[c]["yT8192"].T) for c in range(n)], axis=0)
    return yp.astype(np.float32), ys.astype(np.float32)
```

```python
import math
from contextlib import ExitStack
import numpy as np
import ml_dtypes
import concourse.bass as bass
import concourse.mybir as mybir
from concourse.bass_utils import run_bass_kernel_spmd

F32 = mybir.dt.float32
BF16 = mybir.dt.bfloat16
AF = mybir.ActivationFunctionType
ALU = mybir.AluOpType
AX = mybir.AxisListType

D = 1024
NIN = 6400
DFF = 2816
T = 512
ALPHA = 4 ** 0.25
EPS = 1e-5
NEG = -30000.0


class Sem:
    def __init__(self, h):
        self.h = h
        self.cnt = 0


class Eng:
    def __init__(self, name, eng, sem):
        self.name = name
        self.eng = eng
        self.sem = sem
        self.seen = {}


class Buf:
    def __init__(self, ap, excl=False):
        self.ap = ap
        self.excl = excl
        self.w = {}
        self.r = {}

    def __getitem__(self, k):
        return self.ap[k]


class Ctx:
    def __init__(self, nc, stack):
        self.nc = nc
        self.stack = stack
        self.E = {}
        for name, eng in (("pe", nc.tensor), ("act", nc.scalar), ("dve", nc.vector),
                          ("pool", nc.gpsimd), ("sp", nc.sync)):
            s = Sem(stack.enter_context(nc.semaphore("s_" + name)))
            self.E[name] = Eng(name, eng, s)
        self.dsems = []
        self.free_dsems = []
        self.uid = 0

    def dsem(self):
        if self.free_dsems:
            return self.free_dsems.pop()
        s = Sem(self.stack.enter_context(self.nc.semaphore("d%d" % len(self.dsems))))
        self.dsems.append(s)
        return s

    def release(self, sems):
        self.free_dsems.extend(sems)

    def _deps(self, e, reads, writes):
        need = {}
        for b in reads:
            for s, v in b.w.items():
                if need.get(s, 0) < v:
                    need[s] = v
        for b in writes:
            for s, v in b.w.items():
                if need.get(s, 0) < v:
                    need[s] = v
            for s, v in b.r.items():
                if need.get(s, 0) < v:
                    need[s] = v
        for s, v in need.items():
            if s is e.sem and e.name == "pe":
                continue
            if e.seen.get(s, 0) < v:
                e.eng.wait_ge(s.h, v)
                e.seen[s] = v

    @staticmethod
    def _mark(s, v, reads, writes):
        for b in reads:
            if b.r.get(s, 0) < v:
                b.r[s] = v
        for b in writes:
            b.w = {s: v}
            b.r = {}

    def op(self, ename, fn, reads=(), writes=()):
        e = self.E[ename]
        if any(b.excl for b in reads):
            writes = list(writes) + [b for b in reads if b.excl]
            reads = [b for b in reads if not b.excl]
        self._deps(e, reads, writes)
        ins = fn(e.eng)
        e.sem.cnt += 1
        ins.then_inc(e.sem.h, 1)
        self._mark(e.sem, e.sem.cnt, reads, writes)

    def dma(self, qname, pairs, reads, writes, dsem, **kw):
        e = self.E[qname]
        self._deps(e, reads, writes)
        for (o, i) in pairs:
            e.eng.dma_start(out=o, in_=i, **kw).then_inc(dsem.h, 16)
            dsem.cnt += 16
        self._mark(dsem, dsem.cnt, reads, writes)

    def barrier(self):
        sems = [e.sem for e in self.E.values()] + self.dsems
        for e in self.E.values():
            for s in sems:
                if s.cnt > 0 and e.seen.get(s, 0) < s.cnt:
                    e.eng.wait_ge(s.h, s.cnt)
                    e.seen[s] = s.cnt


class Phase:
    def __init__(self, cx, name):
        self.cx = cx
        self.name = name
        self.stack = ExitStack()
        self.sems = []
        self.n = 0

    def sb(self, shape, dt, name=None):
        self.n += 1
        nm = "%s_%s_%d" % (self.name, name or "t", self.n)
        return self.stack.enter_context(self.cx.nc.sbuf_tensor(nm, list(shape), dt))

    def ps(self, shape, dt=F32, name=None):
        self.n += 1
        nm = "%s_%s_%d" % (self.name, name or "p", self.n)
        return self.stack.enter_context(self.cx.nc.psum_tensor(nm, list(shape), dt))

    def dsem(self):
        s = self.cx.dsem()
        self.sems.append(s)
        return s

    def close(self):
        self.cx.barrier()
        self.cx.release(self.sems)
        self.stack.close()


def t5_bucket_np(rel):
    half, max_exact = 16, 8
    ret = np.where(rel > 0, half, 0)
    n = np.abs(rel)
    nf = np.maximum(n, 1).astype(np.float32)
    large = max_exact + (np.log(nf / max_exact) / np.float32(math.log(128 / max_exact))
                         * (half - max_exact)).astype(np.int32)
    large = np.minimum(large, half - 1)
    return ret + np.where(n < max_exact, n, large)


def host_consts():
    c = {}
    c["c_ident"] = np.eye(128, dtype=np.float32).astype(ml_dtypes.bfloat16)
    k = np.arange(128)
    ang = 2 * np.pi * np.outer(k, k) / 128.0
    c["c_d128"] = np.concatenate([np.cos(ang), -np.sin(ang)], axis=1).astype(ml_dtypes.bfloat16)
    c["c_identf"] = np.eye(128, dtype=np.float32)
    c["c_jrev"] = np.ascontiguousarray(np.eye(128, dtype=np.float32)[::-1])
    c["c_onesm"] = np.full((128, 128), 1.0 / 1024.0, dtype=np.float32).astype(ml_dtypes.bfloat16)
    rel = np.arange(512) - 256
    oh = np.zeros((33, 512), np.float32)
    b = t5_bucket_np(rel)
    for j in range(512):
        if abs(rel[j]) <= 128:
            oh[b[j], j] = 1.0
        else:
            oh[32, j] = 1.0
    c["c_oh"] = oh
    for S in (2048, 8192):
        N2 = S // 128
        s = np.arange(S, dtype=np.float64)[:, None]
        k1 = np.arange(128, dtype=np.float64)[None, :]
        ang = 2 * np.pi * s * k1 / S
        c["c_ec%d" % S] = np.cos(ang).astype(ml_dtypes.bfloat16)
        c["c_es%d" % S] = np.sin(ang).astype(ml_dtypes.bfloat16)
        c["c_en%d" % S] = (-np.sin(ang)).astype(ml_dtypes.bfloat16)
        s2 = np.arange(N2, dtype=np.float64)[:, None]
        k2 = np.arange(N2, dtype=np.float64)[None, :]
        a2 = 2 * np.pi * s2 * k2 / N2
        nrm = 1.0 / math.sqrt(S * 128.0)
        c["c_cs%d" % S] = (np.concatenate([np.cos(a2), np.sin(a2)], axis=0) * nrm).astype(ml_dtypes.bfloat16)
    return c


WNAMES = ["rel_bias", "w_in", "b_in", "attn_sink", "w_att_o", "w_four_o", "conv_rnn_w", "conv_rnn_b",
          "w_rg_r", "b_rg_r", "w_rg_i", "b_rg_i", "rg_lambda", "w_rnn_o", "w_out", "b_out", "ln1_g", "ln1_b",
          "w_ffn_up", "conv_ffn_w", "conv_ffn_b", "w_ffn_down", "ln2_g", "ln2_b"]
WSHAPES = {"rel_bias": [32, 8], "w_in": [2, 1024, 6400], "b_in": [2, 6400], "attn_sink": [2, 8],
           "w_att_o": [2, 512, 1024], "w_four_o": [2, 512, 1024], "conv_rnn_w": [2, 4, 1024],
           "conv_rnn_b": [2, 1024], "w_rg_r": [2, 2, 16, 64, 64], "b_rg_r": [2, 2, 1024],
           "w_rg_i": [2, 2, 16, 64, 64], "b_rg_i": [2, 2, 1024], "rg_lambda": [2, 2, 1024],
           "w_rnn_o": [2, 1024, 1024], "w_out": [2, 1024, 1024], "b_out": [2, 1024], "ln1_g": [2, 1024],
           "ln1_b": [2, 1024], "w_ffn_up": [2, 1024, 5632], "conv_ffn_w": [2, 3, 5632],
           "conv_ffn_b": [2, 5632], "w_ffn_down": [2, 2816, 1024], "ln2_g": [2, 1024], "ln2_b": [2, 1024]}


def build_program(seqs=(2048, 8192), layers=2, stop_after=None, debug=False):
    nc = bass.Bass("TRN2", target_bir_lowering=False)
    W = {n: nc.dram_tensor(n, WSHAPES[n], F32, kind="ExternalInput").ap() for n in WNAMES}
    hc = host_consts()
    C = {}
    for n, a in hc.items():
        if n[-4:] in ("2048", "8192") and int(n[-4:]) not in seqs:
            continue
        C[n] = nc.dram_tensor(n, list(a.shape), BF16 if a.dtype != np.float32 else F32, kind="ExternalInput").ap()
    skind = "ExternalOutput" if debug else "Internal"
    X = {}
    Y = {}
    SCR = {}
    for S in seqs:
        X[S] = nc.dram_tensor("xT%d" % S, [D, S], F32, kind="ExternalInput").ap()
        Y[S] = nc.dram_tensor("yT%d" % S, [D, S], F32, kind="ExternalOutput").ap()
        d = {}
        d["q"] = nc.dram_tensor("s_q%d" % S, [512, S], BF16, kind=skind).ap()
        d["kd"] = nc.dram_tensor("s_kd%d" % S, [256, S], BF16, kind=skind).ap()
        d["v"] = nc.dram_tensor("s_v%d" % S, [S, 128], BF16, kind=skind).ap()
        d["a"] = nc.dram_tensor("s_a%d" % S, [S, 1024], BF16, kind=skind).ap()
        d["rxh"] = nc.dram_tensor("s_rxh%d" % S, [D, S], BF16, kind=skind).ap()
        d["rxl"] = nc.dram_tensor("s_rxl%d" % S, [D, S], BF16, kind=skind).ap()
        d["xr"] = nc.dram_tensor("s_xr%d" % S, [D, S], F32, kind=skind).ap()
        d["hf"] = nc.dram_tensor("s_hf%d" % S, [D, S], F32, kind=skind).ap()
        d["h"] = nc.dram_tensor("s_h%d" % S, [D, S], BF16, kind=skind).ap()
        d["bp"] = nc.dram_tensor("s_bp%d" % S, [128, S // 128, 1024], BF16, kind=skind).ap()
        d["yt"] = nc.dram_tensor("s_yt%d" % S, [512, S], BF16, kind=skind).ap()
        d["at"] = nc.dram_tensor("s_at%d" % S, [512, S], BF16, kind=skind).ap()
        d["x1"] = nc.dram_tensor("s_x1%d" % S, [D, S], F32, kind=skind).ap()
        d["xm"] = nc.dram_tensor("s_xm%d" % S, [D, S], F32, kind=skind).ap()
        SCR[S] = d
    tb_dram = nc.dram_tensor("s_tb", [8, 512], F32, kind=skind)

    with ExitStack() as stack:
        cx = Ctx(nc, stack)
        op = cx.op
        dma = cx.dma

        def done(tag):
            return stop_after is not None and stop_after == tag

        G = Phase(cx, "g")
        ident = G.sb([128, 128], BF16, "ident")
        onesm = G.sb([128, 128], BF16, "onesm")
        b_ident, b_onesm = Buf(ident), Buf(onesm)
        gsem = G.dsem()
        dma("sp", [(ident[:], C["c_ident"]), (onesm[:], C["c_onesm"])], [], [b_ident, b_onesm], gsem)
        with ExitStack() as st0:
            rb = st0.enter_context(nc.sbuf_tensor("g_rb", [33, 8], F32))
            oh = st0.enter_context(nc.sbuf_tensor("g_oh", [33, 512], F32))
            tbs = st0.enter_context(nc.sbuf_tensor("g_tbs", [8, 512], F32))
            tps = st0.enter_context(nc.psum_tensor("g_tps", [8, 512], F32))
            b_rb, b_oh, b_tbs, b_tps = Buf(rb), Buf(oh), Buf(tbs), Buf(tps, True)
            op("dve", lambda e: e.memset(rb[32:33, :], NEG), [], [b_rb])
            dma("sp", [(rb[0:32, :], W["rel_bias"]), (oh[:], C["c_oh"])], [], [b_rb, b_oh], gsem)
            op("pe", lambda e: e.matmul(tps[:], lhsT=rb[:], rhs=oh[:], start=True, stop=True), [b_rb, b_oh], [b_tps])
            op("dve", lambda e: e.tensor_copy(out=tbs[:], in_=tps[:]), [b_tps], [b_tbs])
            dma("sp", [(tb_dram.ap(), tbs[:])], [b_tbs], [], gsem)
            cx.barrier()

        class WB:
            def __init__(self, ph, dst, src2d, blocks, q="pool"):
                self.blocks = []
                self.ph, self.dst, self.src2d, self.q = ph, dst, src2d, q
                self.add(blocks)

            def add(self, blocks):
                K = self.src2d.shape[0] // 128
                v = self.src2d.rearrange("(k p) n -> p k n", p=128)
                dst = self.dst
                for (d0, s0, n) in blocks:
                    b = Buf(dst[:, :, d0:d0 + n])
                    dma(self.q, [(dst[:, k, d0:d0 + n], v[:, k, s0:s0 + n]) for k in range(K)], [], [b], self.ph.dsem())
                    self.blocks.append((d0, d0 + n, b))

            def at(self, col):
                for (a, b_, buf) in self.blocks:
                    if a <= col < b_:
                        return buf
                raise KeyError(col)

        def load_cols(ph, dst, src2d, c0, c1, dsem, bufs, q="pool"):
            K = src2d.shape[0] // 128
            v = src2d.rearrange("(k p) n -> p k n", p=128)
            pairs = [(dst[:, k, :], v[:, k, c0:c1]) for k in range(K)]
            dma(q, pairs, [], bufs, dsem)

        def vec_param(dst, src1d, dsem, buf):
            dma("sp", [(dst, src1d.rearrange("(c p) -> p c", p=128))], [], [buf], dsem,
                allow_slow_non_contiguous=True)

        for l in range(layers):
            last = (l == layers - 1)
            xin = {S: (X[S] if l == 0 else SCR[S]["xm"]) for S in seqs}
            xout = {S: (Y[S] if last else SCR[S]["xm"]) for S in seqs}

            P = Phase(cx, "p1_%d" % l)
            w1 = P.sb([128, 8, 2432], BF16, "w1")
            ws = P.dsem()
            segs = [(0, 0, 512), (512, 512, 64), (576, 512, 64), (640, 576, 64), (704, 576, 64),
                    (768, 640, 128), (896, 768, 512), (1408, 1280, 512), (1920, 1792, 512)]
            W1 = WB(P, w1, W["w_in"][l], segs)
            bia = P.sb([128, 19], F32, "bia")
            b_bia = Buf(bia)
            bin_l = W["b_in"][l]
            bpairs = [(bia[:, 0:4], bin_l[0:512].rearrange("(c p) -> p c", p=128)),
                      (bia[0:64, 4:5], bin_l[512:576].rearrange("(c p) -> p c", p=64)),
                      (bia[64:128, 4:5], bin_l[512:576].rearrange("(c p) -> p c", p=64)),
                      (bia[0:64, 5:6], bin_l[576:640].rearrange("(c p) -> p c", p=64)),
                      (bia[64:128, 5:6], bin_l[576:640].rearrange("(c p) -> p c", p=64)),
                      (bia[:, 6:10], bin_l[768:1280].rearrange("(c p) -> p c", p=128)),
                      (bia[:, 10:18], bin_l[1280:2304].rearrange("(c p) -> p c", p=128))]
            dma("sp", bpairs, [], [b_bia], ws, allow_slow_non_contiguous=True)
            vb = P.sb([128, 128], F32, "vb")
            b_vb = Buf(vb)
            dma("sp", [(vb[:], bin_l[640:768].partition_broadcast(128))], [], [b_vb], ws)
            d128 = P.sb([128, 256], BF16, "d128")
            b_d128 = Buf(d128)
            dma("sp", [(d128[:], C["c_d128"])], [], [b_d128], ws)

            NB = 2
            xb = [P.sb([128, 8, T], BF16, "xb") for _ in range(NB)]
            b_xb = [Buf(t) for t in xb]
            xs = [P.dsem() for _ in range(NB)]
            oq = [P.sb([128, 4, T], BF16, "oq") for _ in range(NB)]
            okd = [P.sb([128, 2, T], BF16, "okd") for _ in range(NB)]
            of = [P.sb([128, 4, T], BF16, "of") for _ in range(NB)]
            orx = [P.sb([128, 8, T], F32, "orx") for _ in range(NB)]
            orh = [P.sb([128, 8, T], BF16, "orh") for _ in range(NB)]
            orl = [P.sb([128, 8, T], BF16, "orl") for _ in range(NB)]
            b_orh = [[Buf(t[:, c]) for c in range(8)] for t in orh]
            b_orl = [[Buf(t[:, c]) for c in range(8)] for t in orl]
            ov = [P.sb([128, 4, 128], BF16, "ov") for _ in range(NB)]
            oa = [P.sb([128, 4, 1024], BF16, "oa") for _ in range(NB)]
            b_oq = [[Buf(t[:, c]) for c in range(4)] for t in oq]
            b_okd = [[Buf(t[:, c]) for c in range(2)] for t in okd]
            b_of = [[Buf(t[:, c]) for c in range(4)] for t in of]
            b_orx = [[Buf(t[:, c]) for c in range(8)] for t in orx]
            b_ov = [[Buf(t[:, c]) for c in range(4)] for t in ov]
            b_oa = [[Buf(t[:, c]) for c in range(4)] for t in oa]
            osem = [[P.dsem() for _ in range(7)] for _ in range(NB)]
            pm = [P.ps([128, 512], F32, "pm") for _ in range(4)]
            b_pm = [Buf(t, True) for t in pm]
            pa = [P.ps([128, 1024], F32, "pa") for _ in range(2)]
            b_pa = [Buf(t, True) for t in pa]

            tiles = [(S, i) for S in seqs for i in range(S // T)]

            def p1_load(n):
                S, i = tiles[n]
                sl = n % NB
                v = xin[S].rearrange("(k p) t -> p k t", p=128)
                dma("pool", [(xb[sl][:, :, :], v[:, :, i * T:(i + 1) * T])], [], [b_xb[sl]], xs[sl])

            p1_load(0)
            pmi = 0
            pai = 0
            for n, (S, i) in enumerate(tiles):
                sl = n % NB
                if n + 1 < len(tiles):
                    p1_load(n + 1)
                t0 = i * T
                sc = SCR[S]
                chunks = []
                for c in range(4):
                    chunks.append((c * 128, c, oq[sl][:, c, :], b_oq[sl][c]))
                for c in range(2):
                    chunks.append((512 + c * 128, 4 + c, okd[sl][:, c, :], b_okd[sl][c]))
                for c in range(4):
                    chunks.append((896 + c * 128, 6 + c, of[sl][:, c, :], b_of[sl][c]))
                for c in range(8):
                    chunks.append((1408 + c * 128, 10 + c, orx[sl][:, c, :], b_orx[sl][c]))
                for (wc, bc, oap, ob) in chunks:
                    pb = pm[pmi % 4]
                    bpb = b_pm[pmi % 4]
                    pmi += 1
                    for k in range(8):
                        op("pe", lambda e, k=k, wc=wc, pb=pb: e.matmul(pb[:], lhsT=w1[:, k, wc:wc + 128],
                                                                     rhs=xb[sl][:, k, :], start=(k == 0), stop=(k == 7)),
                           [W1.at(wc), W1.at(wc + 64), b_xb[sl]], [bpb])
                    op("act", lambda e, pb=pb, oap=oap, bc=bc: e.activation(out=oap, in_=pb[:], func=AF.Identity,
                                                                           bias=bia[:, bc:bc + 1], scale=1.0),
                       [bpb, b_bia], [ob])
                    if bc >= 10:
                        rc = bc - 10
                        op("pool", lambda e, rc=rc: e.tensor_copy(out=orh[sl][:, rc, :], in_=orx[sl][:, rc, :]), [ob], [b_orh[sl][rc]])
                        op("dve", lambda e, rc=rc: e.tensor_tensor(out=orl[sl][:, rc, :], in0=orx[sl][:, rc, :], in1=orh[sl][:, rc, :],
                                                                 op=ALU.subtract), [ob, b_orh[sl][rc]], [b_orl[sl][rc]])
                for j in range(4):
                    pb = pm[pmi % 4]
                    bpb = b_pm[pmi % 4]
                    pmi += 1
                    for k in range(8):
                        op("pe", lambda e, k=k, j=j, pb=pb: e.matmul(pb[:, 0:128], lhsT=xb[sl][:, k, j * 128:(j + 1) * 128],
                                                                   rhs=w1[:, k, 768:896], start=(k == 0), stop=(k == 7)),
                           [W1.at(768), b_xb[sl]], [bpb])
                    op("dve", lambda e, j=j, pb=pb: e.tensor_tensor(out=ov[sl][:, j, :], in0=pb[:, 0:128], in1=vb[:], op=ALU.add),
                       [bpb, b_vb], [b_ov[sl][j]])
                for j in range(4):
                    pb = pa[pai % 2]
                    bpb = b_pa[pai % 2]
                    pai += 1
                    for g in range(4):
                        op("pe", lambda e, j=j, g=g, pb=pb: e.matmul(pb[:, g * 256:(g + 1) * 256],
                                                                   lhsT=of[sl][:, g, j * 128:(j + 1) * 128],
                                                                   rhs=d128[:], start=True, stop=True),
                           [b_of[sl][g], b_d128], [bpb])
                    ov4 = oa[sl][:, j, :].rearrange("p (r g c) -> p g r c", r=2, g=4)
                    iv4 = pb[:].rearrange("p (g r c) -> p g r c", g=4, r=2)
                    op("dve" if j % 2 else "act",
                       (lambda e, ov4=ov4, iv4=iv4: e.tensor_copy(out=ov4, in_=iv4)) if j % 2 else
                       (lambda e, ov4=ov4, iv4=iv4: e.activation(out=ov4, in_=iv4, func=AF.Copy)),
                       [bpb], [b_oa[sl][j]])
                sls = slice(t0, t0 + T)
                dma("sp", [(sc["q"].rearrange("(c p) t -> p c t", p=128)[:, :, sls], oq[sl][:])], b_oq[sl], [], osem[sl][0])
                dma("sp", [(sc["kd"].rearrange("(c p) t -> p c t", p=128)[:, :, sls], okd[sl][:])], b_okd[sl], [], osem[sl][1])
                dma("sp", [(sc["rxh"].rearrange("(c p) t -> p c t", p=128)[:, :, sls], orh[sl][:])], b_orh[sl], [], osem[sl][2])
                dma("sp", [(sc["rxl"].rearrange("(c p) t -> p c t", p=128)[:, :, sls], orl[sl][:])], b_orl[sl], [], osem[sl][5])
                dma("sp", [(sc["v"][sls, :].rearrange("(j p) d -> p j d", p=128), ov[sl][:])], b_ov[sl], [], osem[sl][3])
                dma("sp", [(sc["a"][sls, :].rearrange("(j p) d -> p j d", p=128), oa[sl][:])], b_oa[sl], [], osem[sl][4])
            P.close()
            if done("p1"):
                break

            for dr in (0, 1):
                P = Phase(cx, "p2_%d_%d" % (l, dr))
                wr = P.sb([128, 8, 128], BF16, "wr")
                wi = P.sb([128, 8, 128], BF16, "wi")
                b_wr, b_wi = Buf(wr), Buf(wi)
                ws = P.dsem()
                op("dve", lambda e: e.memset(wr[:], 0.0), [], [b_wr])
                op("dve", lambda e: e.memset(wi[:], 0.0), [], [b_wi])
                vr = W["w_rg_r"][l, dr].rearrange("(c two) i o -> two i c o", two=2)
                vi = W["w_rg_i"][l, dr].rearrange("(c two) i o -> two i c o", two=2)
                dma("pool", [(wr[0:64, :, 0:64], vr[0]), (wr[64:128, :, 64:128], vr[1])], [], [b_wr], ws)
                dma("pool", [(wi[0:64, :, 0:64], vi[0]), (wi[64:128, :, 64:128], vi[1])], [], [b_wi], ws)
                prm = P.sb([128, 12, 8], F32, "prm")
                b_prm = Buf(prm)
                ppairs = [(prm[:, 0, :], W["b_rg_r"][l, dr].rearrange("(c p) -> p c", p=128)),
                          (prm[:, 1, :], W["b_rg_i"][l, dr].rearrange("(c p) -> p c", p=128)),
                          (prm[:, 2, :], W["rg_lambda"][l, dr].rearrange("(c p) -> p c", p=128)),
                          (prm[:, 3, :], W["conv_rnn_b"][l].rearrange("(c p) -> p c", p=128))]
                for k in range(4):
                    ppairs.append((prm[:, 4 + k, :], W["conv_rnn_w"][l, k].rearrange("(c p) -> p c", p=128)))
                dma("sp", ppairs, [], [b_prm], ws, allow_slow_non_contiguous=True)
                op("dve", lambda e: e.tensor_scalar_mul(out=prm[:, 8, :], in0=prm[:, 0, :], scalar1=0.5), [b_prm], [b_prm])
                op("dve", lambda e: e.tensor_scalar_mul(out=prm[:, 9, :], in0=prm[:, 1, :], scalar1=0.5), [b_prm], [b_prm])
                op("act", lambda e: e.activation(out=prm[:, 11, :], in_=prm[:, 2, :], func=AF.Exp, scale=-1.0), [b_prm], [b_prm])
                op("act", lambda e: e.activation(out=prm[:, 11, :], in_=prm[:, 11, :], func=AF.Ln, bias=1.0, scale=1.0), [b_prm], [b_prm])
                op("dve", lambda e: e.tensor_scalar_mul(out=prm[:, 10, :], in0=prm[:, 11, :], scalar1=-4.0), [b_prm], [b_prm])

                if dr == 0:
                    exh = [P.sb([128, 8, 515], BF16, "exh") for _ in range(2)]
                    b_exh = [Buf(t) for t in exh]
                    hsem = [P.dsem() for _ in range(2)]
                    exl = [P.sb([128, 8, 515], BF16, "exl") for _ in range(2)]
                    b_exl = [Buf(t) for t in exl]
                    lsem2 = [P.dsem() for _ in range(2)]
                if dr == 0:
                    NXR = 4
                    xr_t = [P.sb([128, T], F32, "xr") for _ in range(NXR)]
                    b_xr = [Buf(t) for t in xr_t]
                    xrsem = [P.dsem() for _ in range(NXR)]
                    xrb_t = [P.sb([128, T], BF16, "xrb") for _ in range(2)]
                    b_xrb = [Buf(t) for t in xrb_t]
                else:
                    xrF = [P.sb([128, 8, T], F32, "xrF") for _ in range(2)]
                    b_xrF = [Buf(t) for t in xrF]
                    xrB = [P.sb([128, 8, T], BF16, "xrB") for _ in range(2)]
                    b_xrB = [Buf(t) for t in xrB]
                    xfsem = [P.dsem() for _ in range(2)]
                    xbsem = [P.dsem() for _ in range(2)]
                tr_t = [P.sb([128, T], F32, "tr") for _ in range(2)]
                b_tr = [Buf(t) for t in tr_t]
                ti_t = [P.sb([128, T], F32, "ti") for _ in range(2)]
                b_ti = [Buf(t) for t in ti_t]
                A_t = [P.sb([128, 8, T], F32, "A") for _ in range(2)]
                TM_t = [P.sb([128, 8, T], F32, "TM") for _ in range(2)]
                A2_t = [P.sb([128, 8, T], F32, "A2") for _ in range(2)]
                b_A = [[Buf(t[:, c]) for c in range(8)] for t in A_t]
                b_TM = [[Buf(t[:, c]) for c in range(8)] for t in TM_t]
                b_A2 = [[Buf(t[:, c]) for c in range(8)] for t in A2_t]
                hosem = [P.dsem() for _ in range(2)]
                carry = P.sb([128, 8], F32, "carry")
                b_carry = [Buf(carry[:, c:c + 1]) for c in range(8)]
                if dr == 1:
                    hfi = P.sb([128, 8, T], F32, "hfi")
                    b_hfi = [Buf(hfi[:, c]) for c in range(8)]
                    hfsem = P.dsem()
                    hb16 = P.sb([128, 8, T], BF16, "hb16")
                    b_hb16 = [Buf(hb16[:, c]) for c in range(8)]
                    hbsem = P.dsem()
                pz = [P.ps([128, 512], F32, "pz") for _ in range(4)]
                b_pz = [Buf(t, True) for t in pz]
                if dr == 0:
                    pc = [P.ps([128, 512], F32, "pc") for _ in range(2)]
                    b_pc = [Buf(t, True) for t in pc]
                    idf = P.sb([128, 128], F32, "idf")
                    b_idf = Buf(idf)
                    dma("sp", [(idf[:], C["c_identf"])], [], [b_idf], ws)
                    wsp = P.sb([128, 2, 4, 8], F32, "wsp")
                    wb16 = P.sb([128, 4, 8], BF16, "wb16")
                    b_wsp = Buf(wsp)
                    op("dve", lambda e: e.tensor_copy(out=wb16[:], in_=prm[:, 4:8, :]), [b_prm], [b_wsp])
                    op("dve", lambda e: e.tensor_copy(out=wsp[:, 0], in_=wb16[:]), [b_wsp], [b_wsp])
                    op("dve", lambda e: e.tensor_tensor(out=wsp[:, 1], in0=prm[:, 4:8, :], in1=wsp[:, 0], op=ALU.subtract), [b_prm, b_wsp], [b_wsp])
                    dgh = P.sb([128, 8, 4, 128], BF16, "dgh")
                    dgl = P.sb([128, 8, 4, 128], BF16, "dgl")
                    b_dg = Buf(dgh)
                    for c in range(8):
                        for k in range(4):
                            op("dve", lambda e, c=c, k=k: e.tensor_scalar(out=dgh[:, c, k, :], in0=idf[:], scalar1=wsp[:, 0, k, c:c + 1],
                                                                          scalar2=None, op0=ALU.mult), [b_idf, b_wsp], [b_dg])
                            op("dve", lambda e, c=c, k=k: e.tensor_scalar(out=dgl[:, c, k, :], in0=idf[:], scalar1=wsp[:, 1, k, c:c + 1],
                                                                          scalar2=None, op0=ALU.mult), [b_idf, b_wsp], [b_dg])


                tl = []
                for S in seqs:
                    order = list(range(S // T))
                    if dr == 1:
                        order = order[::-1]
                    for pos, i in enumerate(order):
                        tl.append((S, i, pos == 0))
                NTL = len(tl)

                def rng(n):
                    S, i, first = tl[n]
                    t0 = i * T
                    lo = max(t0 - 2, 0)
                    hi = min(t0 + 513, S)
                    return S, t0, lo, hi

                def load_x(n):
                    if n >= NTL:
                        return
                    S, t0, lo, hi = rng(n)
                    sl = n % 2
                    if dr == 1:
                        v = SCR[S]["xr"].rearrange("(c p) t -> p c t", p=128)[:, :, t0:t0 + T]
                        dma("sp", [(xrF[sl][:], v)], [], [b_xrF[sl]], xfsem[sl])
                        dma("pool", [(xrB[sl][:], v)], [], [b_xrB[sl]], xbsem[sl])
                        return
                    for (tl_, bt, nm, sem) in ((exh[sl], b_exh[sl], "rxh", hsem[sl]), (exl[sl], b_exl[sl], "rxl", lsem2[sl])):
                        if t0 == 0:
                            op("dve", lambda e, tl_=tl_: e.memset(tl_[:, :, 0:2], 0.0), [], [bt])
                        if t0 + T == S:
                            op("dve", lambda e, tl_=tl_: e.memset(tl_[:, :, 514:515], 0.0), [], [bt])
                        v = SCR[S][nm].rearrange("(c p) t -> p c t", p=128)
                        dma("sp", [(tl_[:, :, lo - (t0 - 2):hi - (t0 - 2)], v[:, :, lo:hi])], [], [bt], sem)

                pcs = {}
                pzs = {}
                cnt = [0, 0]

                def L1_conv(n, c):
                    sl = n % 2
                    pcb = pc[cnt[0] % 2]; bpc = b_pc[cnt[0] % 2]; cnt[0] += 1
                    pcs[(n, c)] = (pcb, bpc)
                    trip = []
                    for k in range(4):
                        trip += [(dgh, exh[sl], k), (dgh, exl[sl], k), (dgl, exh[sl], k)]
                    for ti_, (dgt, xt_, k) in enumerate(trip):
                        op("pe", lambda e, pcb=pcb, dgt=dgt, xt_=xt_, k=k, ti_=ti_: e.matmul(
                            pcb[:], lhsT=dgt[:, c, k, :], rhs=xt_[:, c, k:k + 512], start=(ti_ == 0), stop=(ti_ == 11)),
                           [b_dg, b_exh[sl], b_exl[sl]], [bpc])

                def L1_evac_b(n, c):
                    s2 = c % 2
                    pcb, bpc = pcs[(n, c)]
                    op("act", lambda e: e.activation(out=xrb_t[s2][:], in_=pcb[:], func=AF.Identity,
                                                   bias=prm[:, 3, c:c + 1], scale=1.0), [bpc, b_prm], [b_xrb[s2]])

                def L1_evac_f(n, c):
                    s4 = c % NXR
                    S, i, first = tl[n]
                    pcb, bpc = pcs.pop((n, c))
                    op("act", lambda e: e.activation(out=xr_t[s4][:], in_=pcb[:], func=AF.Identity,
                                                   bias=prm[:, 3, c:c + 1], scale=1.0), [bpc, b_prm], [b_xr[s4]])
                    dma("sp", [(SCR[S]["xr"][c * 128:(c + 1) * 128, i * T:(i + 1) * T], xr_t[s4][:])], [b_xr[s4]], [], xrsem[s4])

                def L1_gates(n, c):
                    s2 = c % 2
                    pr = pz[cnt[1] % 4]; bpr = b_pz[cnt[1] % 4]; cnt[1] += 1
                    pi = pz[cnt[1] % 4]; bpi = b_pz[cnt[1] % 4]; cnt[1] += 1
                    pzs[(n, c)] = (pr, bpr, pi, bpi)
                    if dr == 0:
                        rhs, brhs = xrb_t[s2][:], b_xrb[s2]
                    else:
                        rhs, brhs = xrB[n % 2][:, c, :], b_xrB[n % 2]
                    op("pe", lambda e: e.matmul(pr[:], lhsT=wr[:, c, :], rhs=rhs, start=True, stop=True), [b_wr, brhs], [bpr])
                    op("pe", lambda e: e.matmul(pi[:], lhsT=wi[:, c, :], rhs=rhs, start=True, stop=True), [b_wi, brhs], [bpi])

                def L1_act(n, c):
                    s2 = c % 2
                    sl = n % 2
                    pr, bpr, pi, bpi = pzs.pop((n, c))
                    op("act", lambda e: e.activation(out=tr_t[s2][:], in_=pr[:], func=AF.Tanh, bias=prm[:, 8, c:c + 1], scale=0.5),
                       [bpr, b_prm], [b_tr[s2]])
                    op("act", lambda e: e.activation(out=ti_t[s2][:], in_=pi[:], func=AF.Tanh, bias=prm[:, 9, c:c + 1], scale=0.5),
                       [bpi, b_prm], [b_ti[s2]])
                    op("act", lambda e: e.activation(out=A_t[sl][:, c, :], in_=tr_t[s2][:], func=AF.Exp,
                                                   bias=prm[:, 10, c:c + 1], scale=prm[:, 10, c:c + 1]), [b_tr[s2], b_prm], [b_A[sl][c]])
                    if dr == 0:
                        xsrc, bxs = xr_t[c % NXR][:], b_xr[c % NXR]
                    else:
                        xsrc, bxs = xrF[sl][:, c, :], b_xrF[sl]
                    op("dve", lambda e: e.scalar_tensor_tensor(out=TM_t[sl][:, c, :], in0=ti_t[s2][:], scalar=1.0, in1=xsrc,
                                                             op0=ALU.add, op1=ALU.mult), [b_ti[s2], bxs], [b_TM[sl][c]])
                    op("pool", lambda e: e.tensor_tensor(out=A2_t[sl][:, c, :], in0=A_t[sl][:, c, :], in1=A_t[sl][:, c, :], op=ALU.mult),
                       [b_A[sl][c]], [b_A2[sl][c]])

                def L2_head(m):
                    sl = m % 2
                    S, i, first = tl[m]
                    if dr == 1:
                        v = SCR[S]["hf"].rearrange("(c p) t -> p c t", p=128)
                        dma("sp", [(hfi[:], v[:, :, i * T:(i + 1) * T])], [], b_hfi, hfsem)
                    for c in range(8):
                        op("act", lambda e, c=c: e.activation(out=A2_t[sl][:, c, :], in_=A2_t[sl][:, c, :], func=AF.Sqrt, bias=1.0, scale=-1.0),
                           [b_A2[sl][c]], [b_A2[sl][c]])

                def L2_chunk(m, c):
                    sl = m % 2
                    S, i, first = tl[m]
                    TMc = TM_t[sl][:, c, :]
                    op("dve", lambda e: e.scalar_tensor_tensor(out=TMc, in0=TMc, scalar=0.5, in1=A2_t[sl][:, c, :],
                                                             op0=ALU.mult, op1=ALU.mult), [b_TM[sl][c], b_A2[sl][c]], [b_TM[sl][c]])
                    rds = [b_A[sl][c], b_TM[sl][c]]
                    if first:
                        init = 0.0
                    else:
                        init = carry[:, c:c + 1]
                        rds = rds + [b_carry[c]]
                    if dr == 0:
                        op("dve", lambda e: e.tensor_tensor_scan(out=TMc, data0=A_t[sl][:, c, :], data1=TMc,
                                                               initial=init, op0=ALU.mult, op1=ALU.add), rds, [b_TM[sl][c]])
                        op("dve", lambda e: e.tensor_copy(out=carry[:, c:c + 1], in_=TM_t[sl][:, c, 511:512]), [b_TM[sl][c]], [b_carry[c]])
                    else:
                        op("dve", lambda e: e.tensor_tensor_scan(out=TM_t[sl][:, c, ::-1], data0=A_t[sl][:, c, ::-1], data1=TM_t[sl][:, c, ::-1],
                                                               initial=init, op0=ALU.mult, op1=ALU.add), rds, [b_TM[sl][c]])
                        op("dve", lambda e: e.tensor_copy(out=carry[:, c:c + 1], in_=TM_t[sl][:, c, 0:1]), [b_TM[sl][c]], [b_carry[c]])
                        op("pool", lambda e: e.tensor_tensor(out=hb16[:, c, :], in0=TMc, in1=hfi[:, c, :], op=ALU.add),
                           [b_TM[sl][c], b_hfi[c]], [b_hb16[c]])

                def L2_tail(m):
                    S, i, first = tl[m]
                    sl = m % 2
                    sls = slice(i * T, (i + 1) * T)
                    if dr == 0:
                        v = SCR[S]["hf"].rearrange("(c p) t -> p c t", p=128)
                        dma("sp", [(v[:, :, sls], TM_t[sl][:])], b_TM[sl], [], hosem[sl])
                    else:
                        v = SCR[S]["h"].rearrange("(c p) t -> p c t", p=128)
                        dma("sp", [(v[:, :, sls], hb16[:])], b_hb16, [], hbsem)

                load_x(0)
                load_x(1)
                for n in range(NTL + 1):
                    live = n < NTL
                    if n >= 1:
                        L2_head(n - 1)
                    if live and dr == 0:
                        L1_conv(n, 0)
                    for c in range(8):
                        if live:
                            if dr == 0:
                                if c + 1 < 8:
                                    L1_conv(n, c + 1)
                                L1_evac_b(n, c)
                            L1_gates(n, c)
                            if c >= 1:
                                L1_act(n, c - 1)
                            if dr == 0:
                                L1_evac_f(n, c)
                        if n >= 1:
                            L2_chunk(n - 1, c)
                    if live:
                        L1_act(n, 7)
                        load_x(n + 2)
                    if n >= 1:
                        L2_tail(n - 1)
                P.close()
            if done("p2"):
                break

            for S in seqs:
                N2 = S // 128
                sc = SCR[S]
                P = Phase(cx, "p3a_%d_%d" % (l, S))
                ec = P.sb([128, N2, 128], BF16, "ec")
                es = P.sb([128, N2, 128], BF16, "es")
                en = P.sb([128, N2, 128], BF16, "en")
                b_tab = Buf(ec)
                ws = P.dsem()
                dma("sp", [(ec[:], C["c_ec%d" % S].rearrange("(a b) k -> a b k", b=N2)),
                           (es[:], C["c_es%d" % S].rearrange("(a b) k -> a b k", b=N2)),
                           (en[:], C["c_en%d" % S].rearrange("(a b) k -> a b k", b=N2))], [], [b_tab], ws)
                GB = 4
                At = [P.sb([128, GB, 1024], BF16, "At") for _ in range(2)]
                b_At = [Buf(t) for t in At]
                asem = [P.dsem() for _ in range(2)]
                Bt = [P.sb([128, GB, 1024], BF16, "Bt") for _ in range(2)]
                b_Bt = [[Buf(t[:, j]) for j in range(GB)] for t in Bt]
                bsem = [P.dsem() for _ in range(2)]
                pp = [P.ps([128, 512], F32, "pp") for _ in range(4)]
                b_pp = [Buf(t, True) for t in pp]
                av = sc["a"].rearrange("(a b) c -> a b c", b=N2)
                ngr = N2 // GB

                def p3_load(gi):
                    sl = gi % 2
                    dma("sp", [(At[sl][:], av[:, gi * GB:(gi + 1) * GB, :])], [], [b_At[sl]], asem[sl])

                p3_load(0)
                ppi = 0
                for gi in range(ngr):
                    sl = gi % 2
                    if gi + 1 < ngr:
                        p3_load(gi + 1)
                    for j in range(GB):
                        s2 = gi * GB + j
                        pre = pp[ppi % 4]; bpre = b_pp[ppi % 4]; ppi += 1
                        pim = pp[ppi % 4]; bpim = b_pp[ppi % 4]; ppi += 1
                        a_re = At[sl][:, j, 0:512]
                        a_im = At[sl][:, j, 512:1024]
                        op("pe", lambda e, pre=pre, s2=s2, a_re=a_re: e.matmul(pre[:], lhsT=ec[:, s2, :], rhs=a_re, start=True, stop=False),
                           [b_tab, b_At[sl]], [bpre])
                        op("pe", lambda e, pre=pre, s2=s2, a_im=a_im: e.matmul(pre[:], lhsT=es[:, s2, :], rhs=a_im, start=False, stop=True),
                           [b_tab, b_At[sl]], [bpre])
                        op("pe", lambda e, pim=pim, s2=s2, a_im=a_im: e.matmul(pim[:], lhsT=ec[:, s2, :], rhs=a_im, start=True, stop=False),
                           [b_tab, b_At[sl]], [bpim])
                        op("pe", lambda e, pim=pim, s2=s2, a_re=a_re: e.matmul(pim[:], lhsT=en[:, s2, :], rhs=a_re, start=False, stop=True),
                           [b_tab, b_At[sl]], [bpim])
                        op("act", lambda e, pre=pre, j=j: e.activation(out=Bt[sl][:, j, 0:512], in_=pre[:], func=AF.Copy),
                           [bpre], [b_Bt[sl][j]])
                        op("dve", lambda e, pim=pim, j=j: e.tensor_copy(out=Bt[sl][:, j, 512:1024], in_=pim[:]),
                           [bpim], [b_Bt[sl][j]])
                    dma("sp", [(sc["bp"][:, gi * GB:(gi + 1) * GB, :], Bt[sl][:])], b_Bt[sl], [], bsem[sl])
                P.close()

                P = Phase(cx, "p3b_%d_%d" % (l, S))
                cs2 = P.sb([2 * N2, N2], BF16, "cs2")
                b_cs2 = Buf(cs2)
                ws = P.dsem()
                dma("sp", [(cs2[:], C["c_cs%d" % S])], [], [b_cs2], ws)
                yts = P.sb([128, 4, S], BF16, "yts")
                b_yts = [Buf(yts[:, g]) for g in range(4)]
                ysem = P.dsem()
                KB = 8
                Bs = [P.sb([2 * N2, KB, 512], BF16, "Bs") for _ in range(2)]
                b_Bs = [Buf(t) for t in Bs]
                ssem = [P.dsem() for _ in range(2)]
                pq = [P.ps([128, 512], F32, "pq") for _ in range(4)]
                b_pq = [Buf(t, True) for t in pq]
                nk = 128 // KB

                def p3b_load(ki):
                    sl = ki % 2
                    src = sc["bp"][ki * KB:(ki + 1) * KB]
                    dma("sp", [(Bs[sl][0:N2, :, :], src[:, :, 0:512].rearrange("k s c -> s k c")),
                               (Bs[sl][N2:2 * N2, :, :], src[:, :, 512:1024].rearrange("k s c -> s k c"))],
                        [], [b_Bs[sl]], ssem[sl])

                p3b_load(0)
                pqi = 0
                for ki in range(nk):
                    sl = ki % 2
                    if ki + 1 < nk:
                        p3b_load(ki + 1)
                    for g in range(4):
                        pb = pq[pqi % 4]; bpb = b_pq[pqi % 4]; pqi += 1
                        for kl in range(KB):
                            op("pe", lambda e, pb=pb, kl=kl, g=g: e.matmul(
                                pb[:, kl * N2:(kl + 1) * N2], lhsT=Bs[sl][:, kl, g * 128:(g + 1) * 128], rhs=cs2[:],
                                start=True, stop=True), [b_Bs[sl], b_cs2], [bpb])
                        oap = yts[:, g, :].rearrange("p (k2 k1) -> p k1 k2", k1=128)[:, ki * KB:(ki + 1) * KB, :]
                        iap = pb[:, 0:KB * N2].rearrange("p (a b) -> p a b", b=N2)
                        if g % 2 == 0:
                            op("act", lambda e, oap=oap, iap=iap: e.activation(out=oap, in_=iap, func=AF.Copy), [bpb], [b_yts[g]])
                        else:
                            op("dve", lambda e, oap=oap, iap=iap: e.tensor_copy(out=oap, in_=iap), [bpb], [b_yts[g]])
                dma("sp", [(sc["yt"].rearrange("(g p) s -> p g s", p=128), yts[:])], b_yts, [], ysem)
                P.close()
            if done("p3"):
                break

            P = Phase(cx, "p4a_%d" % l)
            band = P.sb([128, 8, 384], F32, "band")
            b_band = Buf(band)
            bsm = P.dsem()
            with ExitStack() as st1:
                rr = st1.enter_context(nc.sbuf_tensor("g_rr%d" % l, [128, 8, 384], F32))
                jm = st1.enter_context(nc.sbuf_tensor("g_jm%d" % l, [128, 128], F32))
                bps = st1.enter_context(nc.psum_tensor("g_bps%d" % l, [128, 512], F32))
                b_rr, b_jm, b_bps = Buf(rr), Buf(jm), Buf(bps, True)
                src = bass.AP(tb_dram, 1, [[1, 128], [512, 8], [1, 384]])
                dma("sp", [(rr[:], src), (jm[:], C["c_jrev"])], [], [b_rr, b_jm], bsm)
                rrf = rr[:].rearrange("p h j -> p (h j)")
                bandf = band[:].rearrange("p h j -> p (h j)")
                for cc in range(6):
                    op("pe", lambda e, cc=cc: e.matmul(bps[:], lhsT=jm[:], rhs=rrf[:, cc * 512:(cc + 1) * 512],
                                                       start=True, stop=True), [b_rr, b_jm], [b_bps])
                    op("dve", lambda e, cc=cc: e.tensor_copy(out=bandf[:, cc * 512:(cc + 1) * 512], in_=bps[:]),
                       [b_bps], [b_band])
                cx.barrier()
            skb = P.sb([128, 8], F32, "skb")
            nskb = P.sb([128, 8], F32, "nskb")
            b_skb = Buf(skb)
            ws = P.dsem()
            dma("sp", [(skb[:], W["attn_sink"][l].partition_broadcast(128))], [], [b_skb], ws)
            op("dve", lambda e: e.tensor_scalar_mul(out=nskb[:], in0=skb[:], scalar1=-1.0), [b_skb], [b_skb])
            qt = [P.sb([128, 4, T], BF16, "qt") for _ in range(2)]
            kt = [P.sb([128, 2, 768], BF16, "kt") for _ in range(2)]
            vt = [P.sb([128, 6, 128], BF16, "vt") for _ in range(2)]
            b_qt = [Buf(t) for t in qt]
            b_kt = [Buf(t) for t in kt]
            b_vt = [Buf(t) for t in vt]
            lsem = [[P.dsem() for _ in range(3)] for _ in range(2)]
            aT = [P.sb([128, 4, T], BF16, "aT") for _ in range(2)]
            b_aT = [[Buf(t[:, :, j * 128:(j + 1) * 128]) for j in range(4)] for t in aT]
            atsem = [P.dsem() for _ in range(2)]
            S4 = P.ps([128, 4, 512], F32, "S4")
            b_S4 = Buf(S4, True)
            sb4 = [P.sb([128, 4, 384], F32, "sb4") for _ in range(2)]
            b_sb4 = [Buf(t) for t in sb4]
            P4 = [P.sb([128, 4, 384], BF16, "P4") for _ in range(2)]
            b_P4 = [[Buf(t[:, hh]) for hh in range(4)] for t in P4]
            ptp = [P.ps([128, 2, 3, 128], BF16, "ptp") for _ in range(2)]
            b_ptp = [Buf(t, True) for t in ptp]
            PT = [P.sb([128, 4, 3, 128], BF16, "PT") for _ in range(2)]
            b_PT = [[Buf(t[:, 0:2]), Buf(t[:, 2:4])] for t in PT]
            ops1 = P.ps([128, 512], F32, "ops1")
            b_ops1 = Buf(ops1, True)
            atm = [P.sb([128, 512], BF16, "atm") for _ in range(2)]
            b_atm = [[Buf(t[:, 0:256]), Buf(t[:, 256:512])] for t in atm]
            tpp = P.ps([128, 4, 128], BF16, "tpp")
            b_tpp = Buf(tpp, True)
            stt = [P.sb([128, 24], F32, "stt") for _ in range(4)]
            b_stt = [Buf(t) for t in stt]
            tiles = [(S, i) for S in seqs for i in range(S // T)]

            def p4a_load(n):
                if n >= len(tiles):
                    return
                S, i = tiles[n]
                sl = n % 2
                t0 = i * T
                sc = SCR[S]
                dma("sp", [(qt[sl][:], sc["q"].rearrange("(c p) t -> p c t", p=128)[:, :, t0:t0 + T])], [], [b_qt[sl]], lsem[sl][0])
                lo = max(t0 - 128, 0)
                hi = min(t0 + 640, S)
                dma("sp", [(kt[sl][:, :, lo - (t0 - 128):hi - (t0 - 128)],
                            sc["kd"].rearrange("(c p) t -> p c t", p=128)[:, :, lo:hi])], [], [b_kt[sl]], lsem[sl][1])
                b0 = (lo - (t0 - 128)) // 128
                nb_ = (hi - lo) // 128
                dma("sp", [(vt[sl][:, b0:b0 + nb_, :], sc["v"][lo:hi, :].rearrange("(j p) d -> p j d", p=128))],
                    [], [b_vt[sl]], lsem[sl][2])

            groups = [(n, jb, g) for n in range(len(tiles)) for jb in range(4) for g in range(2)]

            def geom(n, jb):
                S, i = tiles[n]
                t0 = i * T
                gb = t0 // 128 + jb
                kb0 = 1 if gb == 0 else 0
                kb1 = 2 if gb == S // 128 - 1 else 3
                return kb0, kb1 - kb0

            def gparams(gi):
                n, jb, g = groups[gi]
                kb0, nb = geom(n, jb)
                return n, jb, g, n % 2, gi % 2, stt[gi % 4], b_stt[gi % 4], kb0, nb

            def pe_qk(gi):
                n, jb, g, sl, r2, st, bst, kb0, nb = gparams(gi)
                Wd = nb * 128
                kw0 = (jb + kb0) * 128
                for hh in range(4):
                    h = g * 4 + hh
                    c, half = h // 2, h % 2
                    pr = slice(half * 64, half * 64 + 64)
                    op("pe", lambda e, hh=hh, pr=pr, c=c: e.matmul(
                        S4[:, hh, 0:Wd], lhsT=qt[sl][pr, c, jb * 128:(jb + 1) * 128], rhs=kt[sl][pr, g, kw0:kw0 + Wd],
                        start=True, stop=True), [b_qt[sl], b_kt[sl]], [b_S4])

            def dve_softmax_pre(gi):
                n, jb, g, sl, r2, st, bst, kb0, nb = gparams(gi)
                Wd = nb * 128
                bw0 = kb0 * 128
                op("dve", lambda e: e.scalar_tensor_tensor(
                    out=sb4[r2][:, :, 0:Wd], in0=S4[:, :, 0:Wd], scalar=0.125, in1=band[:, g * 4:(g + 1) * 4, bw0:bw0 + Wd],
                    op0=ALU.mult, op1=ALU.add), [b_S4, b_band], [b_sb4[r2]])
                op("dve", lambda e: e.reduce_max(out=st[:, 0:4], in_=sb4[r2][:, :, 0:Wd], axis=AX.X), [b_sb4[r2]], [bst])
                op("dve", lambda e: e.scalar_tensor_tensor(out=st[:, 4:8], in0=st[:, 0:4], scalar=-1.0, in1=nskb[:, g * 4:(g + 1) * 4],
                                                         op0=ALU.mult, op1=ALU.min), [bst, b_skb], [bst])
                op("dve", lambda e: e.tensor_tensor(out=st[:, 12:16], in0=skb[:, g * 4:(g + 1) * 4], in1=st[:, 4:8], op=ALU.add),
                   [bst, b_skb], [bst])

            def act_exp(gi):
                n, jb, g, sl, r2, st, bst, kb0, nb = gparams(gi)
                Wd = nb * 128
                for hh in range(4):
                    op("act", lambda e, hh=hh: e.activation(out=P4[r2][:, hh, 0:Wd], in_=sb4[r2][:, hh, 0:Wd], func=AF.Exp,
                                                          bias=st[:, 4 + hh:5 + hh], scale=1.0, accum_out=st[:, 8 + hh:9 + hh]),
                       [b_sb4[r2], bst], [b_P4[r2][hh], bst])
                op("act", lambda e: e.activation(out=st[:, 12:16], in_=st[:, 12:16], func=AF.Exp), [bst], [bst])

            def pe_T(gi):
                n, jb, g, sl, r2, st, bst, kb0, nb = gparams(gi)
                for hh in range(4):
                    for kb in range(nb):
                        op("pe", lambda e, hh=hh, kb=kb: e.transpose(out=ptp[hh // 2][:, hh % 2, kb, :],
                                                                   in_=P4[r2][:, hh, kb * 128:(kb + 1) * 128], identity=ident[:]),
                           [b_P4[r2][hh], b_ident], [b_ptp[hh // 2]])

            def copies_den(gi):
                n, jb, g, sl, r2, st, bst, kb0, nb = gparams(gi)
                op("act", lambda e: e.activation(out=PT[r2][:, 0:2, 0:nb, :], in_=ptp[0][:, :, 0:nb, :], func=AF.Copy),
                   [b_ptp[0]], [b_PT[r2][0]])
                op("dve", lambda e: e.tensor_copy(out=PT[r2][:, 2:4, 0:nb, :], in_=ptp[1][:, :, 0:nb, :]), [b_ptp[1]], [b_PT[r2][1]])
                op("dve", lambda e: e.tensor_tensor(out=st[:, 16:20], in0=st[:, 8:12], in1=st[:, 12:16], op=ALU.add), [bst], [bst])
                op("dve", lambda e: e.reciprocal(out=st[:, 20:24], in_=st[:, 16:20]), [bst], [bst])

            def pe_pv(gi):
                n, jb, g, sl, r2, st, bst, kb0, nb = gparams(gi)
                for hh in range(4):
                    h = g * 4 + hh
                    for kb in range(nb):
                        op("pe", lambda e, hh=hh, kb=kb, h=h: e.matmul(
                            ops1[:, h * 64:(h + 1) * 64], lhsT=PT[r2][:, hh, kb, :],
                            rhs=vt[sl][:, jb + kb0 + kb, g * 64:(g + 1) * 64], start=(kb == 0), stop=(kb == nb - 1)),
                           [b_PT[r2][hh // 2], b_vt[sl]], [b_ops1])

            def act_norm(gi):
                n, jb, g, sl, r2, st, bst, kb0, nb = gparams(gi)
                osl = (gi // 2) % 2
                for hh in range(4):
                    h = g * 4 + hh
                    op("act", lambda e, hh=hh, h=h: e.activation(out=atm[osl][:, h * 64:(h + 1) * 64], in_=ops1[:, h * 64:(h + 1) * 64],
                                                               func=AF.Identity, scale=st[:, 20 + hh:21 + hh]),
                       [b_ops1, bst], [b_atm[osl][g]])

            def finalize(gi):
                n, jb, g, sl, r2, st, bst, kb0, nb = gparams(gi)
                S, i = tiles[n]
                osl = (gi // 2) % 2
                for c in range(4):
                    op("pe", lambda e, c=c: e.transpose(out=tpp[:, c, :], in_=atm[osl][:, c * 128:(c + 1) * 128], identity=ident[:]),
                       [b_atm[osl][c // 2], b_ident], [b_tpp])
                op("act", lambda e: e.activation(out=aT[sl][:, :, jb * 128:(jb + 1) * 128], in_=tpp[:], func=AF.Copy),
                   [b_tpp], [b_aT[sl][jb]])
                if jb == 3:
                    t0 = i * T
                    dma("sp", [(SCR[S]["at"].rearrange("(c p) t -> p c t", p=128)[:, :, t0:t0 + T], aT[sl][:])],
                        b_aT[sl], [], atsem[sl])

            p4a_load(0)
            p4a_load(1)
            NG = len(groups)
            for sidx in range(NG + 4):
                g2, g1, g0, g3 = sidx - 2, sidx - 1, sidx, sidx - 3
                if 0 <= g2 < NG:
                    copies_den(g2)
                if g0 < NG:
                    pe_qk(g0)
                if 0 <= g2 < NG:
                    pe_pv(g2)
                if g0 < NG:
                    dve_softmax_pre(g0)
                if 0 <= g1 < NG:
                    act_exp(g1)
                    pe_T(g1)
                if 0 <= g2 < NG:
                    act_norm(g2)
                    if groups[g2][1] == 3 and groups[g2][2] == 1:
                        p4a_load(groups[g2][0] + 2)
                if 0 <= g3 < NG and groups[g3][2] == 1:
                    finalize(g3)
            P.close()
            if done("p4a"):
                break

            P = Phase(cx, "p4b_%d" % l)
            w4 = P.sb([128, 8, 4096], BF16, "w4")
            wao = P.sb([128, 4, 1024], BF16, "wao")
            wfo = P.sb([128, 4, 1024], BF16, "wfo")
            wro = P.sb([128, 8, 1024], BF16, "wro")
            wo = P.sb([128, 8, 1024], BF16, "wo")
            ws = P.dsem()
            blk4 = [(0, 2304, 512), (512, 2816, 512)]
            for hlf in range(2):
                for b in range(3):
                    blk4.append((1024 + b * 1024 + hlf * 512, 3328 + b * 1024 + hlf * 512, 512))
            W4 = WB(P, w4, W["w_in"][l], blk4[:5])
            b_wao = WB(P, wao, W["w_att_o"][l], [(0, 0, 1024)]).at(0)
            b_wfo = WB(P, wfo, W["w_four_o"][l], [(0, 0, 1024)]).at(0)
            b_wro = WB(P, wro, W["w_rnn_o"][l], [(0, 0, 1024)]).at(0)
            W4.add(blk4[5:])
            b_wo = WB(P, wo, W["w_out"][l], [(0, 0, 1024)]).at(0)
            prm = P.sb([128, 56], F32, "prm4")
            b_prm = Buf(prm)
            dma("sp", [(prm[:, 0:32], W["b_in"][l][2304:6400].rearrange("(c p) -> p c", p=128)),
                       (prm[:, 32:40], W["b_out"][l].rearrange("(c p) -> p c", p=128)),
                       (prm[:, 40:48], W["ln1_g"][l].rearrange("(c p) -> p c", p=128)),
                       (prm[:, 48:56], W["ln1_b"][l].rearrange("(c p) -> p c", p=128))], [], [b_prm], ws,
                allow_slow_non_contiguous=True)
            xf = P.sb([128, 8, T], F32, "xf")
            b_xf = [Buf(xf[:, c]) for c in range(8)]
            xb = P.sb([128, 8, T], BF16, "xb")
            b_xb = Buf(xb)
            att = P.sb([128, 4, T], BF16, "att")
            b_att = Buf(att)
            ytt = P.sb([128, 4, T], BF16, "ytt")
            b_ytt = Buf(ytt)
            ht = P.sb([128, 8, T], BF16, "ht")
            b_ht = [Buf(ht[:, c]) for c in range(8)]
            hg = P.sb([128, 8, T], BF16, "hg")
            b_hg = [Buf(hg[:, c]) for c in range(8)]
            mixed = P.sb([128, 8, T], BF16, "mixed")
            b_mixed = [Buf(mixed[:, c]) for c in range(8)]
            zsq = P.sb([128, 8, T], BF16, "zsq")
            b_zsq = [Buf(zsq[:, c]) for c in range(8)]
            sems4 = [P.dsem() for _ in range(6)]
            gy = [P.sb([128, T], F32, "gy") for _ in range(2)]
            b_gy = [Buf(t) for t in gy]
            sg = [P.sb([128, T], F32, "sg") for _ in range(4)]
            b_sg = [Buf(t) for t in sg]
            tm = [P.sb([128, T], F32, "tm") for _ in range(4)]
            b_tm = [Buf(t) for t in tm]
            lnt = [P.sb([128, T], F32, "lnt") for _ in range(3)]
            b_lnt = [Buf(t) for t in lnt]
            pg = [P.ps([128, 512], F32, "pg") for _ in range(3)]
            b_pg = [Buf(t, True) for t in pg]
            po = [P.ps([128, 512], F32, "po") for _ in range(3)]
            b_po = [Buf(t, True) for t in po]
            pl = [P.ps([128, 512], F32, "pl") for _ in range(2)]
            b_pl = [Buf(t, True) for t in pl]

            def p4b_load(n, which):
                S, i = tiles[n]
                sls = slice(i * T, (i + 1) * T)
                sc = SCR[S]
                if which == 0:
                    dma("pool", [(xb[:], xin[S].rearrange("(k p) t -> p k t", p=128)[:, :, sls])], [], [b_xb], sems4[0])
                    dma("sp", [(ht[:], sc["h"].rearrange("(c p) t -> p c t", p=128)[:, :, sls])], [], b_ht, sems4[1])
                    dma("sp", [(att[:], sc["at"].rearrange("(c p) t -> p c t", p=128)[:, :, sls])], [], [b_att], sems4[2])
                    dma("sp", [(ytt[:], sc["yt"].rearrange("(c p) t -> p c t", p=128)[:, :, sls])], [], [b_ytt], sems4[3])
                else:
                    dma("sp", [(xf[:], xin[S].rearrange("(k p) t -> p k t", p=128)[:, :, sls])], [], b_xf, sems4[4])

            def layer_norm(xfb, b_xfb, zb_t, b_zb, zsq_t, b_zsq_, gcol, bcol, prm_t, b_prm_t):
                for m in range(8):
                    op("pe", lambda e, m=m: e.matmul(pl[0][:], lhsT=onesm[:], rhs=zb_t[:, m, 0:T], start=(m == 0), stop=(m == 7)),
                       [b_onesm, b_zb[m]], [b_pl[0]])
                for m in range(8):
                    op("pe", lambda e, m=m: e.matmul(pl[1][:], lhsT=onesm[:], rhs=zsq_t[:, m, :], start=(m == 0), stop=(m == 7)),
                       [b_onesm, b_zsq_[m]], [b_pl[1]])
                op("act", lambda e: e.activation(out=lnt[0][:], in_=pl[0][:], func=AF.Copy), [b_pl[0]], [b_lnt[0]])
                op("act", lambda e: e.activation(out=lnt[1][:], in_=pl[0][:], func=AF.Square), [b_pl[0]], [b_lnt[1]])
                op("dve", lambda e: e.tensor_tensor(out=lnt[1][:], in0=pl[1][:], in1=lnt[1][:], op=ALU.subtract), [b_pl[1], b_lnt[1]], [b_lnt[1]])
                op("act", lambda e: e.activation(out=lnt[1][:], in_=lnt[1][:], func=AF.Sqrt, bias=EPS, scale=1.0), [b_lnt[1]], [b_lnt[1]])
                ri = 2 if len(lnt) > 2 else 1
                op("dve", lambda e: e.reciprocal(out=lnt[ri][:], in_=lnt[1][:]), [b_lnt[1]], [b_lnt[ri]])
                for m in range(8):
                    op("dve", lambda e, m=m: e.tensor_tensor(out=xfb[:, m, :], in0=xfb[:, m, :], in1=lnt[0][:], op=ALU.subtract),
                       [b_xfb[m], b_lnt[0]], [b_xfb[m]])
                    op("dve", lambda e, m=m: e.tensor_tensor(out=xfb[:, m, :], in0=xfb[:, m, :], in1=lnt[ri][:], op=ALU.mult),
                       [b_xfb[m], b_lnt[ri]], [b_xfb[m]])
                    op("act", lambda e, m=m: e.activation(out=xfb[:, m, :], in_=xfb[:, m, :], func=AF.Identity,
                                                        bias=prm_t[:, bcol + m:bcol + m + 1], scale=prm_t[:, gcol + m:gcol + m + 1]),
                       [b_xfb[m], b_prm_t], [b_xfb[m]])

            p4b_load(0, 0)
            p4b_load(0, 1)
            pgi = 0
            poi = 0
            for n, (S, i) in enumerate(tiles):
                sls = slice(i * T, (i + 1) * T)
                for c in range(8):
                    pb = pg[pgi % 3]; bpb = b_pg[pgi % 3]; pgi += 1
                    for k in range(8):
                        op("pe", lambda e, pb=pb, k=k, c=c: e.matmul(pb[:], lhsT=w4[:, k, c * 128:(c + 1) * 128], rhs=xb[:, k, :],
                                                                   start=(k == 0), stop=(k == 7)), [W4.at(c * 128), b_xb], [bpb])
                    op("act", lambda e, pb=pb, c=c: e.activation(out=gy[c % 2][:], in_=pb[:], func=AF.Gelu_apprx_tanh,
                                                               bias=prm[:, c:c + 1], scale=1.0), [bpb, b_prm], [b_gy[c % 2]])
                    op("pool", lambda e, c=c: e.tensor_tensor(out=hg[:, c, :], in0=ht[:, c, :], in1=gy[c % 2][:], op=ALU.mult),
                       [b_ht[c], b_gy[c % 2]], [b_hg[c]])
                sgi = 0
                for m in range(8):
                    srcs = [(wao, b_wao, att, [b_att] * 4, 4), (wfo, b_wfo, ytt, [b_ytt] * 4, 4), (wro, b_wro, hg, b_hg, 8)]
                    tms = []
                    for b in range(3):
                        pb = pg[pgi % 3]; bpb = b_pg[pgi % 3]; pgi += 1
                        col = 1024 + b * 1024 + m * 128
                        for k in range(8):
                            op("pe", lambda e, pb=pb, k=k, col=col: e.matmul(pb[:], lhsT=w4[:, k, col:col + 128], rhs=xb[:, k, :],
                                                                           start=(k == 0), stop=(k == 7)), [W4.at(col), b_xb], [bpb])
                        sgt = sg[sgi % 4]; bsg = b_sg[sgi % 4]
                        tmt = tm[sgi % 4]; btm = b_tm[sgi % 4]
                        sgi += 1
                        op("act", lambda e, pb=pb, sgt=sgt, b=b, m=m: e.activation(out=sgt[:], in_=pb[:], func=AF.Sigmoid,
                                                                                 bias=prm[:, 8 + b * 8 + m:9 + b * 8 + m], scale=1.0),
                           [bpb, b_prm], [bsg])
                        wt, bwt, rt, brt, nk_ = srcs[b]
                        pb2 = po[poi % 3]; bpb2 = b_po[poi % 3]; poi += 1
                        for k in range(nk_):
                            op("pe", lambda e, pb2=pb2, k=k, wt=wt, rt=rt, m=m, nk_=nk_: e.matmul(
                                pb2[:], lhsT=wt[:, k, m * 128:(m + 1) * 128], rhs=rt[:, k, :], start=(k == 0), stop=(k == nk_ - 1)),
                               [bwt, brt[k]], [bpb2])
                        op("dve", lambda e, pb2=pb2, sgt=sgt, tmt=tmt: e.tensor_tensor(out=tmt[:], in0=pb2[:], in1=sgt[:], op=ALU.mult),
                           [bpb2, bsg], [btm])
                        tms.append((tmt, btm))
                    op("pool", lambda e, tms=tms: e.tensor_tensor(out=tms[0][0][:], in0=tms[0][0][:], in1=tms[1][0][:], op=ALU.add),
                       [tms[0][1], tms[1][1]], [tms[0][1]])
                    op("pool", lambda e, tms=tms, m=m: e.tensor_tensor(out=mixed[:, m, :], in0=tms[0][0][:], in1=tms[2][0][:], op=ALU.add),
                       [tms[0][1], tms[2][1]], [b_mixed[m]])
                for m in range(8):
                    pb2 = po[poi % 3]; bpb2 = b_po[poi % 3]; poi += 1
                    for k in range(8):
                        op("pe", lambda e, pb2=pb2, k=k, m=m: e.matmul(pb2[:], lhsT=wo[:, k, m * 128:(m + 1) * 128], rhs=mixed[:, k, :],
                                                                     start=(k == 0), stop=(k == 7)), [b_wo, b_mixed[k]], [bpb2])
                    tmt = tm[m % 4]; btm = b_tm[m % 4]
                    op("act", lambda e, pb2=pb2, tmt=tmt, m=m: e.activation(out=tmt[:], in_=pb2[:], func=AF.Identity,
                                                                          bias=prm[:, 32 + m:33 + m], scale=1.0), [bpb2, b_prm], [btm])
                    op("dve", lambda e, tmt=tmt, m=m: e.scalar_tensor_tensor(out=xf[:, m, :], in0=xf[:, m, :], scalar=ALPHA, in1=tmt[:],
                                                                           op0=ALU.mult, op1=ALU.add), [b_xf[m], btm], [b_xf[m]])
                    op("pool", lambda e, m=m: e.tensor_copy(out=hg[:, m, :], in_=xf[:, m, :]), [b_xf[m]], [b_hg[m]])
                    op("act", lambda e, m=m: e.activation(out=zsq[:, m, :], in_=xf[:, m, :], func=AF.Square), [b_xf[m]], [b_zsq[m]])
                if n + 1 < len(tiles):
                    p4b_load(n + 1, 0)
                layer_norm(xf, b_xf, hg, b_hg, zsq, b_zsq, 40, 48, prm, b_prm)
                dma("sp", [(SCR[S]["x1"].rearrange("(c p) t -> p c t", p=128)[:, :, sls], xf[:])], b_xf, [], sems4[5])
                if n + 1 < len(tiles):
                    p4b_load(n + 1, 1)
            P.close()
            if done("p4b"):
                break

            P = Phase(cx, "p5_%d" % l)
            wup = P.sb([128, 8, 5632], BF16, "wup")
            wdn = P.sb([128, 22, 1024], BF16, "wdn")
            ws = P.dsem()
            blk5 = []
            for (c0, n) in ((0, 768), (768, 768), (1536, 640), (2176, 640)):
                blk5.append((c0, c0, n))
                blk5.append((2816 + c0, 2816 + c0, n))
            WUP = WB(P, wup, W["w_ffn_up"][l], blk5)
            b_wdn = WB(P, wdn, W["w_ffn_down"][l], [(0, 0, 1024)]).at(0)
            prm = P.sb([128, 192], F32, "prm5")
            b_prm = Buf(prm)
            pp5 = [(prm[:, 0:44], W["conv_ffn_b"][l].rearrange("(c p) -> p c", p=128)),
                   (prm[:, 176:184], W["ln2_g"][l].rearrange("(c p) -> p c", p=128)),
                   (prm[:, 184:192], W["ln2_b"][l].rearrange("(c p) -> p c", p=128))]
            for k in range(3):
                pp5.append((prm[:, 44 + 44 * k:88 + 44 * k], W["conv_ffn_w"][l, k].rearrange("(c p) -> p c", p=128)))
            dma("sp", pp5, [], [b_prm], ws, allow_slow_non_contiguous=True)
            xf = P.sb([128, 8, T], F32, "xf5")
            b_xf = [Buf(xf[:, c]) for c in range(8)]
            xbe2 = [P.sb([128, 8, 514], BF16, "xbe") for _ in range(2)]
            b_xbe2 = [Buf(t) for t in xbe2]
            actb = P.sb([128, 22, T], BF16, "actb")
            b_actb = [Buf(actb[:, j]) for j in range(22)]
            zsq = actb
            b_zsq = b_actb[0:8]
            ext = [P.sb([128, 514], F32, "ext5") for _ in range(3)]
            b_ext = [Buf(t) for t in ext]
            cv = [P.sb([128, T], F32, "cv5") for _ in range(4)]
            b_cv = [Buf(t) for t in cv]
            lnt = [P.sb([128, T], F32, "lnt5") for _ in range(2)]
            b_lnt = [Buf(t) for t in lnt]
            sems5 = [P.dsem() for _ in range(4)]
            pu = [P.ps([128, 512], F32, "pu") for _ in range(3)]
            b_pu = [Buf(t, True) for t in pu]
            phl = [P.ps([128, 2], F32, "phl") for _ in range(2)]
            b_phl = [Buf(t, True) for t in phl]
            po = [P.ps([128, 512], F32, "po5") for _ in range(2)]
            b_po = [Buf(t, True) for t in po]
            pl = po
            b_pl = b_po

            def p5_load(n, which):
                if n >= len(tiles):
                    return
                S, i = tiles[n]
                t0 = i * T
                if which == 0:
                    xbe = xbe2[n % 2]
                    b_xbe = b_xbe2[n % 2]
                    lo = max(t0 - 1, 0)
                    hi = min(t0 + 513, S)
                    if t0 == 0:
                        op("dve", lambda e: e.memset(xbe[:, :, 0:1], 0.0), [], [b_xbe])
                    if t0 + T == S:
                        op("dve", lambda e: e.memset(xbe[:, :, 513:514], 0.0), [], [b_xbe])
                    dma("pool", [(xbe[:, :, lo - (t0 - 1):hi - (t0 - 1)],
                                  SCR[S]["x1"].rearrange("(k p) t -> p k t", p=128)[:, :, lo:hi])], [], [b_xbe], sems5[n % 2])
                else:
                    dma("sp", [(xf[:], SCR[S]["x1"].rearrange("(k p) t -> p k t", p=128)[:, :, t0:t0 + T])], [], b_xf, sems5[2])

            p5_load(0, 0)
            p5_load(0, 1)
            p5_load(1, 0)
            pui = 0
            phi_ = 0
            eci = 0
            poi = 0
            for n, (S, i) in enumerate(tiles):
                sls = slice(i * T, (i + 1) * T)
                xbe = xbe2[n % 2]
                b_xbe = b_xbe2[n % 2]
                b_zb5 = [b_xbe] * 8
                for j in range(22):
                    cvs = []
                    for which, ch in ((0, j), (1, 22 + j)):
                        pb = pu[pui % 3]; bpb = b_pu[pui % 3]; pui += 1
                        ph = phl[phi_ % 2][:, :]; bph = b_phl[phi_ % 2]; phi_ += 1
                        for k in range(8):
                            op("pe", lambda e, pb=pb, k=k, ch=ch: e.matmul(pb[:], lhsT=wup[:, k, ch * 128:(ch + 1) * 128], rhs=xbe[:, k, 1:513],
                                                                         start=(k == 0), stop=(k == 7)), [WUP.at(ch * 128), b_xbe], [bpb])
                        for k in range(8):
                            op("pe", lambda e, ph=ph, k=k, ch=ch: e.matmul(ph, lhsT=wup[:, k, ch * 128:(ch + 1) * 128], rhs=xbe[:, k, 0:514:513],
                                                                         start=(k == 0), stop=(k == 7)), [WUP.at(ch * 128), b_xbe], [bph])
                        ex = ext[eci % 3]; bex = b_ext[eci % 3]
                        cvt = cv[eci % 4]; bcv = b_cv[eci % 4]
                        eci += 1
                        op("act", lambda e, ex=ex, pb=pb: e.activation(out=ex[:, 1:513], in_=pb[:], func=AF.Copy), [bpb], [bex])
                        op("act", lambda e, ex=ex, ph=ph: e.activation(out=ex[:, 0:514:513], in_=ph, func=AF.Copy), [bph], [bex])
                        op("act", lambda e, pb=pb, cvt=cvt, ch=ch: e.activation(out=cvt[:], in_=pb[:], func=AF.Identity,
                                                                              bias=prm[:, ch:ch + 1], scale=prm[:, 88 + ch:89 + ch]),
                           [bpb, b_prm], [bcv])
                        for k in (0, 2):
                            op("dve", lambda e, ex=ex, cvt=cvt, ch=ch, k=k: e.scalar_tensor_tensor(
                                out=cvt[:], in0=ex[:, k:k + 512], scalar=prm[:, 44 + 44 * k + ch:45 + 44 * k + ch], in1=cvt[:],
                                op0=ALU.mult, op1=ALU.add), [bex, b_prm, bcv], [bcv])
                        cvs.append((cvt, bcv))
                    op("act", lambda e, cvs=cvs: e.activation(out=cvs[0][0][:], in_=cvs[0][0][:], func=AF.Gelu_apprx_tanh),
                       [cvs[0][1]], [cvs[0][1]])
                    op("pool", lambda e, cvs=cvs, j=j: e.tensor_tensor(out=actb[:, j, :], in0=cvs[0][0][:], in1=cvs[1][0][:], op=ALU.mult),
                       [cvs[0][1], cvs[1][1]], [b_actb[j]])
                for m in range(8):
                    pb2 = po[poi % 2]; bpb2 = b_po[poi % 2]; poi += 1
                    for j in range(22):
                        op("pe", lambda e, pb2=pb2, j=j, m=m: e.matmul(pb2[:], lhsT=wdn[:, j, m * 128:(m + 1) * 128], rhs=actb[:, j, :],
                                                                     start=(j == 0), stop=(j == 21)), [b_wdn, b_actb[j]], [bpb2])
                    op("dve", lambda e, pb2=pb2, m=m: e.scalar_tensor_tensor(out=xf[:, m, :], in0=xf[:, m, :], scalar=ALPHA, in1=pb2[:],
                                                                           op0=ALU.mult, op1=ALU.add), [b_xf[m], bpb2], [b_xf[m]])
                for m in range(8):
                    op("pool", lambda e, m=m: e.tensor_copy(out=xbe[:, m, 0:T], in_=xf[:, m, :]), [b_xf[m]], [b_xbe])
                    op("act", lambda e, m=m: e.activation(out=zsq[:, m, :], in_=xf[:, m, :], func=AF.Square), [b_xf[m]], [b_zsq[m]])
                layer_norm(xf, b_xf, xbe, b_zb5, zsq, b_zsq, 176, 184, prm, b_prm)
                dma("sp", [(xout[S].rearrange("(c p) t -> p c t", p=128)[:, :, sls], xf[:])], b_xf, [], sems5[3])
                p5_load(n + 1, 1)
                p5_load(n + 2, 0)
            P.close()

        cx.barrier()
    return nc, hc


def kernel(**inputs):
    n = 8
    xp = np.asarray(inputs["x_prompt"], dtype=np.float32)
    xsm = np.asarray(inputs["x_sample"], dtype=np.float32)
    nc, hc = build_program()
    shared = {k: np.ascontiguousarray(np.asarray(inputs[k], dtype=np.float32)) for k in WNAMES}
    shared.update(hc)
    in_maps = []
    for c in range(n):
        m = dict(shared)
        m["xT2048"] = np.ascontiguousarray(xp[c].T)
        m["xT8192"] = np.ascontiguousarray(xsm[c].T)
        in_maps.append(m)
    res = run_bass_kernel_spmd(nc, in_maps, core_ids=list(range(n)))
    yp = np.stack([np.ascontiguousarray(res.results[c]["yT2048"].T) for c in range(n)], axis=0)
    ys = np.stack([np.ascontiguousarray(res.results[c]["yT8192"].T) for c in range(n)], axis=0)
    return yp.astype(np.float32), ys.astype(np.float32)
```
